# Optimizing a Trainium2 kernel written in Bass

```python
import math
import jax, jax.numpy as jnp
from jax import lax
import numpy as np

D_MODEL = 1024
BATCH = 8
SEQ = 2048
DEPTH = 4
DEC_BATCH = 128
DEC_SEQ = 1
PAST_LEN = 16384
PAGE_SIZE = 128

MIX = D_MODEL
GROUP_W = MIX // 4
GM_HEADS = 4
GM_HEAD_DIM = GROUP_W // GM_HEADS
GM_CHUNK = 128
SSM_CH = 16
SSM_GROUPS = GROUP_W // SSM_CH
SSM_P = 64
CONV_W = 3
WKV_N = 64
WKV_HEADS = GROUP_W // WKV_N
LORA_W = 32
LORA_A = 32
LORA_G = 64
D_TM = 3 * GROUP_W + LORA_W + LORA_A + LORA_G
IN_COLS = 6 * GROUP_W + D_TM
IN_SPLITS = [GROUP_W * i for i in range(1, 7)]
TM_SPLITS = [GROUP_W, 2 * GROUP_W, 3 * GROUP_W, 3 * GROUP_W + LORA_W, 3 * GROUP_W + LORA_W + LORA_A]
D_FF = -(-(8 * D_MODEL) // (3 * 256)) * 256
NORM_EPS = 1e-6
GM_LN_EPS = 1e-5
WKV_LN_EPS = 64e-5

kernel_name = 'hybrid_gmlp_s5_conv_rwkv7_step'


def rmsnorm(x, g):
    xf = x.astype(jnp.float32)
    y = xf * lax.rsqrt(jnp.mean(xf * xf, axis=-1, keepdims=True) + NORM_EPS) * g.astype(jnp.float32)
    return y.astype(x.dtype)


def chunk_gmlp(zu, zv, ln_g, ln_b, ws, bs):
    bsz, t, _ = zu.shape
    u = jax.nn.gelu(zu)
    vf = jax.nn.gelu(zv.astype(jnp.float32))
    mu = jnp.mean(vf, axis=-1, keepdims=True)
    var = jnp.mean(jnp.square(vf - mu), axis=-1, keepdims=True)
    vn = ((vf - mu) * lax.rsqrt(var + GM_LN_EPS) * ln_g.astype(jnp.float32)
          + ln_b.astype(jnp.float32)).astype(zu.dtype)
    tc = min(t, GM_CHUNK)
    nc = t // tc
    wm = ws[:, :tc, :tc] * jnp.tril(jnp.ones((tc, tc), ws.dtype))
    vc = vn.reshape(bsz, nc, tc, GM_HEADS, GM_HEAD_DIM)
    s = jnp.einsum('hts,bcshd->bcthd', wm, vc) + jnp.transpose(bs[:, :tc])[None, None, :, :, None]
    return u * s.reshape(bsz, t, GROUP_W).astype(zu.dtype), vn


def s5_ssm(zu, s_re0, s_im0, a_re, a_im, log_dt, b_re, b_im, c_re, c_im, d, glu_w, glu_b):
    f32 = jnp.float32
    bsz, t, _ = zu.shape
    u = zu.astype(f32).reshape(bsz, t, SSM_GROUPS, SSM_CH)
    lam_re = jnp.minimum(a_re.astype(f32), -1e-4)
    lam_im = a_im.astype(f32)
    dt = jnp.exp(log_dt.astype(f32))
    mag = jnp.exp(lam_re * dt)
    lb_re = mag * jnp.cos(lam_im * dt)
    lb_im = mag * jnp.sin(lam_im * dt)
    den = lam_re * lam_re + lam_im * lam_im
    f_re = ((lb_re - 1.0) * lam_re + lb_im * lam_im) / den
    f_im = (lb_im * lam_re - (lb_re - 1.0) * lam_im) / den
    br, bi = b_re.astype(f32), b_im.astype(f32)
    bb_re = f_re[..., None] * br - f_im[..., None] * bi
    bb_im = f_re[..., None] * bi + f_im[..., None] * br
    bu_re = jnp.einsum('btgh,gph->tbgp', u, bb_re)
    bu_im = jnp.einsum('btgh,gph->tbgp', u, bb_im)
    a_r = jnp.broadcast_to(lb_re, (t, 1, SSM_GROUPS, SSM_P))
    a_i = jnp.broadcast_to(lb_im, (t, 1, SSM_GROUPS, SSM_P))

    def combine(e1, e2):
        a1r, a1i, b1r, b1i = e1
        a2r, a2i, b2r, b2i = e2
        return (a2r * a1r - a2i * a1i, a2r * a1i + a2i * a1r,
                a2r * b1r - a2i * b1i + b2r, a2r * b1i + a2i * b1r + b2i)

    pa_r, pa_i, sr, si = lax.associative_scan(combine, (a_r, a_i, bu_re, bu_im), axis=0)
    s0r = s_re0.astype(f32)[None]
    s0i = s_im0.astype(f32)[None]
    st_re = pa_r * s0r - pa_i * s0i + sr
    st_im = pa_r * s0i + pa_i * s0r + si
    y = (jnp.einsum('tbgp,ghp->btgh', st_re, c_re.astype(f32))
         - jnp.einsum('tbgp,ghp->btgh', st_im, c_im.astype(f32))
         + d.astype(f32) * u)
    y = jax.nn.gelu(y.reshape(bsz, t, GROUP_W))
    out = y * jax.nn.sigmoid(y @ glu_w.astype(f32) + glu_b.astype(f32))
    return out.astype(zu.dtype), st_re[-1], st_im[-1]


def short_conv(zx, zb, zc, buf, conv_w, conv_b):
    t = zx.shape[1]
    z = zc * zx
    zp = jnp.concatenate([buf.astype(z.dtype), z], axis=1)
    y = conv_b + sum(conv_w[j] * zp[:, j:j + t] for j in range(CONV_W))
    return zb * y, zp[:, t:]


def rwkv7(zd, prev, s0, mu, w0, w2, a0, a2, g2, k_k, k_a, r_k, ln_g, ln_b):
    f32 = jnp.float32
    bsz, t, _ = zd.shape
    zprev = jnp.concatenate([prev[:, None, :].astype(zd.dtype), zd[:, :-1]], axis=1)
    zs = zd + mu * (zprev - zd)
    r, k, v, xw, xa, xg = jnp.split(zs, TM_SPLITS, axis=-1)
    w = -jax.nn.softplus(-(w0 + jnp.tanh(xw) @ w2).astype(f32)) - 0.5
    decay = jnp.exp(-jnp.exp(w))
    a = jax.nn.sigmoid((a0 + xa @ a2).astype(f32))
    g = (jax.nn.sigmoid(xg) @ g2).astype(f32)
    hs = (bsz, t, WKV_HEADS, WKV_N)
    rf = r.astype(f32).reshape(hs)
    kf = k.astype(f32).reshape(hs)
    vf = v.astype(f32).reshape(hs)
    ah = a.reshape(hs)
    dh = decay.reshape(hs)
    kk = kf * k_k.astype(f32).reshape(WKV_HEADS, WKV_N)
    kk = kk / jnp.maximum(jnp.linalg.norm(kk, axis=-1, keepdims=True), 1e-12)
    kf = kf * (1.0 + (ah - 1.0) * k_a.astype(f32).reshape(WKV_HEADS, WKV_N))
    a_vec = -kk
    b_vec = kk * ah

    def step(S, inp):
        r_t, d_t, k_t, v_t, a_t, b_t = inp
        sa = jnp.einsum('bhvk,bhk->bhv', S, a_t)
        S = (S * d_t[:, :, None, :] + sa[..., None] * b_t[:, :, None, :]
             + v_t[..., None] * k_t[:, :, None, :])
        return S, jnp.einsum('bhvk,bhk->bhv', S, r_t)

    xs = tuple(jnp.moveaxis(q, 1, 0) for q in (rf, dh, kf, vf, a_vec, b_vec))
    S, o = lax.scan(step, s0.astype(f32), xs)
    o = jnp.moveaxis(o, 0, 1)
    m = jnp.mean(o, axis=-1, keepdims=True)
    var = jnp.mean(jnp.square(o - m), axis=-1, keepdims=True)
    on = ((o - m) * lax.rsqrt(var + WKV_LN_EPS) * ln_g.astype(f32).reshape(WKV_HEADS, WKV_N)
          + ln_b.astype(f32).reshape(WKV_HEADS, WKV_N))
    bonus = jnp.sum(rf * kf * r_k.astype(f32), axis=-1, keepdims=True) * vf
    out = (on + bonus).reshape(bsz, t, GROUP_W) * g
    return out.astype(zd.dtype), zd[:, -1], S


def setup_inputs(seed: int = 0) -> dict:
    key = jax.random.key(seed)
    ks = iter(jax.random.split(key, 64))
    f32 = jnp.float32

    def nrm(shape, scale):
        return jax.random.normal(next(ks), shape, f32) * scale

    def gain(shape):
        return 1.0 + nrm(shape, 0.02)

    def unif(shape, lo, hi):
        return jax.random.uniform(next(ks), shape, f32, lo, hi)

    L, G, P, H, N = DEPTH, SSM_GROUPS, SSM_P, WKV_HEADS, WKV_N
    return {
        'x_prompt': nrm((BATCH, SEQ, D_MODEL), 1.0),
        'x_sample': nrm((DEC_BATCH, DEC_SEQ, D_MODEL), 1.0),
        'state_wkv': nrm((L, DEC_BATCH, H, N, N), 0.3),
        'state_shift': nrm((L, DEC_BATCH, D_TM), 1.0),
        'state_ssm_re': nrm((L, DEC_BATCH, G, P), 0.3),
        'state_ssm_im': nrm((L, DEC_BATCH, G, P), 0.3),
        'state_conv': nrm((L, DEC_BATCH, CONV_W - 1, GROUP_W), 1.0),
        'norm1_g': gain((L, D_MODEL)),
        'w_in': nrm((L, D_MODEL, IN_COLS), D_MODEL ** -0.5),
        'gm_ln_g': gain((L, GROUP_W)),
        'gm_ln_b': nrm((L, GROUP_W), 0.02),
        'gm_ws': nrm((L, GM_HEADS, GM_CHUNK, GM_CHUNK), GM_CHUNK ** -0.5),
        'gm_bs': gain((L, GM_HEADS, GM_CHUNK)),
        'ssm_a_re': -0.5 + nrm((L, G, P), 0.01),
        'ssm_a_im': math.pi * jnp.arange(P, dtype=f32) + nrm((L, G, P), 0.01),
        'ssm_log_dt': unif((L, G, P), math.log(1e-3), math.log(1e-1)),
        'ssm_b_re': nrm((L, G, P, SSM_CH), (2 * SSM_CH) ** -0.5),
        'ssm_b_im': nrm((L, G, P, SSM_CH), (2 * SSM_CH) ** -0.5),
        'ssm_c_re': nrm((L, G, SSM_CH, P), P ** -0.5),
        'ssm_c_im': nrm((L, G, SSM_CH, P), P ** -0.5),
        'ssm_d': nrm((L, G, SSM_CH), 1.0),
        'ssm_glu_w': nrm((L, GROUP_W, GROUP_W), GROUP_W ** -0.5),
        'ssm_glu_b': nrm((L, GROUP_W), 0.02),
        'conv_w': nrm((L, CONV_W, GROUP_W), CONV_W ** -0.5),
        'conv_b': nrm((L, GROUP_W), 0.02),
        'tm_mu': unif((L, D_TM), 0.0, 1.0),
        'tm_w0': unif((L, GROUP_W), -6.0, -1.0),
        'tm_w2': nrm((L, LORA_W, GROUP_W), 0.1 * LORA_W ** -0.5),
        'tm_a0': nrm((L, GROUP_W), 0.1),
        'tm_a2': nrm((L, LORA_A, GROUP_W), 0.1 * LORA_A ** -0.5),
        'tm_g2': nrm((L, LORA_G, GROUP_W), LORA_G ** -0.5),
        'tm_k_k': 0.85 + nrm((L, GROUP_W), 0.02),
        'tm_k_a': gain((L, GROUP_W)),
        'tm_r_k': nrm((L, H, N), 0.1),
        'tm_ln_g': gain((L, GROUP_W)),
        'tm_ln_b': nrm((L, GROUP_W), 0.02),
        'w_out': nrm((L, MIX, D_MODEL), 0.5 * MIX ** -0.5),
        'norm2_g': gain((L, D_MODEL)),
        'ffn_w_gu': nrm((L, D_MODEL, 2 * D_FF), D_MODEL ** -0.5),
        'ffn_w_down': nrm((L, D_FF, D_MODEL), 0.5 * D_FF ** -0.5),
        'norm_f_g': gain((D_MODEL,)),
    }


def reference(x_prompt, x_sample, state_wkv, state_shift, state_ssm_re, state_ssm_im, state_conv,
              norm1_g, w_in, gm_ln_g, gm_ln_b, gm_ws, gm_bs,
              ssm_a_re, ssm_a_im, ssm_log_dt, ssm_b_re, ssm_b_im, ssm_c_re, ssm_c_im, ssm_d,
              ssm_glu_w, ssm_glu_b, conv_w, conv_b,
              tm_mu, tm_w0, tm_w2, tm_a0, tm_a2, tm_g2, tm_k_k, tm_k_a, tm_r_k, tm_ln_g, tm_ln_b,
              w_out, norm2_g, ffn_w_gu, ffn_w_down, norm_f_g):

    def run_layer(x, l, wkv0, shift0, sre0, sim0, conv0):
        h = rmsnorm(x, norm1_g[l])
        z = h @ w_in[l]
        zau, zav, zbu, zcx, zcb, zcc, zd = jnp.split(z, IN_SPLITS, axis=-1)
        ya, v_rows = chunk_gmlp(zau, zav, gm_ln_g[l], gm_ln_b[l], gm_ws[l], gm_bs[l])
        yb, s_re, s_im = s5_ssm(zbu, sre0, sim0, ssm_a_re[l], ssm_a_im[l], ssm_log_dt[l],
                                ssm_b_re[l], ssm_b_im[l], ssm_c_re[l], ssm_c_im[l], ssm_d[l],
                                ssm_glu_w[l], ssm_glu_b[l])
        yc, conv_new = short_conv(zcx, zcb, zcc, conv0, conv_w[l], conv_b[l])
        yd, shift_new, wkv_new = rwkv7(zd, shift0, wkv0, tm_mu[l], tm_w0[l], tm_w2[l], tm_a0[l],
                                       tm_a2[l], tm_g2[l], tm_k_k[l], tm_k_a[l], tm_r_k[l],
                                       tm_ln_g[l], tm_ln_b[l])
        x = x + jnp.concatenate([ya, yb, yc, yd], axis=-1) @ w_out[l]
        gate, up = jnp.split(rmsnorm(x, norm2_g[l]) @ ffn_w_gu[l], 2, axis=-1)
        x = x + (jax.nn.silu(gate) * up) @ ffn_w_down[l]
        return x, wkv_new, shift_new, s_re, s_im, conv_new, v_rows

    bp = x_prompt.shape[0]
    dtp = x_prompt.dtype
    xp, xs = x_prompt, x_sample
    wkv_p, wkv_s, sh_p, sh_s, re_p, re_s, im_p, im_s, cv_p, cv_s, chv_s = ([] for _ in range(11))
    for l in range(DEPTH):
        xp, a1, a2, a3, a4, a5, _ = run_layer(
            xp, l,
            jnp.zeros((bp, WKV_HEADS, WKV_N, WKV_N), dtp),
            jnp.zeros((bp, D_TM), dtp),
            jnp.zeros((bp, SSM_GROUPS, SSM_P), dtp),
            jnp.zeros((bp, SSM_GROUPS, SSM_P), dtp),
            jnp.zeros((bp, CONV_W - 1, GROUP_W), dtp))
        wkv_p.append(a1); sh_p.append(a2); re_p.append(a3); im_p.append(a4); cv_p.append(a5)
        xs, b1, b2, b3, b4, b5, b6 = run_layer(
            xs, l, state_wkv[l], state_shift[l], state_ssm_re[l], state_ssm_im[l], state_conv[l])
        wkv_s.append(b1); sh_s.append(b2); re_s.append(b3); im_s.append(b4); cv_s.append(b5)
        chv_s.append(b6)
    y_prompt = rmsnorm(xp, norm_f_g)
    y_sample = rmsnorm(xs, norm_f_g)
    return (y_prompt, y_sample,
            jnp.stack(wkv_p), jnp.stack(wkv_s),
            jnp.stack(sh_p), jnp.stack(sh_s),
            jnp.stack(re_p), jnp.stack(re_s),
            jnp.stack(im_p), jnp.stack(im_s),
            jnp.stack(cv_p), jnp.stack(cv_s),
            jnp.stack(chv_s))
```

```python
from contextlib import ExitStack
import math
import numpy as np
import concourse.bass as bass
import concourse.mybir as mybir
from concourse.bass_utils import run_bass_kernel_spmd

F32 = mybir.dt.float32
F32R = mybir.dt.float32r
I32 = mybir.dt.int32
ALU = mybir.AluOpType
AF = mybir.ActivationFunctionType
AX = mybir.AxisListType

EPOCH = 20000
DBG = set()
NCORES = 8
L = 4
D = 1024
T = 2048
TB = 512
NBLK = T // TB
NS = 16
INC = 2432
DFF = 2816
DTM = 896


class V:
    __slots__ = ("ap", "keys")

    def __init__(self, ap, keys):
        self.ap = ap
        self.keys = tuple(keys)


def bc(v, shape):
    return V(v.ap.to_broadcast(list(shape)), v.keys)


class TT:
    def __init__(self, h, name, keys=None):
        self.h = h
        self.name = name
        self.keys = (name,) if keys is None else tuple(keys)

    def __getitem__(self, idx):
        return V(self.h[idx], self.keys)

    def r(self):
        return TT(self.h.bitcast(F32R), self.name, self.keys)

    def i32(self):
        return TT(self.h.bitcast(I32), self.name, self.keys)


class Prog:
    ENG = ["pe", "act", "dve", "pool", "sp"]

    def __init__(self, nc):
        self.nc = nc
        self.ops = {e: [] for e in self.ENG}
        self.count = {e: 0 for e in self.ENG}
        self.lastw = {}
        self.readers = {}
        self.dma_cnt = {}
        self.sealed = {}
        self.semids = set()
        self.stack = ExitStack()
        self.tag = 'init'
        self.suffix = ''
        self.name2tag = {}

    def sbuf(self, name, shape, dtype=F32):
        h = self.stack.enter_context(self.nc.sbuf_tensor(name, list(shape), dtype))
        return TT(h, name)

    def psum(self, name, shape, dtype=F32):
        h = self.stack.enter_context(self.nc.psum_tensor(name, list(shape), dtype))
        return TT(h, name)

    def seal(self, group):
        self.sealed[("d", group)] = self.dma_cnt.get(group, 0)

    def _add(self, eng, fn, reads, writes, dma_group=None):
        deps = {}
        for k in reads:
            w = self.lastw.get(k)
            if w:
                for s, v in w.items():
                    if deps.get(s, 0) < v:
                        deps[s] = v
            if isinstance(k, str) and k.startswith("ps"):
                rd = self.readers.get(k)
                if rd:
                    for s, v in rd.items():
                        if s[1] != eng and deps.get(s, 0) < v:
                            deps[s] = v
        for k in writes:
            w = self.lastw.get(k)
            if w:
                for s, v in w.items():
                    if deps.get(s, 0) < v:
                        deps[s] = v
            rd = self.readers.get(k)
            if rd:
                for s, v in rd.items():
                    if deps.get(s, 0) < v:
                        deps[s] = v
        for s_, v_ in list(deps.items()):
            sv = self.sealed.get(s_)
            if sv is not None and v_ <= sv:
                deps[s_] = sv
        if dma_group is None:
            self.count[eng] += 1
            ep, val = divmod(self.count[eng] - 1, EPOCH)
            tok = (("e", eng, ep), val + 1)
            inc = 1
        else:
            g = self.dma_cnt.get(dma_group, 0) + 16
            self.dma_cnt[dma_group] = g
            tok = (("d", dma_group), g)
            inc = 16
        if eng == "pe":
            deps = {s: v for s, v in deps.items() if not (s[0] == "e" and s[1] == "pe")}
        self.semids.add(tok[0])
        self.ops[eng].append((fn, deps, tok, inc, self.tag + self.suffix))
        for k in reads:
            rd = self.readers.setdefault(k, {})
            if rd.get(tok[0], 0) < tok[1]:
                rd[tok[0]] = tok[1]
        for k in writes:
            w = self.lastw.setdefault(k, {})
            if w.get(tok[0], 0) < tok[1]:
                w[tok[0]] = tok[1]
        return tok

    @staticmethod
    def _keys(*vs):
        ks = []
        for v in vs:
            if isinstance(v, V):
                ks.extend(v.keys)
        return ks

    @staticmethod
    def _ap(v):
        return v.ap if isinstance(v, V) else v

    def mm(self, out, lhsT, rhs, start=True, stop=True):
        o, l, r = out.ap, lhsT.ap, rhs.ap
        return self._add("pe", lambda e: e.matmul(o, l, r, start=start, stop=stop),
                         self._keys(lhsT, rhs), self._keys(out))

    def transpose(self, out, in_, ident):
        o, i, d = out.ap, in_.ap, ident.ap
        return self._add("pe", lambda e: e.transpose(o, i, d), self._keys(in_, ident), self._keys(out))

    def act(self, out, in_, func, scale=1.0, bias=None, accum_out=None):
        o, i = out.ap, in_.ap
        kw = {}
        if bias is not None:
            kw["bias"] = self._ap(bias)
        if accum_out is not None:
            kw["accum_out"] = accum_out.ap
        sc = self._ap(scale)
        return self._add("act", lambda e: e.activation(o, i, func, scale=sc, **kw),
                         self._keys(in_, scale, bias), self._keys(out, accum_out))

    def tt(self, eng, out, in0, in1, op):
        o, a, b = out.ap, in0.ap, in1.ap
        return self._add(eng, lambda e: e.tensor_tensor(o, a, b, op), self._keys(in0, in1), self._keys(out))

    def ts(self, eng, out, in0, s1, op0, s2=None, op1=None):
        o, a = out.ap, in0.ap
        x1, x2 = self._ap(s1), self._ap(s2)
        kw = {}
        if op1 is not None:
            kw["op1"] = op1
        return self._add(eng, lambda e: e.tensor_scalar(o, a, x1, x2, op0, **kw),
                         self._keys(in0, s1, s2), self._keys(out))

    def stt(self, eng, out, in0, scalar, in1, op0, op1):
        o, a, b = out.ap, in0.ap, in1.ap
        s = self._ap(scalar)
        return self._add(eng, lambda e: e.scalar_tensor_tensor(o, a, s, b, op0, op1),
                         self._keys(in0, scalar, in1), self._keys(out))

    def scan(self, out, d0, d1, initial, op0, op1):
        o, a, b = out.ap, d0.ap, d1.ap
        ini = self._ap(initial)
        return self._add("dve", lambda e: e.tensor_tensor_scan(o, a, b, ini, op0, op1),
                         self._keys(d0, d1, initial), self._keys(out))

    def copy(self, eng, out, in_):
        o, i = out.ap, in_.ap
        if eng == "act":
            return self._add("act", lambda e: e.copy(o, i), self._keys(in_), self._keys(out))
        return self._add(eng, lambda e: e.tensor_copy(o, i), self._keys(in_), self._keys(out))

    def memset(self, eng, out, val):
        o = out.ap
        return self._add(eng, lambda e: e.memset(o, val), [], self._keys(out))

    def reduce(self, out, in_, op, axis=AX.X):
        o, i = out.ap, in_.ap
        return self._add("dve", lambda e: e.tensor_reduce(o, i, axis, op), self._keys(in_), self._keys(out))

    def recip(self, out, in_):
        o, i = out.ap, in_.ap
        return self._add("dve", lambda e: e.reciprocal(o, i), self._keys(in_), self._keys(out))

    def bn_stats(self, out, in_):
        o, i = out.ap, in_.ap
        return self._add("dve", lambda e: e.bn_stats(o, i), self._keys(in_), self._keys(out))

    def bn_aggr(self, out, in_):
        o, i = out.ap, in_.ap
        return self._add("dve", lambda e: e.bn_aggr(o, i), self._keys(in_), self._keys(out))

    def affine_select(self, out, in_, pattern, compare_op, fill, base, channel_multiplier):
        o, i = out.ap, in_.ap
        return self._add("pool", lambda e: e.affine_select(o, i, pattern, compare_op, fill, base=base,
                                                           channel_multiplier=channel_multiplier),
                         self._keys(in_), self._keys(out))

    def dma(self, q, out, in_, group, **kw):
        o, i = out.ap, in_.ap
        return self._add(q, lambda e: e.dma_start(out=o, in_=i, **kw), self._keys(in_), self._keys(out),
                         dma_group=group)

    def emit(self):
        nc = self.nc
        sems = {}
        for sid in sorted(self.semids, key=str):
            nm = "s_" + "_".join(str(x) for x in sid)
            sems[sid] = self.stack.enter_context(nc.semaphore(nm))
        finals = [(("d", g), v) for g, v in self.dma_cnt.items()]
        sealed = self.sealed
        bname = {"pe": "tensor", "act": "scalar", "dve": "vector", "pool": "gpsimd", "sp": "sync"}
        with nc.Block() as block:
            for eng in self.ENG:
                ops = self.ops[eng]

                def body(e, ops=ops, eng=eng):
                    seen = {}
                    for fn, deps, tok, inc, tag in ops:
                        for s, v in deps.items():
                            if seen.get(s, 0) < v:
                                e.wait_ge(sems[s], v)
                                seen[s] = v
                        ins = fn(e)
                        ins.then_inc(sems[tok[0]], inc)
                        try:
                            self.name2tag[ins.ins.name] = tag
                        except Exception:
                            pass
                    if eng == "sp":
                        for s, v in finals:
                            if seen.get(s, 0) < v:
                                e.wait_ge(sems[s], v)

                getattr(block, bname[eng])(body)
        self.stack.close()


NSLOT = 28
NWB = 3
TWO_PI_SAFE = 6.2831845
INV_2PI = 1.0 / (2.0 * math.pi)

W_SHAPES = {
    "norm1_g": (L, D), "w_in": (L, D, INC), "gm_ln_g": (L, 256), "gm_ln_b": (L, 256),
    "gm_ws": (L, 4, 128, 128), "gm_bs": (L, 4, 128),
    "ssm_a_re": (L, 16, 64), "ssm_a_im": (L, 16, 64), "ssm_log_dt": (L, 16, 64),
    "ssm_b_re": (L, 16, 64, 16), "ssm_b_im": (L, 16, 64, 16),
    "ssm_c_re": (L, 16, 16, 64), "ssm_c_im": (L, 16, 16, 64), "ssm_d": (L, 16, 16),
    "ssm_glu_w": (L, 256, 256), "ssm_glu_b": (L, 256), "conv_w": (L, 3, 256), "conv_b": (L, 256),
    "tm_mu": (L, DTM), "tm_w0": (L, 256), "tm_w2": (L, 32, 256), "tm_a0": (L, 256),
    "tm_a2": (L, 32, 256), "tm_g2": (L, 64, 256), "tm_k_k": (L, 256), "tm_k_a": (L, 256),
    "tm_r_k": (L, 4, 64), "tm_ln_g": (L, 256), "tm_ln_b": (L, 256),
    "w_out": (L, D, D), "norm2_g": (L, D), "ffn_w_gu": (L, D, 2 * DFF), "ffn_w_down": (L, DFF, D),
    "norm_f_g": (D,),
}
IN_SHAPES = {
    "x_prompt": (T, D), "x_sample": (NS, D), "state_wkv": (L, NS, 4, 64, 64),
    "state_shift": (L, NS, DTM), "state_ssm_re": (L, NS, 16, 64), "state_ssm_im": (L, NS, 16, 64),
    "state_conv": (L, NS, 2, 256),
}
OUT_SHAPES = {
    "y_p": (T, D), "y_s": (NS, D), "wkv_p": (L, 4, 64, 64), "wkv_s": (L, NS, 4, 64, 64),
    "shift_p": (L, DTM), "shift_s": (L, NS, DTM), "re_p": (L, 16, 64), "re_s": (L, NS, 16, 64),
    "im_p": (L, 16, 64), "im_s": (L, NS, 16, 64), "conv_p": (L, 2, 256), "conv_s": (L, NS, 2, 256),
    "chv_s": (L, NS, 256),
}


class Builder0:
    def __init__(self, do_sample=True, nblk=NBLK, nlayer=L, max_stages=None, do_prep=True):
        self.do_sample = do_sample
        self.nblk = nblk
        self.nlayer = nlayer
        nc = bass.Bass("TRN2", target_bir_lowering=False)
        self.nc = nc
        self.P = Prog(nc)
        self.din = {}
        for n, s in list(IN_SHAPES.items()) + list(W_SHAPES.items()):
            self.din[n] = nc.dram_tensor(n, list(s), F32, kind="ExternalInput")
        self.dout = {}
        for n, s in OUT_SHAPES.items():
            self.dout[n] = nc.dram_tensor(n, list(s), F32, kind="ExternalOutput")
        self.scr = {}
        for n, s in {"etab": (L, 8, 128, 1024), "lcd": (L, 128, 1536),
                     "svec": (6, NS, 256), "son": (NS, 256)}.items():
            self.scr[n] = nc.dram_tensor("scr_" + n, list(s), F32, kind="Internal")
        self.max_stages = max_stages
        self.alloc()
        if do_prep:
            self.prep()
        self.main()
        self.P.emit()

    def di(self, name):
        return self.din[name].ap()

    def dir_(self, name):
        return self.din[name].bitcast(F32R).ap()

    def ar(self, i, c0=0, c1=512, p0=0, p1=128, r=False):
        h = self.arena_r if r else self.arena.h
        return V(h[p0:p1, i, c0:c1], (("ar", i),))

    def hs(self, i, c0=0, c1=512, p0=0, p1=128, r=True):
        h = self.hTr if r else self.hTf
        return V(h[p0:p1, i, c0:c1], (("hT", i),))

    def arn(self, i, n, c0=0, c1=512, r=False):
        h = self.arena_r if r else self.arena.h
        return V(h[:, i:i + n, c0:c1], tuple(("ar", j) for j in range(i, i + n)))

    def ps(self):
        t = self.PS[self.psi % self.ps_n]
        self.psi += 1
        return t

    def mq(self, q, c0=0, c1=128, p0=0, p1=128, n=1):
        if n == 1:
            return V(self.mat.h[p0:p1, q, c0:c1], (("m", q),))
        hh = self.mat.h[p0:p1, q:q + n, :].rearrange("p a b -> p (a b)")
        return V(hh, tuple(("m", j) for j in range(q, q + n)))

    def ms(self, i, c0=0, c1=512, p0=0, p1=128):
        hh = self.mat.h[p0:p1, 4 * i:4 * i + 4, :].rearrange("p a b -> p (a b)")[:, c0:c1]
        return V(hh, tuple(("m", j) for j in range(4 * i, 4 * i + 4)))

    def pt(self, g, r0, n=1):
        return self.PT[:, g, r0:r0 + n]

    def alias_r(self, tt, shape):
        addr = None
        for a in self.nc.allocations:
            if a.name == tt.name + "_set":
                addr = a.memorylocations[0].addr
        assert addr is not None
        return self.nc.alloc_sbuf_tensor_at(tt.name + "_r", list(shape), F32R, offset=addr)

    def alloc(self):
        P = self.P
        self.arena = P.sbuf("arena", [128, NSLOT, 512])
        self.arena_r = self.alias_r(self.arena, [128, NSLOT, 512])
        self.PS = [P.psum("ps%d" % i, [128, 512]) for i in range(8)]
        self.psi = 0
        self.PSL = self.PS[4:8]
        self.ps_n = 4
        self.ident = P.sbuf("ident", [128, 128])
        self.ones = P.sbuf("ones", [128, 128])
        self.zeros = P.sbuf("zeros", [128, 512])
        self.m_le = P.sbuf("m_le", [128, 128])
        self.m_lt = P.sbuf("m_lt", [128, 128])
        self.m_gt = P.sbuf("m_gt", [128, 128])
        self.bones = P.sbuf("bones", [128, 128])
        self.sel = P.sbuf("sel", [128, 2])
        self.RT = P.sbuf("RT", [128, 3, 128])
        self.PT = P.sbuf("PT", [128, 3, 128])
        self.SPt = P.sbuf("SPt", [128, 24, 32])
        self.SPi = P.sbuf("SPi", [128, 32], I32)
        self.lora = P.sbuf("lora", [128, L, 256])
        self.xb = P.sbuf("xb", [128, 4, D])
        self.hT = P.sbuf("hT", [128, 8, TB])
        self.hT.keys = tuple(("hT", k) for k in range(8))
        hT_addr = [a.memorylocations[0].addr for a in self.nc.allocations if a.name == "hT_set"][0]
        self.hTf = self.nc.alloc_sbuf_tensor_at("hT_f", [128, 8, TB], F32, offset=hT_addr)
        self.hTr = self.hT.h.bitcast(F32R)
        self.yT = P.sbuf("yT", [128, 8, TB])
        self.WB = [P.sbuf("wb%d" % i, [128, 4096]) for i in range(NWB)]
        self.wbi = 0
        stg_addr = [a.memorylocations[0].addr for a in self.nc.allocations if a.name == "wb0_set"][0]
        self.STG = TT(self.nc.alloc_sbuf_tensor_at("STG", [128, 1536], F32, offset=stg_addr), "STG", ("wb0",))
        self.LC = P.sbuf("LC", [128, 2048])
        self.LCR = TT(self.alias_r(self.LC, [128, 2048]), "LC")
        self.BCP = P.sbuf("BCP", [128, 4, 256])
        self.small = P.sbuf("small", [128, 64])
        self.ssm_st = P.sbuf("ssm_st", [128, L, 8, 2])
        self.conv_st = P.sbuf("conv_st", [128, L, 2, 2])
        self.shift_st = P.sbuf("shift_st", [128, L, 7])
        self.wkv_st = P.sbuf("wkv_st", [128, L, 2, 128])
        self.mstat = P.sbuf("mstat", [128, 4, 8])
        self.mat = P.sbuf("mat", [128, 40, 128])

    def prep(self):
        self.P.tag = 'prep'
        P = self.P
        d = self.di
        P.memset("pool", self.ones[:, :], 1.0)
        P.memset("pool", self.zeros[:, :], 0.0)
        P.affine_select(self.ident[:, :], self.ones[:, :], [[-1, 128]], ALU.is_equal, 0.0, 0, 1)
        P.affine_select(self.m_gt[:, :], self.ones[:, :], [[-1, 128]], ALU.is_gt, 0.0, 0, 1)
        P.affine_select(self.m_le[:, :], self.ones[:, :], [[1, 128]], ALU.is_ge, 0.0, 0, -1)
        P.affine_select(self.m_lt[:, :], self.ones[:, :], [[1, 128]], ALU.is_gt, 0.0, 0, -1)
        P.memset("pool", self.bones[:, :], 0.0)
        P.memset("pool", self.bones[0:64, 0:64], 1.0)
        P.memset("pool", self.bones[64:128, 64:128], 1.0)
        P.memset("pool", self.sel[:, :], 0.0)
        P.memset("pool", self.sel[0:64, 0:1], 1.0)
        P.memset("pool", self.sel[64:128, 1:2], 1.0)
        for t in (self.ssm_st, self.conv_st, self.shift_st, self.wkv_st):
            P.memset("dve", t[:], 0.0)
        P.memset("dve", self.RT[:], 0.0)
        self.rt_keys = []

        def rows(g, r0, name, n):
            src = self.din[name].ap()
            nd = len(src.shape)
            letters = "abcd"[:nd]
            flat = src.rearrange("%s -> (%s)" % (" ".join(letters), " ".join(letters)))
            src2 = flat.rearrange("(r c) -> r c", c=128)
            fk = ("rt", g, r0)
            self.rt_keys.append(fk)
            P.dma("sp", V(self.RT.h[r0:r0 + n, g, :], (fk,)), V(src2, self.RT.keys), "rt")

        rows(0, 0, "norm1_g", 32); rows(0, 32, "norm2_g", 32); rows(0, 64, "norm_f_g", 8)
        rows(0, 72, "conv_b", 8); rows(0, 80, "conv_w", 24); rows(0, 104, "tm_w0", 8)
        rows(0, 112, "tm_a0", 8); rows(0, 120, "tm_k_k", 8)
        rows(1, 0, "tm_k_a", 8); rows(1, 8, "tm_r_k", 8); rows(1, 16, "ssm_d", 8)
        rows(1, 24, "ssm_glu_b", 8); rows(1, 32, "tm_mu", 28)
        rows(1, 64, "ssm_a_re", 32); rows(1, 96, "ssm_a_im", 32)
        rows(2, 0, "ssm_log_dt", 32); rows(2, 32, "gm_bs", 16)
        P.seal("rt")
        for g in range(3):
            pp = self.ps()
            P.transpose(pp[:, 0:128], V(self.RT.h[:, g, :], self.RT.keys + tuple(self.rt_keys)), self.ident[:, :])
            P.copy("dve", self.PT[:, g, :], pp[:, 0:128])
        P.ts("dve", self.PT[:, 2, 48:56], self.PT[:, 1, 0:8], -1.0, ALU.mult, 1.0, ALU.add)
        for l in range(L):
            lr = self.lora.r()
            P.dma("pool", lr[0:32, l, :], V(self.dir_("tm_w2")[l], ()), "lora")
            P.dma("pool", lr[32:64, l, :], V(self.dir_("tm_a2")[l], ()), "lora")
            P.dma("pool", lr[64:128, l, :], V(self.dir_("tm_g2")[l], ()), "lora")
        P.seal("lora")
        sp = lambda i: self.SPt[:, i, :]
        a_re, a_im, ldt = self.PT[:, 1, 64:96], self.PT[:, 1, 96:128], self.PT[:, 2, 0:32]
        LAM, DT, MAG, TH, FS, FF, FR, SIN, COS, LBR, LBI, DEN, FRE, FIM, T1, T2 = range(16)
        self.I_LBR, self.I_LBI, self.I_MAG, self.I_FRE, self.I_FIM = LBR, LBI, MAG, FRE, FIM
        P.ts("dve", sp(LAM), a_re, -1e-4, ALU.min)
        P.act(sp(DT), ldt, AF.Exp)
        P.tt("dve", sp(T1), sp(LAM), sp(DT), ALU.mult)
        P.act(sp(MAG), sp(T1), AF.Exp)
        P.tt("dve", sp(TH), a_im, sp(DT), ALU.mult)
        for dst, off in ((SIN, 0.0), (COS, 0.25)):
            P.ts("dve", sp(FS), sp(TH), INV_2PI, ALU.mult, off, ALU.add)
            P.copy("dve", self.SPi[:, :], sp(FS))
            P.copy("dve", sp(FF), self.SPi[:, :])
            P.tt("dve", sp(FR), sp(FS), sp(FF), ALU.subtract)
            P.act(sp(dst), sp(FR), AF.Sin, scale=TWO_PI_SAFE)
        P.tt("dve", sp(LBR), sp(MAG), sp(COS), ALU.mult)
        P.tt("dve", sp(LBI), sp(MAG), sp(SIN), ALU.mult)
        P.tt("dve", sp(T1), sp(LAM), sp(LAM), ALU.mult)
        P.tt("dve", sp(T2), a_im, a_im, ALU.mult)
        P.tt("dve", sp(DEN), sp(T1), sp(T2), ALU.add)
        P.recip(sp(DEN), sp(DEN))
        LM1 = 16
        P.ts("dve", sp(LM1), sp(LBR), -1.0, ALU.add)
        P.tt("dve", sp(T1), sp(LM1), sp(LAM), ALU.mult)
        P.tt("dve", sp(T2), sp(LBI), a_im, ALU.mult)
        P.tt("dve", sp(T1), sp(T1), sp(T2), ALU.add)
        P.tt("dve", sp(FRE), sp(T1), sp(DEN), ALU.mult)
        P.tt("dve", sp(T1), sp(LBI), sp(LAM), ALU.mult)
        P.tt("dve", sp(T2), sp(LM1), a_im, ALU.mult)
        P.tt("dve", sp(T1), sp(T1), sp(T2), ALU.subtract)
        P.tt("dve", sp(FIM), sp(T1), sp(DEN), ALU.mult)
        etab = TT(self.scr["etab"], "etab")
        lcd = TT(self.scr["lcd"], "lcd")
        for l in range(self.nlayer):
            EC, ES, TA, TBs = 0, 8, 16, 20
            cs = self.SPt[:, COS, l * 8:(l + 1) * 8]
            sn = self.SPt[:, SIN, l * 8:(l + 1) * 8]
            P.copy("dve", self.arn(EC, 8, 0, 1), V(cs.ap.unsqueeze(2), cs.keys))
            P.copy("dve", self.arn(ES, 8, 0, 1), V(sn.ap.unsqueeze(2), sn.keys))
            n = 1
            while n < 512:
                cn = bc(self.arn(EC, 8, n - 1, n), [128, 8, n])
                snb = bc(self.arn(ES, 8, n - 1, n), [128, 8, n])
                ns_ = (n + 511) // 512
                def tmpv(base, n=n):
                    hh = self.arena.h[:, base:base + 4, :].rearrange("p a b -> p (a b)")[:, 0:8 * n]
                    return V(hh.rearrange("p (a b) -> p a b", a=8), tuple(("ar", j) for j in range(base, base + 4)))
                t1, t2 = tmpv(TA), tmpv(TBs)
                P.tt("dve", t1, self.arn(EC, 8, 0, n), cn, ALU.mult)
                P.tt("dve", t2, self.arn(ES, 8, 0, n), snb, ALU.mult)
                P.tt("dve", self.arn(EC, 8, n, 2 * n), t1, t2, ALU.subtract)
                P.tt("dve", t1, self.arn(ES, 8, 0, n), cn, ALU.mult)
                P.tt("dve", t2, self.arn(EC, 8, 0, n), snb, ALU.mult)
                P.tt("dve", self.arn(ES, 8, n, 2 * n), t1, t2, ALU.add)
                n *= 2
            P.dma("sp", V(etab.h[l].rearrange("j p c -> p j c")[:, :, 0:512], ("etab",)), self.arn(EC, 8), "etab_w")
            P.dma("sp", V(etab.h[l].rearrange("j p c -> p j c")[:, :, 512:1024], ("etab",)), self.arn(ES, 8), "etab_w")
            WS = 0
            wsv = self.ms(WS)
            P.dma("sp", V(wsv.ap.rearrange("p (h s) -> p h s", h=4), wsv.keys),
                  V(d("gm_ws")[l].rearrange("h t s -> t h s"), ()), "prep_ws")
            for h in range(4):
                pp = self.ps()
                P.transpose(pp[:, 0:128], self.ms(WS, h * 128, (h + 1) * 128), self.ident[:, :])
                P.tt("dve", self.STG[:, h * 128:(h + 1) * 128], pp[:, 0:128], self.m_le[:, :], ALU.mult)
            CN = 1
            P.memset("pool", self.ms(CN), 0.0)
            cn_keys = []
            for ri, nm in enumerate(("ssm_c_re", "ssm_c_im")):
                for g in range(16):
                    half, gl = divmod(g, 8)
                    c0 = (half * 2 + ri) * 128 + (g % 2) * 64
                    fk = ("cn", l, ri, g)
                    cn_keys.append(fk)
                    P.dma("sp", V(self.ms(CN, c0, c0 + 64, gl * 16, gl * 16 + 16).ap, (fk,)),
                          V(d(nm)[l, g], self.ms(CN).keys), "prep_c")
            for half in range(2):
                for ri in range(2):
                    c0 = (half * 2 + ri) * 128
                    pp = self.ps()
                    src = self.ms(CN, c0, c0 + 128)
                    P.transpose(pp[:, 0:128], V(src.ap, src.keys + tuple(cn_keys)), self.ident[:, :])
                    if ri == 0:
                        P.copy("act", self.STG[:, 512 + c0:512 + c0 + 128], pp[:, 0:128])
                    else:
                        P.act(self.STG[:, 512 + c0:512 + c0 + 128], pp[:, 0:128], AF.Copy, scale=-1.0)
            BN = 2
            P.memset("pool", self.ms(BN), 0.0)
            bn_keys = []
            for ri, nm in enumerate(("ssm_b_re", "ssm_b_im")):
                for g in range(16):
                    half, gl = divmod(g, 8)
                    c0 = ri * 256 + half * 128 + gl * 16
                    p0 = (g % 2) * 64
                    fk = ("bn", l, ri, g)
                    bn_keys.append(fk)
                    P.dma("sp", V(self.ms(BN, c0, c0 + 16, p0, p0 + 64).ap, (fk,)),
                          V(d(nm)[l, g], self.ms(BN).keys), "prep_b")
            BO = 3
            def v8(slot, c0):
                vv = self.ms(slot, c0, c0 + 256)
                return V(vv.ap.rearrange("p (a b) -> p a b", a=8), vv.keys + (tuple(bn_keys) if slot == BN else ()))
            fre = self.SPt[:, FRE, l * 8:(l + 1) * 8]
            fim = self.SPt[:, FIM, l * 8:(l + 1) * 8]
            freb = V(fre.ap.unsqueeze(2).to_broadcast([128, 8, 32]), fre.keys)
            fimb = V(fim.ap.unsqueeze(2).to_broadcast([128, 8, 32]), fim.keys)
            t1 = v8(WS, 0); t2 = v8(WS, 256)
            P.tt("dve", t1, v8(BN, 0), freb, ALU.mult)
            P.tt("dve", t2, v8(BN, 256), fimb, ALU.mult)
            P.tt("dve", v8(BO, 0), t1, t2, ALU.subtract)
            P.tt("dve", t1, v8(BN, 0), fimb, ALU.mult)
            P.tt("dve", t2, v8(BN, 256), freb, ALU.mult)
            P.tt("dve", v8(BO, 256), t1, t2, ALU.add)
            for half in range(2):
                for ri in range(2):
                    pp = self.ps()
                    c0 = ri * 256 + half * 128
                    P.transpose(pp[:, 0:128], self.ms(BO, c0, c0 + 128), self.ident[:, :])
                    o0 = 1024 + (half * 2 + ri) * 128
                    P.copy("act", self.STG[:, o0:o0 + 128], pp[:, 0:128])
            P.dma("sp", V(lcd.h[l], ("lcd",)), self.STG[:, :], "lcd_w")

    def wload(self, src, a, b):
        buf = self.WB[self.wbi % NWB]
        self.wbi += 1
        view = buf.h.bitcast(F32R)[:, 0:a * b].rearrange("p (a b) -> p a b", a=a)
        self.P.dma("pool", V(view, buf.keys), V(src, ()), "ld_" + buf.name)
        return TT(view, buf.name, buf.keys)

    def main(self):
        P = self.P
        self.stages = []
        for tb in range(self.nblk):
            self.stages.append((None, lambda w, tb=tb: self.load_x(tb)))
            for l in range(self.nlayer):
                self.layer(tb, l)
            self.stages.append((None, lambda w, tb=tb: self.final_norm(tb)))
        if self.do_sample:
            self.sample()
        st = self.stages if self.max_stages is None else self.stages[:self.max_stages]
        views = [None] * len(st)
        nxt = 0

        def advance():
            nonlocal nxt
            while nxt < len(st):
                i = nxt
                nxt += 1
                if st[i][0] is not None:
                    views[i] = st[i][0]()
                    return

        for _ in range(NWB - 1):
            advance()
        for i in range(len(st)):
            if st[i][0] is not None:
                advance()
            st[i][1](views[i])

    def load_x(self, tb):
        self.P.tag = 'load_x'
        src = self.di("x_prompt")[tb * TB:(tb + 1) * TB, :].rearrange("(s p) c -> p s c", p=128)
        self.P.dma("sp", self.xb[:, :, :], V(src, ()), "xb_ld")

    def norm_to_hT(self, gbase):
        self.P.tag = 'norm'
        P = self.P
        hview = self.arena.h[:, 0:8, :].rearrange("p (s a) b -> p s (a b)", s=4)
        hkeys = tuple(("ar", j) for j in range(8))
        for sub in range(4):
            hv = V(hview[:, sub, :], hkeys[2 * sub:2 * sub + 2])
            P.act(hv, self.xb[:, sub, :], AF.Square, accum_out=self.small[:, sub:sub + 1])
            P.act(self.small[:, 4 + sub:5 + sub], self.small[:, sub:sub + 1], AF.Sqrt, scale=1.0 / D, bias=1e-6)
            P.recip(self.small[:, 8 + sub:9 + sub], self.small[:, 4 + sub:5 + sub])
            P.ts("dve", hv, self.xb[:, sub, :], self.small[:, 8 + sub:9 + sub], ALU.mult)
        for kc in range(8):
            pp = self.ps()
            for sub in range(4):
                hv = V(hview[:, sub, kc * 128:(kc + 1) * 128], hkeys[2 * sub:2 * sub + 2])
                P.transpose(pp[:, sub * 128:(sub + 1) * 128], hv, self.ident[:, :])
            P.act(self.hs(kc), pp[:, :], AF.Identity, scale=self.PT[:, 0, gbase + kc:gbase + kc + 1])

    def final_norm(self, tb):
        self.P.tag = 'final_norm'
        self.ps_n = 4
        P = self.P
        for sub in range(4):
            hv = self.arn(10 + 2 * sub, 2)
            hv = V(hv.ap.rearrange("p a b -> p (a b)"), hv.keys)
            P.act(hv, self.xb[:, sub, :], AF.Square, accum_out=self.small[:, sub:sub + 1])
            P.act(self.small[:, 4 + sub:5 + sub], self.small[:, sub:sub + 1], AF.Sqrt, scale=1.0 / D, bias=1e-6)
            P.recip(self.small[:, 8 + sub:9 + sub], self.small[:, 4 + sub:5 + sub])
            P.ts("dve", hv, self.xb[:, sub, :], self.small[:, 8 + sub:9 + sub], ALU.mult)
            if sub == 0:
                gfb = self.arn(18, 2)
                gfb = V(gfb.ap.rearrange("p a b -> p (a b)"), gfb.keys)
                P.dma("sp", gfb, V(self.di("norm_f_g").partition_broadcast(128), ()), "gfb")
            P.tt("dve", hv, hv, gfb, ALU.mult)
            r0 = tb * TB + sub * 128
            P.dma("sp", V(self.dout["y_p"].ap()[r0:r0 + 128, :], ("y_p",)), hv, "st_y%d" % sub)

    def load_consts(self, l):
        P = self.P
        tag = P.tag
        P.tag = 'pre'
        lcd = self.scr["lcd"].ap()
        P.dma("sp", self.LC[:, 0:1024], V(lcd[l][:, 0:1024], ("lcd",)), "lc_a")
        P.dma("pool", self.LCR[:, 1024:1536], V(self.scr["lcd"].bitcast(F32R).ap()[l][:, 1024:1536], ("lcd",)), "lc_b")
        P.dma("pool", V(self.LCR.h[:, 1536:2048].rearrange("p (k c) -> p k c", k=2), self.LC.keys),
              V(self.dir_("ssm_glu_w")[l].rearrange("(k p) c -> p k c", p=128), ()), "lc_b")
        for i, nm in enumerate(("gm_ln_g", "gm_ln_b", "tm_ln_g", "tm_ln_b")):
            P.dma("sp", self.BCP[:, i, :], V(self.di(nm)[l].partition_broadcast(128), ()), "bcp")
        P.tag = tag

    def layer(self, tb, l):
        P = self.P
        last = tb == self.nblk - 1

        class _S:
            def append(_, item, stages=self.stages):
                ld, comp = item

                def comp2(w, comp=comp):
                    P.suffix = '@%d.%d' % (tb, l)
                    comp(w)
                stages.append((ld, comp2))
        S = _S()

        def pre(w):
            P.tag = 'pre'
            self.ps_n = 8
            if tb == 0 and l == 0:
                self.load_consts(0)
            self.norm_to_hT(l * 8)

        S.append((None, pre))
        tiles = [[(0, 512)], [(1536, 512)], [(2048, 384)], [(768, 512)], [(1280, 256), (512, 256)]]
        for i, segs in enumerate(tiles):
            ncol = sum(n for _, n in segs)

            def ld(segs=segs, ncol=ncol):
                buf = self.WB[self.wbi % NWB]
                self.wbi += 1
                view = buf.h.bitcast(F32R)[:, 0:8 * ncol].rearrange("p (k c) -> p k c", k=8)
                off = 0
                for c0, n in segs:
                    src = self.dir_("w_in")[l][:, c0:c0 + n].rearrange("(k p) c -> p k c", p=128)
                    P.dma("pool", V(view[:, :, off:off + n], buf.keys), V(src, ()), "ld_" + buf.name)
                    off += n
                return TT(view, buf.name, buf.keys)

            def comp(w, i=i, segs=segs, ncol=ncol):
                P.tag = 'w_in'
                if i == 0:
                    for sub in range(6):
                        if sub < 4:
                            P.tag = 'w_in'
                            pp = self.ps()
                            for kc in range(8):
                                P.mm(pp[:, :], self.hs(kc, sub * 128, (sub + 1) * 128), w[:, kc, :], kc == 0, kc == 7)
                            self.gmlp_1(l, sub, pp)
                        if 1 <= sub < 5:
                            self.gmlp_2(l, sub - 1)
                        if 2 <= sub:
                            self.gmlp_3(l, sub - 2)
                else:
                    qs = []
                    for c0, n in segs:
                        qs += [(c0 + g * 128 - 512) // 128 for g in range(n // 128)]
                    pend = []
                    for cg, q in enumerate(qs):
                        P.tag = 'w_in'
                        pp = self.ps()
                        for kc in range(8):
                            P.mm(pp[:, :], w[:, kc, cg * 128:(cg + 1) * 128], self.hs(kc), kc == 0, kc == 7)
                        if q < 2:
                            pend.append((q, pp))
                        else:
                            self.consume_fm(tb, l, q, pp)
                    for q, pp in pend:
                        self.consume_fm(tb, l, q, pp)
                    if i == 2:
                        self.rwkv(tb, l, 0)
                    if i == 4:
                        self.rwkv(tb, l, 1)
                        nl = l + 1 if l + 1 < self.nlayer else 0
                        if not (tb == self.nblk - 1 and l == self.nlayer - 1):
                            self.load_consts(nl)

            S.append((ld, comp))
        for half in range(2):
            def ld(half=half):
                return self.wload(self.dir_("w_out")[l][:, half * 512:(half + 1) * 512].rearrange("(k p) c -> p k c", p=128), 8, 512)

            def comp(w, half=half):
                P.tag = 'w_out'
                yr = self.yT.r()
                for sub in range(4):
                    for kc in range(8):
                        P.mm(self.PSL[sub][:, :], yr[:, kc, sub * 128:(sub + 1) * 128], w[:, kc, :], kc == 0, kc == 7)
                    xs = self.xb[:, sub, half * 512:(half + 1) * 512]
                    P.tt("dve", xs, xs, self.PSL[sub][:, :], ALU.add)
                if half == 1:
                    self.norm_to_hT(32 + l * 8)

            S.append((ld, comp))
        for g in range(5):
            for which in range(2):
                def ld(g=g, which=which):
                    c0 = which * DFF + 512 * g
                    return self.wload(self.dir_("ffn_w_gu")[l][:, c0:c0 + 512].rearrange("(k p) c -> p k c", p=128), 8, 512)

                def comp(w, g=g, which=which):
                    P.tag = 'ffn_gu'
                    self.ps_n = 8
                    for fo in range(4):
                        j = 4 * g + fo
                        pz = self.ps()
                        for kc in range(8):
                            P.mm(pz[:, :], w[:, kc, fo * 128:(fo + 1) * 128], self.hs(kc), kc == 0, kc == 7)
                        if which == 0:
                            P.act(self.ar(j), pz[:, :], AF.Silu)
                        else:
                            P.tt("dve", self.ar(j, r=True), self.ar(j), pz[:, :], ALU.mult)

                S.append((ld, comp))

        def ld_last():
            buf = self.WB[self.wbi % NWB]
            self.wbi += 1
            view = buf.h.bitcast(F32R)[:, 0:4096].rearrange("p (k g c) -> p k g c", k=8, g=2)
            for g in range(2):
                c0 = g * DFF + 2560
                src = self.dir_("ffn_w_gu")[l][:, c0:c0 + 256].rearrange("(k p) c -> p k c", p=128)
                P.dma("pool", V(view[:, :, g, :], buf.keys), V(src, ()), "ld_" + buf.name)
            return TT(view, buf.name, buf.keys)

        def comp_last(w):
            P.tag = 'ffn_gu'
            self.ps_n = 8
            for fo in range(2):
                j = 20 + fo
                pg = self.ps()
                pu = self.ps()
                for kc in range(8):
                    P.mm(pg[:, :], w[:, kc, 0, fo * 128:(fo + 1) * 128], self.hs(kc), kc == 0, kc == 7)
                for kc in range(8):
                    P.mm(pu[:, :], w[:, kc, 1, fo * 128:(fo + 1) * 128], self.hs(kc), kc == 0, kc == 7)
                tmp = self.ar(22 + (j % 2))
                P.act(tmp, pg[:, :], AF.Silu)
                P.tt("dve", self.ar(j, r=True), tmp, pu[:, :], ALU.mult)

        S.append((ld_last, comp_last))
        for half in range(2):
            for jg in range(3):
                nj = 8 if jg < 2 else 6

                def ld(half=half, jg=jg, nj=nj):
                    src = self.dir_("ffn_w_down")[l][jg * 1024:jg * 1024 + nj * 128, half * 512:(half + 1) * 512]
                    return self.wload(src.rearrange("(j p) c -> p j c", p=128), nj, 512)

                def comp(w, half=half, jg=jg, nj=nj):
                    P.tag = 'ffn_down'
                    self.ps_n = 4
                    for jj in range(nj):
                        j = jg * 8 + jj
                        for sub in range(4):
                            P.mm(self.PSL[sub][:, :], self.ar(j, sub * 128, (sub + 1) * 128, r=True), w[:, jj, :],
                                 j == 0, j == 21)
                    if jg == 2:
                        for sub in range(4):
                            xs = self.xb[:, sub, half * 512:(half + 1) * 512]
                            P.tt("dve", xs, xs, self.PSL[sub][:, :], ALU.add)

                S.append((ld, comp))
        if last:
            S.append((None, lambda w: self.prompt_state_out(l)))

    def gmlp_1(self, l, sub, pp):
        self.P.tag = 'gmlp'
        P = self.P
        sl = 20 + sub
        u = self.ar(sl, 0, 256)
        vf = self.ar(sl, 256, 512)
        P.act(u, pp[:, 0:256], AF.Gelu_apprx_tanh)
        P.act(vf, pp[:, 256:512], AF.Gelu_apprx_tanh)
        ms = self.mstat
        P.bn_stats(ms[:, sub, 0:6], vf)
        P.bn_aggr(ms[:, sub, 6:8], ms[:, sub, 0:6])
        P.act(self.small[:, 16 + sub:17 + sub], ms[:, sub, 7:8], AF.Sqrt, bias=1e-5)
        P.recip(self.small[:, 20 + sub:21 + sub], self.small[:, 16 + sub:17 + sub])
        P.ts("dve", vf, vf, ms[:, sub, 6:7], ALU.subtract, self.small[:, 20 + sub:21 + sub], ALU.mult)
        P.tt("dve", vf, vf, self.BCP[:, 0, :], ALU.mult)
        P.tt("dve", vf, vf, self.BCP[:, 1, :], ALU.add)

    def gmlp_2(self, l, sub):
        self.P.tag = 'gmlp'
        P = self.P
        sl = 20 + sub
        p2 = self.ps()
        for h in range(4):
            P.mm(p2[:, h * 64:(h + 1) * 64], self.LC[:, h * 128:(h + 1) * 128], self.ar(sl, 256 + h * 64, 256 + (h + 1) * 64))
        for h in range(4):
            uh = self.ar(sl, h * 64, (h + 1) * 64)
            P.stt("dve", uh, p2[:, h * 64:(h + 1) * 64], self.PT[:, 2, 32 + l * 4 + h:33 + l * 4 + h], uh, ALU.add, ALU.mult)

    def gmlp_3(self, l, sub):
        self.P.tag = 'gmlp'
        P = self.P
        sl = 20 + sub
        p3 = self.ps()
        for t2 in range(2):
            P.transpose(p3[:, t2 * 128:(t2 + 1) * 128], self.ar(sl, t2 * 128, (t2 + 1) * 128), self.ident[:, :])
        yr = self.yT.r()
        for t2 in range(2):
            P.copy("act", yr[:, t2, sub * 128:(sub + 1) * 128], p3[:, t2 * 128:(t2 + 1) * 128])

    def consume_fm(self, tb, l, q, pp):
        self.P.tag = 'evac_fm'
        P = self.P
        if q < 2:
            P.act(self.hs(q), pp[:, :], AF.Copy)
            P.copy("dve", self.hs(4 + q, r=False), pp[:, :])
            P.act(self.hs(2 + q, p0=64, p1=128), pp[64:128, :], AF.Copy)
            P.copy("dve", self.hs(2 + q, p0=64, p1=96), self.zeros[64:96, :])
        elif q < 4:
            P.copy("act", self.ar(8 + q - 2), pp[:, :])
        elif q < 6:
            P.copy("act", self.ar(10 + q - 4), pp[:, :])
        elif q < 8:
            t = q - 6
            P.tt("dve", self.ar(12 + t), pp[:, :], self.ar(8 + t), ALU.mult)
            self.conv(tb, l, t)
        else:
            t = q - 8
            zd = 8 + t
            P.copy("act", self.ar(zd), pp[:, :])
            P.tt("dve", self.ar(15, 1, 512), self.ar(zd, 0, 511), self.ar(zd, 1, 512), ALU.subtract)
            P.tt("dve", self.ar(15, 0, 1), self.shift_st[:, l, t:t + 1], self.ar(zd, 0, 1), ALU.subtract)
            mu = self.PT[:, 1, 32 + l * 7 + t:33 + l * 7 + t]
            P.stt("dve", self.ar(t), self.ar(15), mu, self.ar(zd), ALU.mult, ALU.add)
            P.copy("act", self.shift_st[:, l, t:t + 1], self.ar(zd, 511, 512))

    def conv(self, tb, l, t):
        self.P.tag = 'conv'
        P = self.P
        z = 12 + t
        acc = 8 + t
        w = lambda j: self.PT[:, 0, 80 + l * 6 + j * 2 + t:81 + l * 6 + j * 2 + t]
        cb = self.PT[:, 0, 72 + l * 2 + t:73 + l * 2 + t]
        cst = lambda a, b: self.conv_st[:, l, t, a:b]
        P.ts("dve", self.ar(acc), self.ar(z), w(2), ALU.mult, cb, ALU.add)
        P.stt("dve", self.ar(acc, 1, 512), self.ar(z, 0, 511), w(1), self.ar(acc, 1, 512), ALU.mult, ALU.add)
        P.stt("dve", self.ar(acc, 0, 1), cst(1, 2), w(1), self.ar(acc, 0, 1), ALU.mult, ALU.add)
        P.stt("dve", self.ar(acc, 2, 512), self.ar(z, 0, 510), w(0), self.ar(acc, 2, 512), ALU.mult, ALU.add)
        P.stt("dve", self.ar(acc, 0, 2), cst(0, 2), w(0), self.ar(acc, 0, 2), ALU.mult, ALU.add)
        P.tt("dve", self.yT.r()[:, 4 + t, :], self.ar(acc), self.ar(10 + t), ALU.mult)
        P.copy("act", cst(0, 2), self.ar(z, 510, 512))

    def s5_ctpad(self, l, half):
        P = self.P
        P.tag = 's5'
        for ri in range(2):
            sl = 12 + ri
            P.copy("dve", self.ar(sl, r=True), self.zeros[:, :])
            for jl in range(4):
                c = jl * 128 + jl * 32
                b0 = 512 + (half * 2 + ri) * 128 + jl * 32
                P.copy("act", self.ar(sl, c, c + 32, r=True), self.LC[:, b0:b0 + 32])

    def s5_a(self, l, j):
        P = self.P
        P.tag = 's5'
        LCr = self.LCR
        etab = self.scr["etab"].ap()
        half, jl = divmod(j, 4)
        pr = self.ps()
        pi = self.ps()
        rows = slice(32 * jl, 32 * jl + 32)
        for ri, pz in ((0, pr), (1, pi)):
            c0 = 1024 + (half * 2 + ri) * 128
            if jl < 3:
                P.mm(pz[:, :], V(LCr.h[rows, c0:c0 + 128], LCr.keys), self.hs(half, 0, 512, 32 * jl, 32 * jl + 32))
            else:
                P.mm(pz[:, :], LCr[64:128, c0:c0 + 128], self.hs(2 + half, 0, 512, 64, 128))
        P.copy("act", self.ar(2), pr[:, :])
        P.copy("act", self.ar(3), pi[:, :])
        yield
        etv = self.arn(0, 2)
        P.dma("sp", V(etv.ap.rearrange("p a b -> p (a b)"), etv.keys), V(etab[l, j], ("etab",)), "et0")
        yield
        Ec, Es = self.ar(0), self.ar(1)
        A, B, C, Dd = self.ar(2), self.ar(3), self.ar(8), self.ar(9)
        P.tt("dve", C, B, Ec, ALU.mult)
        yield
        P.tt("dve", Dd, A, Es, ALU.mult)
        yield
        P.tt("dve", A, A, Ec, ALU.mult)
        yield
        P.tt("dve", B, B, Es, ALU.mult)
        yield
        P.tt("dve", A, A, B, ALU.add)
        yield
        P.tt("dve", C, C, Dd, ALU.subtract)
        yield

    def s5_b(self, l, j):
        P = self.P
        P.tag = 's5'
        half, jl = divmod(j, 4)
        Ec, Es = self.ar(0), self.ar(1)
        A, B, C, Dd = self.ar(2), self.ar(3), self.ar(8), self.ar(9)
        rho = bc(self.SPt[:, self.I_MAG, l * 8 + j:l * 8 + j + 1], [128, 512])
        P.scan(B, rho, A, self.ssm_st[:, l, j, 0:1], ALU.mult, ALU.add)
        yield
        P.scan(Dd, rho, C, self.ssm_st[:, l, j, 1:2], ALU.mult, ALU.add)
        yield
        P.tt("dve", A, B, Ec, ALU.mult)
        yield
        P.tt("dve", C, Dd, Es, ALU.mult)
        yield
        P.tt("dve", self.ar(10, r=True), A, C, ALU.subtract)
        yield
        P.tt("dve", self.ssm_st[:, l, j, 0:1], self.ar(2, 511, 512), self.ar(8, 511, 512), ALU.subtract)
        yield
        P.tt("dve", A, Dd, Ec, ALU.mult)
        yield
        P.tt("dve", C, B, Es, ALU.mult)
        yield
        P.tt("dve", self.ar(11, r=True), A, C, ALU.add)
        yield
        P.tt("dve", self.ssm_st[:, l, j, 1:2], self.ar(2, 511, 512), self.ar(8, 511, 512), ALU.add)
        yield

    def s5_c(self, l, j):
        P = self.P
        P.tag = 's5'
        half, jl = divmod(j, 4)
        if jl == 0:
            self.s5_ctpad(l, half)
        pc = self.ps()
        for ri in range(2):
            P.mm(pc[:, :], self.ar(12 + ri, jl * 128, (jl + 1) * 128, r=True), self.ar(10 + ri, r=True), ri == 0, ri == 1)
        if jl == 0:
            P.copy("act", self.ar(14 + half), pc[:, :])
            yield
        else:
            P.tt("dve", self.ar(14 + half), self.ar(14 + half), pc[:, :], ALU.add)
            yield

    def s5_tail(self, l):
        P = self.P
        P.tag = 's5'
        LCr = self.LCR
        for half in range(2):
            dcol = self.PT[:, 1, 16 + l * 2 + half:17 + l * 2 + half]
            P.stt("dve", self.ar(8 + half), self.hs(4 + half, r=False), dcol, self.ar(14 + half), ALU.mult, ALU.add)
            P.act(self.ar(8 + half), self.ar(8 + half), AF.Gelu_apprx_tanh)
            P.copy("act", self.ar(2 + half, r=True), self.ar(8 + half))
        for t in range(2):
            pg = self.ps()
            for kc in range(2):
                c0 = 1536 + kc * 256 + t * 128
                P.mm(pg[:, :], LCr[:, c0:c0 + 128], self.ar(2 + kc, r=True), kc == 0, kc == 1)
            gb = self.PT[:, 1, 24 + l * 2 + t:25 + l * 2 + t]
            P.act(self.ar(10 + t), pg[:, :], AF.Sigmoid, bias=gb)
            P.tt("dve", self.yT.r()[:, 2 + t, :], self.ar(8 + t), self.ar(10 + t), ALU.mult)

    def rwkv(self, tb, l, phase):
        self.P.tag = 'rwkv_prep'
        self.ps_n = 4
        P = self.P
        lr = self.lora.r()
        LX = 7
        if phase == 0:
            P.act(self.ar(LX, p0=0, p1=32, r=True), self.ar(6, p0=0, p1=32), AF.Tanh)
            P.act(self.ar(LX, p0=32, p1=64, r=True), self.ar(6, p0=32, p1=64), AF.Copy)
            P.act(self.ar(LX, p0=64, p1=128, r=True), self.ar(6, p0=64, p1=128), AF.Sigmoid)
        LD, AA, KK, KF, BV, CL, GI, GP, TMP = 8, 9, 10, 11, 12, 13, 14, 15, 15

        def prep_pair(hp):
            P.tag = 'rwkv_prep'
            AT, RT_, KH, BH, G, RK = (16 + 6 * hp + k for k in range(6))
            r_, k_, v_ = self.ar(hp), self.ar(2 + hp), self.ar(4 + hp)
            col = lambda g, base: self.PT[:, g, base + l * 2 + hp:base + l * 2 + hp + 1]
            pq = self.ps()
            P.mm(pq[:, :], lr[0:32, l, hp * 128:(hp + 1) * 128], self.ar(LX, p0=0, p1=32, r=True))
            P.act(self.ar(LD), pq[:, :], AF.Sigmoid, bias=col(0, 104))
            P.act(self.ar(LD), self.ar(LD), AF.Copy, scale=-0.6065306597126334)
            pa = self.ps()
            P.mm(pa[:, :], lr[32:64, l, hp * 128:(hp + 1) * 128], self.ar(LX, p0=32, p1=64, r=True))
            P.act(self.ar(AA), pa[:, :], AF.Sigmoid, bias=col(0, 112))
            P.act(self.ar(KK), k_, AF.Identity, scale=col(0, 120))
            P.act(self.ar(TMP), self.ar(KK), AF.Square)
            pn = self.ps()
            P.mm(pn[:, :], self.bones[:, :], self.ar(TMP))
            P.act(self.ar(TMP), pn[:, :], AF.Sqrt)
            P.ts("dve", self.ar(TMP), self.ar(TMP), 1e-12, ALU.max)
            P.recip(self.ar(TMP), self.ar(TMP))
            P.tt("dve", self.ar(KK), self.ar(KK), self.ar(TMP), ALU.mult)
            P.act(self.ar(TMP), self.ar(AA), AF.Identity, scale=col(1, 0), bias=col(2, 48))
            P.tt("dve", self.ar(KF), k_, self.ar(TMP), ALU.mult)
            P.tt("dve", self.ar(BV), self.ar(KK), self.ar(AA), ALU.mult)
            P.tt("dve", self.ar(TMP), r_, self.ar(KF), ALU.mult)
            P.act(self.ar(RK), self.ar(TMP), AF.Identity, scale=col(1, 8))
            for c in range(4):
                cs = (c * 128, (c + 1) * 128)
                P.scan(self.ar(CL, *cs), self.ones[:, :], self.ar(LD, *cs), 0.0, ALU.mult, ALU.add)
            P.act(self.ar(G), self.ar(CL), AF.Exp)
            P.act(self.ar(GI), self.ar(CL), AF.Exp, scale=-1.0)
            P.tt("dve", self.ar(GP), self.ar(CL), self.ar(LD), ALU.subtract)
            P.act(self.ar(GP), self.ar(GP), AF.Exp)
            P.stt("dve", self.ar(AT), self.ar(KK), -1.0, self.ar(GP), ALU.mult, ALU.mult)
            P.tt("dve", self.ar(RT_), r_, self.ar(G), ALU.mult)
            P.tt("dve", self.ar(KH), self.ar(KF), self.ar(GI), ALU.mult)
            P.tt("dve", self.ar(BH), self.ar(BV), self.ar(GI), ALU.mult)

        def post_a(c):
            P.tag = 'rwkv_post'
            cs = (c * 128, (c + 1) * 128)
            O = self.PSL[c]
            ms = self.mstat
            for h in range(4):
                P.bn_stats(ms[:, h, 0:6], O[:, h * 64:(h + 1) * 64])
                P.bn_aggr(ms[:, h, 6:8], ms[:, h, 0:6])
            P.act(self.small[:, 24:28], V(ms.h[:, :, 7], ms.keys), AF.Sqrt, bias=64e-5)
            P.recip(self.small[:, 28:32], self.small[:, 24:28])
            ON = 6 + (c % 2)
            on = lambda a, b: self.hs(ON, a, b, r=False)
            for h in range(4):
                P.ts("dve", on(h * 64, (h + 1) * 64), O[:, h * 64:(h + 1) * 64], ms[:, h, 6:7], ALU.subtract,
                     self.small[:, 28 + h:29 + h], ALU.mult)
            P.tt("dve", on(0, 256), on(0, 256), self.BCP[:, 2, :], ALU.mult)
            P.tt("dve", on(0, 256), on(0, 256), self.BCP[:, 3, :], ALU.add)
            pb = self.ps()
            for hp in range(2):
                P.mm(pb[:, hp * 2:(hp + 1) * 2], self.ar(16 + 6 * hp + 5, *cs), self.sel[:, :])
            P.copy("act", self.small[:, 32:36], pb[:, 0:4])
            pv = self.ps()
            for hp in range(2):
                P.transpose(pv[:, hp * 128:(hp + 1) * 128], self.ar(4 + hp, *cs), self.ident[:, :])
            for h in range(4):
                P.stt("dve", on(256 + h * 64, 256 + (h + 1) * 64), pv[:, h * 64:(h + 1) * 64],
                      self.small[:, 32 + h:33 + h], on(h * 64, (h + 1) * 64), ALU.mult, ALU.add)
            pg = self.ps()
            P.mm(pg[:, 0:256], self.ar(LX, cs[0], cs[1], 64, 128, r=True), lr[64:128, l, :])
            P.tt("dve", on(0, 256), on(256, 512), pg[:, 0:256], ALU.mult)

        def post_b(c):
            P.tag = 'rwkv_post'
            cs = (c * 128, (c + 1) * 128)
            ON = 6 + (c % 2)
            pt = self.ps()
            for t2 in range(2):
                P.transpose(pt[:, t2 * 128:(t2 + 1) * 128], self.hs(ON, t2 * 128, (t2 + 1) * 128, r=False), self.ident[:, :])
            yr = self.yT.r()
            for t2 in range(2):
                P.copy("act", yr[:, 6 + t2, cs[0]:cs[1]], pt[:, t2 * 128:(t2 + 1) * 128])

        seq = [(hp, c) for hp in range(2) for c in range(4)]
        after = {5: [lambda: post_a(0)], 6: [lambda: post_b(0), lambda: post_a(1)],
                 7: [lambda: post_b(1), lambda: post_a(2), lambda: post_a(3), lambda: post_b(2), lambda: post_b(3)]}
        if phase == 0:
            prep_pair(0)
            prep_pair(1)
            return
        import itertools

        def drain(g, k=None):
            cnt = 0
            for _ in g:
                cnt += 1
                if k is not None and cnt >= k:
                    break

        drain(self.s5_a(l, 0))
        self.rwkv_front(l, 0, 0, 0)
        for n, (hp, c) in enumerate(seq):
            tails = self.rwkv_tail(l, hp, c, n % 2)
            gens = []
            if n >= 1:
                gens.append(self.s5_c(l, n - 1))
            gens.append(self.s5_b(l, n))
            if n + 1 < 8:
                gens.append(self.s5_a(l, n + 1))
            g = itertools.chain(*gens)
            if n + 1 < len(seq):
                hp2, c2 = seq[n + 1]

                def hook(lev, tails=tails, g=g):
                    if 1 <= lev <= 3:
                        tails[lev - 1]()
                    drain(g, {0: 4, 1: 2, 2: 2, 3: 2}.get(lev, 4))
                self.rwkv_front(l, hp2, c2, (n + 1) % 2, hook)
                drain(g)
            else:
                for t in tails:
                    t()
                drain(g)
                drain(self.s5_c(l, n))
            for f in after.get(n, []):
                f()
        self.s5_tail(l)

    def rwkv_front(self, l, hp, c, st, hook=None):
        self.P.tag = 'rwkv_chunk'
        P = self.P
        AT, RT_, KH, BH, G, RK = (16 + 6 * hp + k for k in range(6))
        VS = 4 + hp
        cs = (c * 128, (c + 1) * 128)
        VT = 4 * st
        pp = self.ps()
        P.transpose(pp[:, 0:128], self.ar(VS, *cs), self.ident[:, :])
        P.transpose(pp[:, 128:256], self.ar(KH, *cs), self.ident[:, :])
        P.transpose(pp[:, 256:384], self.ar(BH, *cs), self.ident[:, :])
        P.copy("act", self.mq(VT, n=3), pp[:, 0:384])
        H = []
        for h2 in range(2):
            p0, p1 = h2 * 64, h2 * 64 + 64
            d = dict(zip(("AKT", "RKT", "RBT", "Z"), (8 + st * 8 + h2 * 4 + k for k in range(4))))
            d.update(zip(("L1", "U1", "La", "Lb", "Ua", "Ub"), (24 + h2 * 6 + k for k in range(6))))
            d.update(p0=p0, p1=p1, at=self.ar(AT, cs[0], cs[1], p0, p1), rt=self.ar(RT_, cs[0], cs[1], p0, p1),
                     kh=self.ar(KH, cs[0], cs[1], p0, p1), bh=self.ar(BH, cs[0], cs[1], p0, p1))
            H.append(d)
        if hook is not None:
            hook(0)
            self.P.tag = 'rwkv_chunk'
        banks = []
        for d in H:
            pA = self.ps()
            pB = self.ps()
            P.mm(pA[:, 0:128], d["at"], d["bh"])
            P.mm(pA[:, 128:256], d["bh"], d["at"])
            P.mm(pA[:, 256:384], d["kh"], d["at"])
            P.mm(pA[:, 384:512], d["kh"], d["rt"])
            P.mm(pB[:, 0:128], d["bh"], d["rt"])
            banks.append((pA, pB))
        for d, (pA, pB) in zip(H, banks):
            P.tt("dve", self.mq(d["U1"]), pA[:, 128:256], self.m_lt[:, :], ALU.mult)
            P.tt("dve", self.mq(d["L1"]), pA[:, 0:128], self.m_gt[:, :], ALU.mult)
            P.tt("dve", self.mq(d["Z"]), self.mq(d["U1"]), self.ident[:, :], ALU.add)
        for d, (pA, pB) in zip(H, banks):
            P.tt("dve", self.mq(d["AKT"]), pA[:, 256:384], self.m_lt[:, :], ALU.mult)
            P.tt("dve", self.mq(d["RKT"]), pA[:, 384:512], self.m_le[:, :], ALU.mult)
            P.tt("dve", self.mq(d["RBT"]), pB[:, 0:128], self.m_le[:, :], ALU.mult)
        for d in H:
            d["Lp"], d["Up"] = d["L1"], d["U1"]
        for lev in range(1, 7):
            pLs = []
            for d in H:
                Ln = d["La"] if lev % 2 == 0 else d["Lb"]
                Un = d["Ua"] if lev % 2 == 0 else d["Ub"]
                pL = self.ps()
                P.mm(pL[:, 0:128], self.mq(d["Up"]), self.mq(d["Lp"]))
                if lev < 6:
                    P.mm(pL[:, 128:256], self.mq(d["Lp"]), self.mq(d["Up"]))
                pLs.append((pL, Ln, Un))
            for d, (pL, Ln, Un) in zip(H, pLs):
                P.copy("act", self.mq(Ln), pL[:, 0:128])
                if lev < 6:
                    P.copy("act", self.mq(Un), pL[:, 128:256])
            pZs = []
            for d, (pL, Ln, Un) in zip(H, pLs):
                pZ = self.ps()
                P.mm(pZ[:, 0:128], self.mq(Ln), self.mq(d["Z"]))
                pZs.append(pZ)
                d["Lp"], d["Up"] = Ln, Un
            for d, pZ in zip(H, pZs):
                P.tt("dve", self.mq(d["Z"]), self.mq(d["Z"]), pZ[:, 0:128], ALU.add)
            if hook is not None:
                hook(lev)
                self.P.tag = 'rwkv_chunk'

    def rwkv_tail(self, l, hp, c, st):
        P = self.P
        AT, RT_, KH, BH, G, RK = (16 + 6 * hp + k for k in range(6))
        cs = (c * 128, (c + 1) * 128)
        VT, KT, BT, WT = (4 * st + k for k in range(4))
        RH = 36
        S0p = self.wkv_st[:, l, hp, :]
        q = lambda name, h2: 8 + st * 8 + h2 * 4 + ("AKT", "RKT", "RBT", "Z").index(name)

        def t1():
            P.tag = 'rwkv_tail'
            pR = self.ps()
            P.mm(pR[:, 0:128], self.ar(AT, *cs), S0p, True, False)
            for h2 in range(2):
                P.mm(pR[:, h2 * 64:h2 * 64 + 64], self.mq(q("AKT", h2)), self.mq(VT, h2 * 64, h2 * 64 + 64), False, h2 == 1)
            P.copy("act", self.mq(RH), pR[:, 0:128])

        def t2():
            P.tag = 'rwkv_tail'
            pW = self.ps()
            for h2 in range(2):
                P.mm(pW[:, h2 * 64:h2 * 64 + 64], self.mq(q("Z", h2)), self.mq(RH, h2 * 64, h2 * 64 + 64))
            P.copy("act", self.mq(WT), pW[:, 0:128])

        def t3():
            P.tag = 'rwkv_tail'
            O = self.PSL[c]
            P.mm(O[:, hp * 128:(hp + 1) * 128], self.ar(RT_, *cs), S0p, True, False)
            for h2 in range(2):
                oc = (hp * 2 + h2) * 64
                P.mm(O[:, oc:oc + 64], self.mq(q("RKT", h2)), self.mq(VT, h2 * 64, h2 * 64 + 64), False, False)
                P.mm(O[:, oc:oc + 64], self.mq(q("RBT", h2)), self.mq(WT, h2 * 64, h2 * 64 + 64), False, h2 == 1)
            pS = self.ps()
            P.mm(pS[:, 0:128], self.mq(KT), self.mq(VT), True, False)
            P.mm(pS[:, 0:128], self.mq(BT), self.mq(WT), False, True)
            for h2 in range(2):
                p0, p1 = h2 * 64, h2 * 64 + 64
                TM = 37 + h2
                Sd = self.wkv_st[p0:p1, l, hp, h2 * 64:h2 * 64 + 64]
                P.tt("dve", self.mq(TM, 0, 64, p0, p1), pS[p0:p1, h2 * 64:h2 * 64 + 64], Sd, ALU.add)
                P.ts("dve", Sd, self.mq(TM, 0, 64, p0, p1), self.ar(G, cs[1] - 1, cs[1], p0, p1), ALU.mult)

        return [t1, t2, t3]

    def prompt_state_out(self, l):
        self.P.tag = 'state_out'
        P = self.P
        o = self.dout
        nsc = dict(allow_slow_non_contiguous=True)
        for hp in range(2):
            pp = self.ps()
            P.transpose(pp[:, 0:128], self.wkv_st[:, l, hp, :], self.ident[:, :])
            P.copy("act", self.mq(27), pp[:, 0:128])
            for h2 in range(2):
                P.dma("sp", V(o["wkv_p"].ap()[l, 2 * hp + h2], ("wkv_p",)),
                      self.mq(27, h2 * 64, h2 * 64 + 64, h2 * 64, h2 * 64 + 64), "st_wkv")
        P.dma("sp", V(o["shift_p"].ap()[l].rearrange("(t p) -> p t", p=128), ("shift_p",)), self.shift_st[:, l, :], "stp_sh", **nsc)
        for ri, nm in enumerate(("re_p", "im_p")):
            dst = o[nm].ap()[l].rearrange("(j g) p -> (g p) j", g=2)
            P.dma("sp", V(dst, (nm,)), self.ssm_st[:, l, :, ri], "stp_s%d" % ri, **nsc)
        for t in range(2):
            dst = o["conv_p"].ap()[l][:, t * 128:(t + 1) * 128].rearrange("j c -> c j")
            P.dma("sp", V(dst, ("conv_p",)), self.conv_st[:, l, t, :], "stp_c%d" % t, **nsc)


def _flat(v):
    return V(v.ap.rearrange("p a b -> p (a b)"), v.keys)


class SampleMixin:
    def tk(self, s0, n, c0, c1):
        hh = self.arena.h[0:NS, s0:s0 + n, :].rearrange("p a b -> p (a b)")[:, c0:c1]
        return V(hh, tuple(("ar", j) for j in range(s0, s0 + n)))

    def tok2fm(self, src_fn, k, dst):
        P = self.P
        pp = self.ps()
        for j in range(k):
            P.transpose(pp[:, j * NS:(j + 1) * NS], src_fn(j), self.ident[0:NS, 0:NS])
        P.copy("act", dst, V(pp.h[:, 0:k * NS].rearrange("p (k c) -> p k c", k=k), pp.keys))

    def s_norm(self, gbase):
        self.P.tag = 's_norm'
        P = self.P
        x = self.tk(0, 2, 0, D)
        h = self.tk(9, 2, 0, D)
        sm = self.small
        P.act(h, x, AF.Square, accum_out=sm[0:NS, 0:1])
        P.act(sm[0:NS, 4:5], sm[0:NS, 0:1], AF.Sqrt, scale=1.0 / D, bias=1e-6)
        P.recip(sm[0:NS, 8:9], sm[0:NS, 4:5])
        P.ts("dve", h, x, sm[0:NS, 8:9], ALU.mult)
        hr = self.hT.r()
        pp = self.ps()
        for kc in range(8):
            P.transpose(pp[:, kc * NS:(kc + 1) * NS], self.tk(9, 2, kc * 128, (kc + 1) * 128), self.ident[0:NS, 0:NS])
        for kc in range(8):
            P.act(hr[:, kc, 0:NS], pp[:, kc * NS:(kc + 1) * NS], AF.Identity,
                  scale=self.PT[:, 0, gbase + kc:gbase + kc + 1])

    def sample(self):
        P = self.P
        S = self.stages
        d = self.di

        def start(w):
            self.ps_n = 4
            P.dma("sp", self.tk(0, 2, 0, D), V(d("x_sample"), ()), "xs_ld")

        S.append((None, start))
        for l in range(self.nlayer):
            self.sample_layer(l)

        def fin(w):
            x = self.tk(0, 2, 0, D)
            h = self.tk(9, 2, 0, D)
            sm = self.small
            P.act(h, x, AF.Square, accum_out=sm[0:NS, 0:1])
            P.act(sm[0:NS, 4:5], sm[0:NS, 0:1], AF.Sqrt, scale=1.0 / D, bias=1e-6)
            P.recip(sm[0:NS, 8:9], sm[0:NS, 4:5])
            P.ts("dve", h, x, sm[0:NS, 8:9], ALU.mult)
            g = self.tk(11, 2, 0, D)
            P.dma("sp", g, V(d("norm_f_g").partition_broadcast(NS), ()), "gfb")
            P.tt("dve", h, h, g, ALU.mult)
            P.dma("sp", V(self.dout["y_s"].ap(), ("y_s",)), h, "st_ys")

        S.append((None, fin))

    def sample_layer(self, l):
        P = self.P
        S = self.stages
        d = self.di
        hr = self.hT.r()

        def pre(w):
            lcd = self.scr["lcd"].ap()
            P.dma("sp", self.LC[:, 0:1024], V(lcd[l][:, 0:1024], ("lcd",)), "lc_a")
            P.dma("pool", self.LCR[:, 1024:1536], V(self.scr["lcd"].bitcast(F32R).ap()[l][:, 1024:1536], ("lcd",)), "lc_b")
            P.dma("pool", V(self.LCR.h[:, 1536:2048].rearrange("p (k c) -> p k c", k=2), self.LC.keys),
                  V(self.dir_("ssm_glu_w")[l].rearrange("(k p) c -> p k c", p=128), ()), "lc_b")
            for i, nm in enumerate(("gm_ln_g", "gm_ln_b", "tm_ln_g", "tm_ln_b")):
                P.dma("sp", self.BCP[:, i, :], V(d(nm)[l].partition_broadcast(128), ()), "bcp")
            P.dma("sp", self.tk(15, 2, 0, DTM), V(d("tm_mu")[l].partition_broadcast(NS), ()), "sbc")
            for i, nm in enumerate(("tm_w0", "tm_a0", "tm_k_k", "tm_k_a", None, "tm_r_k")):
                if nm is None:
                    continue
                src = d(nm)[l]
                if nm == "tm_r_k":
                    src = src.rearrange("h n -> (h n)")
                P.dma("sp", self.tk(17, 3, i * 256, (i + 1) * 256), V(src.partition_broadcast(NS), ()), "sbc")
            for j in range(3):
                P.dma("sp", self.tk(20, 3, j * 256, (j + 1) * 256), V(d("conv_w")[l, j].partition_broadcast(NS), ()), "sbc")
            P.dma("sp", self.tk(20, 3, 768, 1024), V(d("conv_b")[l].partition_broadcast(NS), ()), "sbc")
            P.dma("sp", self.tk(20, 3, 1024, 1280), V(d("ssm_d")[l].rearrange("g h -> (g h)").partition_broadcast(NS), ()), "sbc")
            P.dma("sp", self.tk(20, 3, 1280, 1536), V(d("ssm_glu_b")[l].partition_broadcast(NS), ()), "sbc")
            nsc = dict(allow_slow_non_contiguous=True)
            P.dma("sp", self.small[0:NS, 40:44], V(d("gm_ws")[l, :, 0, 0:1].rearrange("h o -> o h").to_broadcast([NS, 4]), ()), "sbc", **nsc)
            P.dma("sp", self.small[0:NS, 44:48], V(d("gm_bs")[l, :, 0:1].rearrange("h o -> o h").to_broadcast([NS, 4]), ()), "sbc", **nsc)
            P.seal("sbc")
            P.ts("dve", self.tk(17, 3, 4 * 256, 5 * 256), self.tk(17, 3, 3 * 256, 4 * 256), -1.0, ALU.mult, 1.0, ALU.add)
            self.s_norm(l * 8)

        S.append((None, pre))
        for i in range(5):
            c0 = 512 * i
            ncol = min(512, INC - c0)

            def ld(c0=c0, ncol=ncol):
                return self.wload(self.dir_("w_in")[l][:, c0:c0 + ncol].rearrange("(k p) c -> p k c", p=128), 8, ncol)

            def comp(w, i=i, c0=c0, ncol=ncol):
                pp = self.ps()
                for kc in range(8):
                    P.mm(pp[0:NS, 0:ncol], hr[:, kc, 0:NS], w[:, kc, :], kc == 0, kc == 7)
                P.copy("act", self.tk(2, 5, c0, c0 + ncol), pp[0:NS, 0:ncol])
                if i == 4:
                    self.s_mixers(l)

            S.append((ld, comp))
        for half in range(2):
            def ld(half=half):
                return self.wload(self.dir_("w_out")[l][:, half * 512:(half + 1) * 512].rearrange("(k p) c -> p k c", p=128), 8, 512)

            def comp(w, half=half):
                pp = self.ps()
                for kc in range(8):
                    P.mm(pp[0:NS, :], hr[:, kc, NS:2 * NS], w[:, kc, :], kc == 0, kc == 7)
                xs = self.tk(0, 2, half * 512, (half + 1) * 512)
                P.tt("dve", xs, xs, pp[0:NS, :], ALU.add)
                if half == 1:
                    self.s_norm(32 + l * 8)

            S.append((ld, comp))
        for g in range(5):
            for which in range(2):
                def ld(g=g, which=which):
                    c0 = which * DFF + 512 * g
                    return self.wload(self.dir_("ffn_w_gu")[l][:, c0:c0 + 512].rearrange("(k p) c -> p k c", p=128), 8, 512)

                def comp(w, g=g, which=which):
                    pp = self.ps()
                    for kc in range(8):
                        P.mm(pp[0:NS, :], hr[:, kc, 0:NS], w[:, kc, :], kc == 0, kc == 7)
                    dst = self.tk(9, 6, 512 * g, 512 * g + 512)
                    if which == 0:
                        P.act(dst, pp[0:NS, :], AF.Silu)
                    else:
                        P.tt("dve", dst, dst, pp[0:NS, :], ALU.mult)

                S.append((ld, comp))

        def ld_last():
            buf = self.WB[self.wbi % NWB]
            self.wbi += 1
            view = buf.h.bitcast(F32R)[:, 0:4096].rearrange("p (k g c) -> p k g c", k=8, g=2)
            for g in range(2):
                c0 = g * DFF + 2560
                src = self.dir_("ffn_w_gu")[l][:, c0:c0 + 256].rearrange("(k p) c -> p k c", p=128)
                P.dma("pool", V(view[:, :, g, :], buf.keys), V(src, ()), "ld_" + buf.name)
            return TT(view, buf.name, buf.keys)

        def comp_last(w):
            pp = self.ps()
            for kc in range(8):
                P.mm(pp[0:NS, :], hr[:, kc, 0:NS], V(w.h[:, kc, :, :].rearrange("p g c -> p (g c)"), w.keys), kc == 0, kc == 7)
            tmp = self.tk(23, 1, 0, 256)
            P.act(tmp, pp[0:NS, 0:256], AF.Silu)
            P.tt("dve", self.tk(9, 6, 2560, 2816), tmp, pp[0:NS, 256:512], ALU.mult)
            yr = self.yT.r()
            dst = V(yr.h[:, 0, 0:352].rearrange("p (k c) -> p k c", k=22), yr.keys)
            self.tok2fm(lambda j: self.tk(9, 6, j * 128, (j + 1) * 128), 22, dst)

        S.append((ld_last, comp_last))
        for half in range(2):
            for jg in range(3):
                nj = 8 if jg < 2 else 6

                def ld(half=half, jg=jg, nj=nj):
                    src = self.dir_("ffn_w_down")[l][jg * 1024:jg * 1024 + nj * 128, half * 512:(half + 1) * 512]
                    return self.wload(src.rearrange("(j p) c -> p j c", p=128), nj, 512)

                def comp(w, half=half, jg=jg, nj=nj):
                    yr = self.yT.r()
                    for jj in range(nj):
                        j = jg * 8 + jj
                        P.mm(self.PSL[0][0:NS, :], V(yr.h[:, 0, j * NS:(j + 1) * NS], yr.keys), w[:, jj, :], j == 0, j == 21)
                    if jg == 2:
                        xs = self.tk(0, 2, half * 512, (half + 1) * 512)
                        P.tt("dve", xs, xs, self.PSL[0][0:NS, :], ALU.add)

                S.append((ld, comp))

    def s_mixers(self, l):
        self.P.tag = 's_mixers'
        P = self.P
        d = self.di
        o = self.dout
        hr = self.hT.r()
        lr = self.lora.r()
        sm = self.small
        ms = self.mstat
        Z = lambda a, b: self.tk(2, 5, a, b)
        M = lambda a, b: self.tk(7, 2, a, b)
        p6 = lambda i: self.tk(17, 3, i * 256, (i + 1) * 256)
        c6 = lambda i: self.tk(20, 3, i * 256, (i + 1) * 256)
        h4 = lambda v: V(v.ap.rearrange("p (h n) -> p h n", h=4), v.keys)
        b4 = lambda v: V(v.ap.unsqueeze(2).to_broadcast([NS, 4, 64]), v.keys)
        u = self.tk(9, 1, 0, 256)
        vf = self.tk(9, 1, 256, 512)
        P.act(u, Z(0, 256), AF.Gelu_apprx_tanh)
        P.act(vf, Z(256, 512), AF.Gelu_apprx_tanh)
        P.bn_stats(ms[0:NS, 0, 0:6], vf)
        P.bn_aggr(ms[0:NS, 0, 6:8], ms[0:NS, 0, 0:6])
        P.act(sm[0:NS, 16:17], ms[0:NS, 0, 7:8], AF.Sqrt, bias=1e-5)
        P.recip(sm[0:NS, 20:21], sm[0:NS, 16:17])
        P.ts("dve", vf, vf, ms[0:NS, 0, 6:7], ALU.subtract, sm[0:NS, 20:21], ALU.mult)
        P.tt("dve", vf, vf, self.BCP[0:NS, 0, :], ALU.mult)
        P.tt("dve", vf, vf, self.BCP[0:NS, 1, :], ALU.add)
        P.dma("sp", V(o["chv_s"].ap()[l], ("chv_s",)), vf, "st_chv")
        t = self.tk(10, 1, 0, 256)
        P.tt("dve", h4(t), h4(vf), b4(sm[0:NS, 40:44]), ALU.mult)
        P.tt("dve", h4(t), h4(t), b4(sm[0:NS, 44:48]), ALU.add)
        P.tt("dve", M(0, 256), u, t, ALU.mult)
        if 'sm1' in DBG:
            return
        buf = self.tk(23, 1, 0, 512)
        P.dma("sp", buf, V(d("state_conv")[l].rearrange("b j c -> b (j c)"), ()), "s_ld_c")
        zz = self.tk(10, 1, 256, 512)
        P.tt("dve", zz, Z(1280, 1536), Z(768, 1024), ALU.mult)
        y = self.tk(11, 1, 0, 256)
        t2 = self.tk(11, 1, 256, 512)
        P.tt("dve", y, zz, c6(2), ALU.mult)
        P.tt("dve", t2, self.tk(23, 1, 0, 256), c6(0), ALU.mult)
        P.tt("dve", y, y, t2, ALU.add)
        P.tt("dve", t2, self.tk(23, 1, 256, 512), c6(1), ALU.mult)
        P.tt("dve", y, y, t2, ALU.add)
        P.tt("dve", y, y, c6(3), ALU.add)
        P.tt("dve", M(512, 768), y, Z(1024, 1280), ALU.mult)
        P.dma("sp", V(o["conv_s"].ap()[l][:, 0, :], ("conv_s",)), self.tk(23, 1, 256, 512), "st_cv0")
        P.dma("sp", V(o["conv_s"].ap()[l][:, 1, :], ("conv_s",)), zz, "st_cv1")
        if 'sm2' in DBG:
            return
        pp = self.ps()
        for j in range(2):
            P.transpose(pp[:, j * NS:(j + 1) * NS], Z(512 + j * 128, 640 + j * 128), self.ident[0:NS, 0:NS])
        ppv = lambda p0, p1: V(pp.h[p0:p1, 0:2 * NS].rearrange("p (k c) -> p k c", k=2), pp.keys)
        P.copy("act", hr[:, 0:2, 32:48], ppv(0, 128))
        P.copy("act", hr[64:128, 0:2, 48:64], ppv(64, 128))
        P.copy("dve", hr[64:96, 0:2, 48:64], V(self.zeros.h[64:96, 0:32].rearrange("p (k c) -> p k c", k=2), self.zeros.keys))
        if 'sm21' in DBG:
            return
        P.dma("sp", self.tk(9, 2, 0, 1024), V(d("state_ssm_re")[l].rearrange("b g p -> b (g p)"), ()), "s_ld_re")
        P.dma("sp", self.tk(11, 2, 0, 1024), V(d("state_ssm_im")[l].rearrange("b g p -> b (g p)"), ()), "s_ld_im")
        m3 = lambda q: V(self.mat.h[:, q, :].rearrange("p (j c) -> p j c", j=8), (("m", q),))
        self.tok2fm(lambda j: self.tk(9, 2, j * 128, (j + 1) * 128), 8, m3(0))
        self.tok2fm(lambda j: self.tk(11, 2, j * 128, (j + 1) * 128), 8, m3(1))
        if 'sm22' in DBG:
            return
        pbs = [self.ps() for _ in range(4)]
        for j in range(8):
            half, jl = divmod(j, 4)
            for ri in range(2):
                c0 = 1024 + (half * 2 + ri) * 128
                col = (half * 2 + ri) * NS
                if jl < 3:
                    rows = slice(32 * jl, 32 * jl + 32)
                    P.mm(pbs[jl][:, col:col + NS], V(self.LCR.h[rows, c0:c0 + 128], self.LCR.keys), hr[rows, half, 32:48])
                else:
                    P.mm(pbs[jl][:, col:col + NS], self.LCR[64:128, c0:c0 + 128], hr[64:128, half, 48:64])
        pbv = lambda jl, ri: V(pbs[jl].h[:, 0:64].rearrange("p (h r c) -> p h r c", h=2, r=2)[:, :, ri, :], pbs[jl].keys)
        m3j = lambda q, jl: V(self.mat.h[:, q, :].rearrange("p (h j c) -> p h j c", h=2, j=4)[:, :, jl, :], (("m", q),))
        lb = lambda idx: V(self.SPt.h[:, idx, l * 8:(l + 1) * 8].unsqueeze(2).to_broadcast([128, 8, NS]), self.SPt.keys)
        lbr, lbi = lb(self.I_LBR), lb(self.I_LBI)
        P.tt("dve", m3(4), m3(0), lbr, ALU.mult)
        P.tt("dve", m3(5), m3(1), lbi, ALU.mult)
        P.tt("dve", m3(4), m3(4), m3(5), ALU.subtract)
        for jl in range(4):
            P.tt("dve", m3j(2, jl), m3j(4, jl), pbv(jl, 0), ALU.add)
        P.tt("dve", m3(4), m3(1), lbr, ALU.mult)
        P.tt("dve", m3(5), m3(0), lbi, ALU.mult)
        P.tt("dve", m3(4), m3(4), m3(5), ALU.add)
        for jl in range(4):
            P.tt("dve", m3j(3, jl), m3j(4, jl), pbv(jl, 1), ALU.add)
        if 'sm23' in DBG:
            return
        py = self.ps()
        for j in range(8):
            half, jl = divmod(j, 4)
            for ri in range(2):
                c0 = 512 + (half * 2 + ri) * 128 + jl * 32
                P.mm(py[0:NS, half * 128 + jl * 32:half * 128 + jl * 32 + 32],
                     V(self.mat.h[:, 2 + ri, j * NS:(j + 1) * NS], (("m", 2 + ri),)), self.LC[:, c0:c0 + 32], ri == 0, ri == 1)
        yv = self.tk(10, 1, 0, 256)
        P.tt("dve", yv, Z(512, 768), c6(4), ALU.mult)
        P.tt("dve", yv, yv, py[0:NS, 0:256], ALU.add)
        P.act(yv, yv, AF.Gelu_apprx_tanh)
        if 'sm24' in DBG:
            return
        self.tok2fm(lambda j: self.tk(10, 1, j * 128, (j + 1) * 128), 2, hr[:, 2:4, 32:48])
        pg = self.ps()
        for kc in range(2):
            P.mm(pg[0:NS, 0:256], hr[:, 2 + kc, 32:48], self.LCR[:, 1536 + kc * 256:1792 + kc * 256], kc == 0, kc == 1)
        P.tt("dve", y, pg[0:NS, 0:256], c6(5), ALU.add)
        P.act(y, y, AF.Sigmoid)
        P.tt("dve", M(256, 512), yv, y, ALU.mult)
        if 'sm25' in DBG:
            return
        for ri, nm in enumerate(("re_s", "im_s")):
            pa_, pb_ = self.ps(), self.ps()
            for j in range(8):
                bank = pa_ if j < 4 else pb_
                P.transpose(bank[0:NS, (j % 4) * 128:(j % 4 + 1) * 128],
                            V(self.mat.h[:, 2 + ri, j * NS:(j + 1) * NS], (("m", 2 + ri),)), self.ident[:, :])
            so = self.tk(13, 2, 0, 1024)
            P.copy("act", self.tk(13, 2, 0, 512), pa_[0:NS, :])
            P.copy("act", self.tk(13, 2, 512, 1024), pb_[0:NS, :])
            P.dma("sp", V(o[nm].ap()[l].rearrange("b g p -> b (g p)"), (nm,)), so, "st_s%d" % ri)
        if 'sm3' in DBG:
            return
        zs = lambda a, b: self.tk(13, 2, a, b)
        zsa = zs(0, DTM)
        zd = Z(1536, 2432)
        P.dma("sp", zsa, V(d("state_shift")[l], ()), "s_ld2")
        P.dma("sp", V(o["shift_s"].ap()[l], ("shift_s",)), zd, "st_sh")
        P.tt("dve", zsa, zsa, zd, ALU.subtract)
        P.tt("dve", zsa, zsa, self.tk(15, 2, 0, DTM), ALU.mult)
        P.tt("dve", zsa, zsa, zd, ALU.add)
        lx = self.tk(23, 1, 0, 128)
        P.act(self.tk(23, 1, 0, 32), zs(768, 800), AF.Tanh)
        P.act(self.tk(23, 1, 32, 64), zs(800, 832), AF.Copy)
        P.act(self.tk(23, 1, 64, 128), zs(832, 896), AF.Sigmoid)
        pp = self.ps()
        P.transpose(pp[:, 0:NS], lx, self.ident[0:NS, 0:NS])
        P.copy("act", hr[:, 4, 32:48], pp[:, 0:NS])
        pq = self.ps()
        P.mm(pq[0:NS, 0:256], hr[0:32, 4, 32:48], lr[0:32, l, :])
        pa = self.ps()
        P.mm(pa[0:NS, 0:256], hr[32:64, 4, 32:48], lr[32:64, l, :])
        P.mm(self.PSL[1][0:NS, 0:256], hr[64:128, 4, 32:48], lr[64:128, l, :])
        vec = lambda j: self.tk(20, 3, j * 256, (j + 1) * 256)
        t = self.tk(9, 1, 0, 256)
        A = self.tk(9, 1, 256, 512)
        P.tt("dve", t, pq[0:NS, 0:256], p6(0), ALU.add)
        P.act(t, t, AF.Sigmoid)
        P.act(vec(1), t, AF.Exp, scale=-0.6065306597126334)
        P.tt("dve", A, pa[0:NS, 0:256], p6(1), ALU.add)
        P.act(A, A, AF.Sigmoid)
        kk = self.tk(10, 1, 0, 256)
        sq = self.tk(10, 1, 256, 512)
        P.tt("dve", kk, zs(256, 512), p6(2), ALU.mult)
        P.tt("dve", sq, kk, kk, ALU.mult)
        P.reduce(sm[0:NS, 24:28], h4(sq), ALU.add)
        P.act(sm[0:NS, 24:28], sm[0:NS, 24:28], AF.Sqrt)
        P.ts("dve", sm[0:NS, 24:28], sm[0:NS, 24:28], 1e-12, ALU.max)
        P.recip(sm[0:NS, 28:32], sm[0:NS, 24:28])
        P.tt("dve", h4(kk), h4(kk), b4(sm[0:NS, 28:32]), ALU.mult)
        P.tt("dve", t, A, p6(3), ALU.mult)
        P.tt("dve", t, t, p6(4), ALU.add)
        P.tt("dve", vec(2), zs(256, 512), t, ALU.mult)
        P.ts("dve", vec(4), kk, -1.0, ALU.mult)
        P.tt("dve", vec(5), kk, A, ALU.mult)
        P.copy("act", vec(0), zs(0, 256))
        P.copy("act", vec(3), zs(512, 768))
        P.tt("dve", t, zs(0, 256), vec(2), ALU.mult)
        P.tt("dve", t, t, p6(5), ALU.mult)
        P.reduce(sm[0:NS, 32:36], h4(t), ALU.add)
        if 'sm4' in DBG:
            return
        svec = self.scr["svec"].ap()
        P.dma("sp", V(svec.rearrange("j b c -> b j c"), ("svec",)),
              V(self.tk(20, 3, 0, 1536).ap.rearrange("p (j c) -> p j c", j=6), self.tk(20, 3, 0, 1536).keys), "sv_w")
        hi = lambda s0, n, c0, c1: V(self.arena.h[64:128, s0:s0 + n, :].rearrange("p a b -> p (a b)")[:, c0:c1],
                                     tuple(("ar", j) for j in range(s0, s0 + n)))
        vS = hi(16, 1, 0, 384)
        P.dma("sp", V(vS.ap.rearrange("p (j n) -> p j n", j=6), vS.keys),
              V(svec.rearrange("j b (h n) -> (b h) j n", h=4), ("svec",)), "sv_r")
        Sv = hi(0, 8, 0, 4096)
        Tv = hi(8, 8, 0, 4096)
        P.dma("sp", Sv, V(d("state_wkv")[l].rearrange("b h v k -> (b h) (v k)"), ()), "s_ld3")
        S3 = V(Sv.ap.rearrange("p (v k) -> p v k", v=64), Sv.keys)
        T3 = V(Tv.ap.rearrange("p (v k) -> p v k", v=64), Tv.keys)
        vj = lambda j: hi(16, 1, j * 64, (j + 1) * 64)
        kbc = lambda j: V(vj(j).ap.unsqueeze(1).to_broadcast([64, 64, 64]), vj(j).keys)
        vbc = lambda v: V(v.ap.unsqueeze(2).to_broadcast([64, 64, 64]), v.keys)
        sa = hi(17, 1, 0, 64)
        ov = hi(17, 1, 64, 128)
        P.tt("dve", T3, S3, kbc(4), ALU.mult)
        P.reduce(sa, T3, ALU.add)
        P.tt("dve", S3, S3, kbc(1), ALU.mult)
        P.tt("dve", T3, vbc(sa), kbc(5), ALU.mult)
        P.tt("dve", S3, S3, T3, ALU.add)
        P.tt("dve", T3, vbc(vj(3)), kbc(2), ALU.mult)
        P.tt("dve", S3, S3, T3, ALU.add)
        P.tt("dve", T3, S3, kbc(0), ALU.mult)
        P.reduce(ov, T3, ALU.add)
        P.dma("sp", V(o["wkv_s"].ap()[l].rearrange("b h v k -> (b h) (v k)"), ("wkv_s",)), Sv, "st_s3")
        if 'sm5' in DBG:
            return
        P.bn_stats(ms[64:128, 0, 0:6], ov)
        P.bn_aggr(ms[64:128, 0, 6:8], ms[64:128, 0, 0:6])
        P.act(sm[64:128, 50:51], ms[64:128, 0, 7:8], AF.Sqrt, bias=64e-5)
        P.recip(sm[64:128, 51:52], sm[64:128, 50:51])
        onh = hi(17, 1, 128, 192)
        P.ts("dve", onh, ov, ms[64:128, 0, 6:7], ALU.subtract, sm[64:128, 51:52], ALU.mult)
        son = self.scr["son"].ap()
        P.dma("sp", V(son.rearrange("b (h n) -> (b h) n", h=4), ("son",)), onh, "so_w")
        on = self.tk(9, 1, 0, 256)
        P.dma("sp", on, V(son, ("son",)), "so_r")
        P.tt("dve", on, on, self.BCP[0:NS, 2, :], ALU.mult)
        P.tt("dve", on, on, self.BCP[0:NS, 3, :], ALU.add)
        P.tt("dve", h4(A), h4(zs(512, 768)), b4(sm[0:NS, 32:36]), ALU.mult)
        P.tt("dve", on, on, A, ALU.add)
        P.tt("dve", M(768, 1024), on, self.PSL[1][0:NS, 0:256], ALU.mult)
        self.tok2fm(lambda j: M(j * 128, (j + 1) * 128), 8, hr[:, :, NS:2 * NS])


class Builder(SampleMixin, Builder0):
    pass


_CACHE = {}


def _get_nc():
    if "nc" not in _CACHE:
        _CACHE["nc"] = Builder().nc
    return _CACHE["nc"]


def kernel(**inputs):
    inp = {k: np.ascontiguousarray(np.asarray(v, dtype=np.float32)) for k, v in inputs.items()}
    nc = _get_nc()
    in_maps = []
    for c in range(NCORES):
        m = {}
        for n in W_SHAPES:
            m[n] = inp[n]
        m["x_prompt"] = np.ascontiguousarray(inp["x_prompt"][c])
        m["x_sample"] = np.ascontiguousarray(inp["x_sample"][c * NS:(c + 1) * NS, 0, :])
        for n in ("state_wkv", "state_shift", "state_ssm_re", "state_ssm_im", "state_conv"):
            m[n] = np.ascontiguousarray(inp[n][:, c * NS:(c + 1) * NS])
        in_maps.append(m)
    res = run_bass_kernel_spmd(nc, in_maps, core_ids=list(range(NCORES)))
    R = res.results
    cat = lambda n, ax: np.concatenate([np.asarray(r[n]) for r in R], axis=ax)
    stk = lambda n: np.stack([np.asarray(r[n]) for r in R], axis=1)
    y_p = np.stack([np.asarray(r["y_p"]) for r in R], axis=0)
    y_s = cat("y_s", 0)[:, None, :]
    outs = (y_p, y_s, stk("wkv_p"), cat("wkv_s", 1), stk("shift_p"), cat("shift_s", 1),
            stk("re_p"), cat("re_s", 1), stk("im_p"), cat("im_s", 1),
            stk("conv_p"), cat("conv_s", 1), cat("chv_s", 1)[:, :, None, :])
    return tuple(np.ascontiguousarray(o.astype(np.float32)) for o in outs)
```

```python
from contextlib import ExitStack
import math
import numpy as np
import concourse.bass as bass
import concourse.mybir as mybir
from concourse.bass_utils import run_bass_kernel_spmd

F32 = mybir.dt.float32
F32R = mybir.dt.float32r
I32 = mybir.dt.int32
ALU = mybir.AluOpType
AF = mybir.ActivationFunctionType
AX = mybir.AxisListType

EPOCH = 20000
DBG = set()
NCORES = 8
L = 4
D = 1024
T = 2048
TB = 512
NBLK = T // TB
NS = 16
INC = 2432
DFF = 2816
DTM = 896


class V:
    __slots__ = ("ap", "keys")

    def __init__(self, ap, keys):
        self.ap = ap
        self.keys = tuple(keys)


def bc(v, shape):
    return V(v.ap.to_broadcast(list(shape)), v.keys)


class TT:
    def __init__(self, h, name, keys=None):
        self.h = h
        self.name = name
        self.keys = (name,) if keys is None else tuple(keys)

    def __getitem__(self, idx):
        return V(self.h[idx], self.keys)

    def r(self):
        return TT(self.h.bitcast(F32R), self.name, self.keys)

    def i32(self):
        return TT(self.h.bitcast(I32), self.name, self.keys)


class Prog:
    ENG = ["pe", "act", "dve", "pool", "sp"]

    def __init__(self, nc):
        self.nc = nc
        self.ops = {e: [] for e in self.ENG}
        self.count = {e: 0 for e in self.ENG}
        self.lastw = {}
        self.readers = {}
        self.dma_cnt = {}
        self.sealed = {}
        self.semids = set()
        self.stack = ExitStack()
        self.tag = 'init'
        self.suffix = ''
        self.name2tag = {}

    def sbuf(self, name, shape, dtype=F32):
        h = self.stack.enter_context(self.nc.sbuf_tensor(name, list(shape), dtype))
        return TT(h, name)

    def psum(self, name, shape, dtype=F32):
        h = self.stack.enter_context(self.nc.psum_tensor(name, list(shape), dtype))
        return TT(h, name)

    def seal(self, group):
        self.sealed[("d", group)] = self.dma_cnt.get(group, 0)

    def _add(self, eng, fn, reads, writes, dma_group=None):
        deps = {}
        for k in reads:
            w = self.lastw.get(k)
            if w:
                for s, v in w.items():
                    if deps.get(s, 0) < v:
                        deps[s] = v
            if isinstance(k, str) and k.startswith("ps"):
                rd = self.readers.get(k)
                if rd:
                    for s, v in rd.items():
                        if s[1] != eng and deps.get(s, 0) < v:
                            deps[s] = v
        for k in writes:
            w = self.lastw.get(k)
            if w:
                for s, v in w.items():
                    if deps.get(s, 0) < v:
                        deps[s] = v
            rd = self.readers.get(k)
            if rd:
                for s, v in rd.items():
                    if deps.get(s, 0) < v:
                        deps[s] = v
        for s_, v_ in list(deps.items()):
            sv = self.sealed.get(s_)
            if sv is not None and v_ <= sv:
                deps[s_] = sv
        if dma_group is None:
            self.count[eng] += 1
            ep, val = divmod(self.count[eng] - 1, EPOCH)
            tok = (("e", eng, ep), val + 1)
            inc = 1
        else:
            g = self.dma_cnt.get(dma_group, 0) + 16
            self.dma_cnt[dma_group] = g
            tok = (("d", dma_group), g)
            inc = 16
        if eng == "pe":
            deps = {s: v for s, v in deps.items() if not (s[0] == "e" and s[1] == "pe")}
        self.semids.add(tok[0])
        self.ops[eng].append((fn, deps, tok, inc, self.tag + self.suffix))
        for k in reads:
            rd = self.readers.setdefault(k, {})
            if rd.get(tok[0], 0) < tok[1]:
                rd[tok[0]] = tok[1]
        for k in writes:
            w = self.lastw.setdefault(k, {})
            if w.get(tok[0], 0) < tok[1]:
                w[tok[0]] = tok[1]
        return tok

    @staticmethod
    def _keys(*vs):
        ks = []
        for v in vs:
            if isinstance(v, V):
                ks.extend(v.keys)
        return ks

    @staticmethod
    def _ap(v):
        return v.ap if isinstance(v, V) else v

    def mm(self, out, lhsT, rhs, start=True, stop=True):
        o, l, r = out.ap, lhsT.ap, rhs.ap
        return self._add("pe", lambda e: e.matmul(o, l, r, start=start, stop=stop),
                         self._keys(lhsT, rhs), self._keys(out))

    def transpose(self, out, in_, ident):
        o, i, d = out.ap, in_.ap, ident.ap
        return self._add("pe", lambda e: e.transpose(o, i, d), self._keys(in_, ident), self._keys(out))

    def act(self, out, in_, func, scale=1.0, bias=None, accum_out=None):
        o, i = out.ap, in_.ap
        kw = {}
        if bias is not None:
            kw["bias"] = self._ap(bias)
        if accum_out is not None:
            kw["accum_out"] = accum_out.ap
        sc = self._ap(scale)
        return self._add("act", lambda e: e.activation(o, i, func, scale=sc, **kw),
                         self._keys(in_, scale, bias), self._keys(out, accum_out))

    def tt(self, eng, out, in0, in1, op):
        o, a, b = out.ap, in0.ap, in1.ap
        return self._add(eng, lambda e: e.tensor_tensor(o, a, b, op), self._keys(in0, in1), self._keys(out))

    def ts(self, eng, out, in0, s1, op0, s2=None, op1=None):
        o, a = out.ap, in0.ap
        x1, x2 = self._ap(s1), self._ap(s2)
        kw = {}
        if op1 is not None:
            kw["op1"] = op1
        return self._add(eng, lambda e: e.tensor_scalar(o, a, x1, x2, op0, **kw),
                         self._keys(in0, s1, s2), self._keys(out))

    def stt(self, eng, out, in0, scalar, in1, op0, op1):
        o, a, b = out.ap, in0.ap, in1.ap
        s = self._ap(scalar)
        return self._add(eng, lambda e: e.scalar_tensor_tensor(o, a, s, b, op0, op1),
                         self._keys(in0, scalar, in1), self._keys(out))

    def scan(self, out, d0, d1, initial, op0, op1):
        o, a, b = out.ap, d0.ap, d1.ap
        ini = self._ap(initial)
        return self._add("dve", lambda e: e.tensor_tensor_scan(o, a, b, ini, op0, op1),
                         self._keys(d0, d1, initial), self._keys(out))

    def copy(self, eng, out, in_):
        o, i = out.ap, in_.ap
        if eng == "act":
            return self._add("act", lambda e: e.copy(o, i), self._keys(in_), self._keys(out))
        return self._add(eng, lambda e: e.tensor_copy(o, i), self._keys(in_), self._keys(out))

    def memset(self, eng, out, val):
        o = out.ap
        return self._add(eng, lambda e: e.memset(o, val), [], self._keys(out))

    def reduce(self, out, in_, op, axis=AX.X):
        o, i = out.ap, in_.ap
        return self._add("dve", lambda e: e.tensor_reduce(o, i, axis, op), self._keys(in_), self._keys(out))

    def recip(self, out, in_):
        o, i = out.ap, in_.ap
        return self._add("dve", lambda e: e.reciprocal(o, i), self._keys(in_), self._keys(out))

    def bn_stats(self, out, in_):
        o, i = out.ap, in_.ap
        return self._add("dve", lambda e: e.bn_stats(o, i), self._keys(in_), self._keys(out))

    def bn_aggr(self, out, in_):
        o, i = out.ap, in_.ap
        return self._add("dve", lambda e: e.bn_aggr(o, i), self._keys(in_), self._keys(out))

    def affine_select(self, out, in_, pattern, compare_op, fill, base, channel_multiplier):
        o, i = out.ap, in_.ap
        return self._add("pool", lambda e: e.affine_select(o, i, pattern, compare_op, fill, base=base,
                                                           channel_multiplier=channel_multiplier),
                         self._keys(in_), self._keys(out))

    def dma(self, q, out, in_, group, **kw):
        o, i = out.ap, in_.ap
        return self._add(q, lambda e: e.dma_start(out=o, in_=i, **kw), self._keys(in_), self._keys(out),
                         dma_group=group)

    def emit(self):
        nc = self.nc
        sems = {}
        for sid in sorted(self.semids, key=str):
            nm = "s_" + "_".join(str(x) for x in sid)
            sems[sid] = self.stack.enter_context(nc.semaphore(nm))
        finals = [(("d", g), v) for g, v in self.dma_cnt.items()]
        sealed = self.sealed
        bname = {"pe": "tensor", "act": "scalar", "dve": "vector", "pool": "gpsimd", "sp": "sync"}
        with nc.Block() as block:
            for eng in self.ENG:
                ops = self.ops[eng]

                def body(e, ops=ops, eng=eng):
                    seen = {}
                    for fn, deps, tok, inc, tag in ops:
                        for s, v in deps.items():
                            if seen.get(s, 0) < v:
                                e.wait_ge(sems[s], v)
                                seen[s] = v
                        ins = fn(e)
                        ins.then_inc(sems[tok[0]], inc)
                        try:
                            self.name2tag[ins.ins.name] = tag
                        except Exception:
                            pass
                    if eng == "sp":
                        for s, v in finals:
                            if seen.get(s, 0) < v:
                                e.wait_ge(sems[s], v)

                getattr(block, bname[eng])(body)
        self.stack.close()


NSLOT = 28
NWB = 3
TWO_PI_SAFE = 6.2831845
INV_2PI = 1.0 / (2.0 * math.pi)

W_SHAPES = {
    "norm1_g": (L, D), "w_in": (L, D, INC), "gm_ln_g": (L, 256), "gm_ln_b": (L, 256),
    "gm_ws": (L, 4, 128, 128), "gm_bs": (L, 4, 128),
    "ssm_a_re": (L, 16, 64), "ssm_a_im": (L, 16, 64), "ssm_log_dt": (L, 16, 64),
    "ssm_b_re": (L, 16, 64, 16), "ssm_b_im": (L, 16, 64, 16),
    "ssm_c_re": (L, 16, 16, 64), "ssm_c_im": (L, 16, 16, 64), "ssm_d": (L, 16, 16),
    "ssm_glu_w": (L, 256, 256), "ssm_glu_b": (L, 256), "conv_w": (L, 3, 256), "conv_b": (L, 256),
    "tm_mu": (L, DTM), "tm_w0": (L, 256), "tm_w2": (L, 32, 256), "tm_a0": (L, 256),
    "tm_a2": (L, 32, 256), "tm_g2": (L, 64, 256), "tm_k_k": (L, 256), "tm_k_a": (L, 256),
    "tm_r_k": (L, 4, 64), "tm_ln_g": (L, 256), "tm_ln_b": (L, 256),
    "w_out": (L, D, D), "norm2_g": (L, D), "ffn_w_gu": (L, D, 2 * DFF), "ffn_w_down": (L, DFF, D),
    "norm_f_g": (D,),
}
IN_SHAPES = {
    "x_prompt": (T, D), "x_sample": (NS, D), "state_wkv": (L, NS, 4, 64, 64),
    "state_shift": (L, NS, DTM), "state_ssm_re": (L, NS, 16, 64), "state_ssm_im": (L, NS, 16, 64),
    "state_conv": (L, NS, 2, 256),
}
OUT_SHAPES = {
    "y_p": (T, D), "y_s": (NS, D), "wkv_p": (L, 4, 64, 64), "wkv_s": (L, NS, 4, 64, 64),
    "shift_p": (L, DTM), "shift_s": (L, NS, DTM), "re_p": (L, 16, 64), "re_s": (L, NS, 16, 64),
    "im_p": (L, 16, 64), "im_s": (L, NS, 16, 64), "conv_p": (L, 2, 256), "conv_s": (L, NS, 2, 256),
    "chv_s": (L, NS, 256),
}


class Builder0:
    def __init__(self, do_sample=True, nblk=NBLK, nlayer=L, max_stages=None, do_prep=True):
        self.do_sample = do_sample
        self.nblk = nblk
        self.nlayer = nlayer
        nc = bass.Bass("TRN2", target_bir_lowering=False)
        self.nc = nc
        self.P = Prog(nc)
        self.din = {}
        for n, s in list(IN_SHAPES.items()) + list(W_SHAPES.items()):
            self.din[n] = nc.dram_tensor(n, list(s), F32, kind="ExternalInput")
        self.dout = {}
        for n, s in OUT_SHAPES.items():
            self.dout[n] = nc.dram_tensor(n, list(s), F32, kind="ExternalOutput")
        self.scr = {}
        for n, s in {"etab": (L, 8, 128, 1024), "lcd": (L, 128, 1536),
                     "svec": (6, NS, 256), "son": (NS, 256)}.items():
            self.scr[n] = nc.dram_tensor("scr_" + n, list(s), F32, kind="Internal")
        self.max_stages = max_stages
        self.alloc()
        if do_prep:
            self.prep()
        self.main()
        self.P.emit()

    def di(self, name):
        return self.din[name].ap()

    def dir_(self, name):
        return self.din[name].bitcast(F32R).ap()

    def ar(self, i, c0=0, c1=512, p0=0, p1=128, r=False):
        h = self.arena_r if r else self.arena.h
        return V(h[p0:p1, i, c0:c1], (("ar", i),))

    def hs(self, i, c0=0, c1=512, p0=0, p1=128, r=True):
        h = self.hTr if r else self.hTf
        return V(h[p0:p1, i, c0:c1], (("hT", i),))

    def arn(self, i, n, c0=0, c1=512, r=False):
        h = self.arena_r if r else self.arena.h
        return V(h[:, i:i + n, c0:c1], tuple(("ar", j) for j in range(i, i + n)))

    def ps(self):
        t = self.PS[self.psi % self.ps_n]
        self.psi += 1
        return t

    def mq(self, q, c0=0, c1=128, p0=0, p1=128, n=1):
        if n == 1:
            return V(self.mat.h[p0:p1, q, c0:c1], (("m", q),))
        hh = self.mat.h[p0:p1, q:q + n, :].rearrange("p a b -> p (a b)")
        return V(hh, tuple(("m", j) for j in range(q, q + n)))

    def ms(self, i, c0=0, c1=512, p0=0, p1=128):
        hh = self.mat.h[p0:p1, 4 * i:4 * i + 4, :].rearrange("p a b -> p (a b)")[:, c0:c1]
        return V(hh, tuple(("m", j) for j in range(4 * i, 4 * i + 4)))

    def pt(self, g, r0, n=1):
        return self.PT[:, g, r0:r0 + n]

    def alias_r(self, tt, shape):
        addr = None
        for a in self.nc.allocations:
            if a.name == tt.name + "_set":
                addr = a.memorylocations[0].addr
        assert addr is not None
        return self.nc.alloc_sbuf_tensor_at(tt.name + "_r", list(shape), F32R, offset=addr)

    def alloc(self):
        P = self.P
        self.arena = P.sbuf("arena", [128, NSLOT, 512])
        self.arena_r = self.alias_r(self.arena, [128, NSLOT, 512])
        self.PS = [P.psum("ps%d" % i, [128, 512]) for i in range(8)]
        self.psi = 0
        self.PSL = self.PS[4:8]
        self.ps_n = 4
        self.ident = P.sbuf("ident", [128, 128])
        self.ones = P.sbuf("ones", [128, 128])
        self.zeros = P.sbuf("zeros", [128, 512])
        self.m_le = P.sbuf("m_le", [128, 128])
        self.m_lt = P.sbuf("m_lt", [128, 128])
        self.m_gt = P.sbuf("m_gt", [128, 128])
        self.bones = P.sbuf("bones", [128, 128])
        self.sel = P.sbuf("sel", [128, 2])
        self.RT = P.sbuf("RT", [128, 3, 128])
        self.PT = P.sbuf("PT", [128, 3, 128])
        self.SPt = P.sbuf("SPt", [128, 24, 32])
        self.SPi = P.sbuf("SPi", [128, 32], I32)
        self.lora = P.sbuf("lora", [128, L, 256])
        self.xb = P.sbuf("xb", [128, 4, D])
        self.hT = P.sbuf("hT", [128, 8, TB])
        self.hT.keys = tuple(("hT", k) for k in range(8))
        hT_addr = [a.memorylocations[0].addr for a in self.nc.allocations if a.name == "hT_set"][0]
        self.hTf = self.nc.alloc_sbuf_tensor_at("hT_f", [128, 8, TB], F32, offset=hT_addr)
        self.hTr = self.hT.h.bitcast(F32R)
        self.yT = P.sbuf("yT", [128, 8, TB])
        self.WB = [P.sbuf("wb%d" % i, [128, 4096]) for i in range(NWB)]
        self.wbi = 0
        stg_addr = [a.memorylocations[0].addr for a in self.nc.allocations if a.name == "wb0_set"][0]
        self.STG = TT(self.nc.alloc_sbuf_tensor_at("STG", [128, 1536], F32, offset=stg_addr), "STG", ("wb0",))
        self.LC = P.sbuf("LC", [128, 2048])
        self.LCR = TT(self.alias_r(self.LC, [128, 2048]), "LC")
        self.BCP = P.sbuf("BCP", [128, 4, 256])
        self.small = P.sbuf("small", [128, 64])
        self.ssm_st = P.sbuf("ssm_st", [128, L, 8, 2])
        self.conv_st = P.sbuf("conv_st", [128, L, 2, 2])
        self.shift_st = P.sbuf("shift_st", [128, L, 7])
        self.wkv_st = P.sbuf("wkv_st", [128, L, 2, 128])
        self.mstat = P.sbuf("mstat", [128, 4, 8])
        self.mat = P.sbuf("mat", [128, 40, 128])

    def prep(self):
        self.P.tag = 'prep'
        P = self.P
        d = self.di
        P.memset("pool", self.ones[:, :], 1.0)
        P.memset("pool", self.zeros[:, :], 0.0)
        P.affine_select(self.ident[:, :], self.ones[:, :], [[-1, 128]], ALU.is_equal, 0.0, 0, 1)
        P.affine_select(self.m_gt[:, :], self.ones[:, :], [[-1, 128]], ALU.is_gt, 0.0, 0, 1)
        P.affine_select(self.m_le[:, :], self.ones[:, :], [[1, 128]], ALU.is_ge, 0.0, 0, -1)
        P.affine_select(self.m_lt[:, :], self.ones[:, :], [[1, 128]], ALU.is_gt, 0.0, 0, -1)
        P.memset("pool", self.bones[:, :], 0.0)
        P.memset("pool", self.bones[0:64, 0:64], 1.0)
        P.memset("pool", self.bones[64:128, 64:128], 1.0)
        P.memset("pool", self.sel[:, :], 0.0)
        P.memset("pool", self.sel[0:64, 0:1], 1.0)
        P.memset("pool", self.sel[64:128, 1:2], 1.0)
        for t in (self.ssm_st, self.conv_st, self.shift_st, self.wkv_st):
            P.memset("dve", t[:], 0.0)
        P.memset("dve", self.RT[:], 0.0)
        self.rt_keys = []

        def rows(g, r0, name, n):
            src = self.din[name].ap()
            nd = len(src.shape)
            letters = "abcd"[:nd]
            flat = src.rearrange("%s -> (%s)" % (" ".join(letters), " ".join(letters)))
            src2 = flat.rearrange("(r c) -> r c", c=128)
            fk = ("rt", g, r0)
            self.rt_keys.append(fk)
            P.dma("sp", V(self.RT.h[r0:r0 + n, g, :], (fk,)), V(src2, self.RT.keys), "rt")

        rows(0, 0, "norm1_g", 32); rows(0, 32, "norm2_g", 32); rows(0, 64, "norm_f_g", 8)
        rows(0, 72, "conv_b", 8); rows(0, 80, "conv_w", 24); rows(0, 104, "tm_w0", 8)
        rows(0, 112, "tm_a0", 8); rows(0, 120, "tm_k_k", 8)
        rows(1, 0, "tm_k_a", 8); rows(1, 8, "tm_r_k", 8); rows(1, 16, "ssm_d", 8)
        rows(1, 24, "ssm_glu_b", 8); rows(1, 32, "tm_mu", 28)
        rows(1, 64, "ssm_a_re", 32); rows(1, 96, "ssm_a_im", 32)
        rows(2, 0, "ssm_log_dt", 32); rows(2, 32, "gm_bs", 16)
        P.seal("rt")
        for g in range(3):
            pp = self.ps()
            P.transpose(pp[:, 0:128], V(self.RT.h[:, g, :], self.RT.keys + tuple(self.rt_keys)), self.ident[:, :])
            P.copy("dve", self.PT[:, g, :], pp[:, 0:128])
        P.ts("dve", self.PT[:, 2, 48:56], self.PT[:, 1, 0:8], -1.0, ALU.mult, 1.0, ALU.add)
        for l in range(L):
            lr = self.lora.r()
            P.dma("pool", lr[0:32, l, :], V(self.dir_("tm_w2")[l], ()), "lora")
            P.dma("pool", lr[32:64, l, :], V(self.dir_("tm_a2")[l], ()), "lora")
            P.dma("pool", lr[64:128, l, :], V(self.dir_("tm_g2")[l], ()), "lora")
        P.seal("lora")
        sp = lambda i: self.SPt[:, i, :]
        a_re, a_im, ldt = self.PT[:, 1, 64:96], self.PT[:, 1, 96:128], self.PT[:, 2, 0:32]
        LAM, DT, MAG, TH, FS, FF, FR, SIN, COS, LBR, LBI, DEN, FRE, FIM, T1, T2 = range(16)
        self.I_LBR, self.I_LBI, self.I_MAG, self.I_FRE, self.I_FIM = LBR, LBI, MAG, FRE, FIM
        P.ts("dve", sp(LAM), a_re, -1e-4, ALU.min)
        P.act(sp(DT), ldt, AF.Exp)
        P.tt("dve", sp(T1), sp(LAM), sp(DT), ALU.mult)
        P.act(sp(MAG), sp(T1), AF.Exp)
        P.tt("dve", sp(TH), a_im, sp(DT), ALU.mult)
        for dst, off in ((SIN, 0.0), (COS, 0.25)):
            P.ts("dve", sp(FS), sp(TH), INV_2PI, ALU.mult, off, ALU.add)
            P.copy("dve", self.SPi[:, :], sp(FS))
            P.copy("dve", sp(FF), self.SPi[:, :])
            P.tt("dve", sp(FR), sp(FS), sp(FF), ALU.subtract)
            P.act(sp(dst), sp(FR), AF.Sin, scale=TWO_PI_SAFE)
        P.tt("dve", sp(LBR), sp(MAG), sp(COS), ALU.mult)
        P.tt("dve", sp(LBI), sp(MAG), sp(SIN), ALU.mult)
        P.tt("dve", sp(T1), sp(LAM), sp(LAM), ALU.mult)
        P.tt("dve", sp(T2), a_im, a_im, ALU.mult)
        P.tt("dve", sp(DEN), sp(T1), sp(T2), ALU.add)
        P.recip(sp(DEN), sp(DEN))
        LM1 = 16
        P.ts("dve", sp(LM1), sp(LBR), -1.0, ALU.add)
        P.tt("dve", sp(T1), sp(LM1), sp(LAM), ALU.mult)
        P.tt("dve", sp(T2), sp(LBI), a_im, ALU.mult)
        P.tt("dve", sp(T1), sp(T1), sp(T2), ALU.add)
        P.tt("dve", sp(FRE), sp(T1), sp(DEN), ALU.mult)
        P.tt("dve", sp(T1), sp(LBI), sp(LAM), ALU.mult)
        P.tt("dve", sp(T2), sp(LM1), a_im, ALU.mult)
        P.tt("dve", sp(T1), sp(T1), sp(T2), ALU.subtract)
        P.tt("dve", sp(FIM), sp(T1), sp(DEN), ALU.mult)
        etab = TT(self.scr["etab"], "etab")
        lcd = TT(self.scr["lcd"], "lcd")
        for l in range(self.nlayer):
            EC, ES, TA, TBs = 0, 8, 16, 20
            cs = self.SPt[:, COS, l * 8:(l + 1) * 8]
            sn = self.SPt[:, SIN, l * 8:(l + 1) * 8]
            P.copy("dve", self.arn(EC, 8, 0, 1), V(cs.ap.unsqueeze(2), cs.keys))
            P.copy("dve", self.arn(ES, 8, 0, 1), V(sn.ap.unsqueeze(2), sn.keys))
            n = 1
            while n < 512:
                cn = bc(self.arn(EC, 8, n - 1, n), [128, 8, n])
                snb = bc(self.arn(ES, 8, n - 1, n), [128, 8, n])
                ns_ = (n + 511) // 512
                def tmpv(base, n=n):
                    hh = self.arena.h[:, base:base + 4, :].rearrange("p a b -> p (a b)")[:, 0:8 * n]
                    return V(hh.rearrange("p (a b) -> p a b", a=8), tuple(("ar", j) for j in range(base, base + 4)))
                t1, t2 = tmpv(TA), tmpv(TBs)
                P.tt("dve", t1, self.arn(EC, 8, 0, n), cn, ALU.mult)
                P.tt("dve", t2, self.arn(ES, 8, 0, n), snb, ALU.mult)
                P.tt("dve", self.arn(EC, 8, n, 2 * n), t1, t2, ALU.subtract)
                P.tt("dve", t1, self.arn(ES, 8, 0, n), cn, ALU.mult)
                P.tt("dve", t2, self.arn(EC, 8, 0, n), snb, ALU.mult)
                P.tt("dve", self.arn(ES, 8, n, 2 * n), t1, t2, ALU.add)
                n *= 2
            P.dma("sp", V(etab.h[l].rearrange("j p c -> p j c")[:, :, 0:512], ("etab",)), self.arn(EC, 8), "etab_w")
            P.dma("sp", V(etab.h[l].rearrange("j p c -> p j c")[:, :, 512:1024], ("etab",)), self.arn(ES, 8), "etab_w")
            WS = 0
            wsv = self.ms(WS)
            P.dma("sp", V(wsv.ap.rearrange("p (h s) -> p h s", h=4), wsv.keys),
                  V(d("gm_ws")[l].rearrange("h t s -> t h s"), ()), "prep_ws")
            for h in range(4):
                pp = self.ps()
                P.transpose(pp[:, 0:128], self.ms(WS, h * 128, (h + 1) * 128), self.ident[:, :])
                P.tt("dve", self.STG[:, h * 128:(h + 1) * 128], pp[:, 0:128], self.m_le[:, :], ALU.mult)
            CN = 1
            P.memset("pool", self.ms(CN), 0.0)
            cn_keys = []
            for ri, nm in enumerate(("ssm_c_re", "ssm_c_im")):
                for g in range(16):
                    half, gl = divmod(g, 8)
                    c0 = (half * 2 + ri) * 128 + (g % 2) * 64
                    fk = ("cn", l, ri, g)
                    cn_keys.append(fk)
                    P.dma("sp", V(self.ms(CN, c0, c0 + 64, gl * 16, gl * 16 + 16).ap, (fk,)),
                          V(d(nm)[l, g], self.ms(CN).keys), "prep_c")
            for half in range(2):
                for ri in range(2):
                    c0 = (half * 2 + ri) * 128
                    pp = self.ps()
                    src = self.ms(CN, c0, c0 + 128)
                    P.transpose(pp[:, 0:128], V(src.ap, src.keys + tuple(cn_keys)), self.ident[:, :])
                    if ri == 0:
                        P.copy("act", self.STG[:, 512 + c0:512 + c0 + 128], pp[:, 0:128])
                    else:
                        P.act(self.STG[:, 512 + c0:512 + c0 + 128], pp[:, 0:128], AF.Copy, scale=-1.0)
            BN = 2
            P.memset("pool", self.ms(BN), 0.0)
            bn_keys = []
            for ri, nm in enumerate(("ssm_b_re", "ssm_b_im")):
                for g in range(16):
                    half, gl = divmod(g, 8)
                    c0 = ri * 256 + half * 128 + gl * 16
                    p0 = (g % 2) * 64
                    fk = ("bn", l, ri, g)
                    bn_keys.append(fk)
                    P.dma("sp", V(self.ms(BN, c0, c0 + 16, p0, p0 + 64).ap, (fk,)),
                          V(d(nm)[l, g], self.ms(BN).keys), "prep_b")
            BO = 3
            def v8(slot, c0):
                vv = self.ms(slot, c0, c0 + 256)
                return V(vv.ap.rearrange("p (a b) -> p a b", a=8), vv.keys + (tuple(bn_keys) if slot == BN else ()))
            fre = self.SPt[:, FRE, l * 8:(l + 1) * 8]
            fim = self.SPt[:, FIM, l * 8:(l + 1) * 8]
            freb = V(fre.ap.unsqueeze(2).to_broadcast([128, 8, 32]), fre.keys)
            fimb = V(fim.ap.unsqueeze(2).to_broadcast([128, 8, 32]), fim.keys)
            t1 = v8(WS, 0); t2 = v8(WS, 256)
            P.tt("dve", t1, v8(BN, 0), freb, ALU.mult)
            P.tt("dve", t2, v8(BN, 256), fimb, ALU.mult)
            P.tt("dve", v8(BO, 0), t1, t2, ALU.subtract)
            P.tt("dve", t1, v8(BN, 0), fimb, ALU.mult)
            P.tt("dve", t2, v8(BN, 256), freb, ALU.mult)
            P.tt("dve", v8(BO, 256), t1, t2, ALU.add)
            for half in range(2):
                for ri in range(2):
                    pp = self.ps()
                    c0 = ri * 256 + half * 128
                    P.transpose(pp[:, 0:128], self.ms(BO, c0, c0 + 128), self.ident[:, :])
                    o0 = 1024 + (half * 2 + ri) * 128
                    P.copy("act", self.STG[:, o0:o0 + 128], pp[:, 0:128])
            P.dma("sp", V(lcd.h[l], ("lcd",)), self.STG[:, :], "lcd_w")

    def wload(self, src, a, b):
        buf = self.WB[self.wbi % NWB]
        self.wbi += 1
        view = buf.h.bitcast(F32R)[:, 0:a * b].rearrange("p (a b) -> p a b", a=a)
        self.P.dma("pool", V(view, buf.keys), V(src, ()), "ld_" + buf.name)
        return TT(view, buf.name, buf.keys)

    def main(self):
        P = self.P
        self.stages = []
        for tb in range(self.nblk):
            self.stages.append((None, lambda w, tb=tb: self.load_x(tb)))
            for l in range(self.nlayer):
                self.layer(tb, l)
            self.stages.append((None, lambda w, tb=tb: self.final_norm(tb)))
        if self.do_sample:
            self.sample()
        st = self.stages if self.max_stages is None else self.stages[:self.max_stages]
        views = [None] * len(st)
        nxt = 0

        def advance():
            nonlocal nxt
            while nxt < len(st):
                i = nxt
                nxt += 1
                if st[i][0] is not None:
                    views[i] = st[i][0]()
                    return

        for _ in range(NWB - 1):
            advance()
        for i in range(len(st)):
            if st[i][0] is not None:
                advance()
            st[i][1](views[i])

    def load_x(self, tb):
        self.P.tag = 'load_x'
        src = self.di("x_prompt")[tb * TB:(tb + 1) * TB, :].rearrange("(s p) c -> p s c", p=128)
        self.P.dma("sp", self.xb[:, :, :], V(src, ()), "xb_ld")

    def norm_to_hT(self, gbase):
        self.P.tag = 'norm'
        P = self.P
        hview = self.arena.h[:, 0:8, :].rearrange("p (s a) b -> p s (a b)", s=4)
        hkeys = tuple(("ar", j) for j in range(8))
        for sub in range(4):
            hv = V(hview[:, sub, :], hkeys[2 * sub:2 * sub + 2])
            P.act(hv, self.xb[:, sub, :], AF.Square, accum_out=self.small[:, sub:sub + 1])
            P.act(self.small[:, 4 + sub:5 + sub], self.small[:, sub:sub + 1], AF.Sqrt, scale=1.0 / D, bias=1e-6)
            P.recip(self.small[:, 8 + sub:9 + sub], self.small[:, 4 + sub:5 + sub])
            P.ts("dve", hv, self.xb[:, sub, :], self.small[:, 8 + sub:9 + sub], ALU.mult)
        for kc in range(8):
            pp = self.ps()
            for sub in range(4):
                hv = V(hview[:, sub, kc * 128:(kc + 1) * 128], hkeys[2 * sub:2 * sub + 2])
                P.transpose(pp[:, sub * 128:(sub + 1) * 128], hv, self.ident[:, :])
            P.act(self.hs(kc), pp[:, :], AF.Identity, scale=self.PT[:, 0, gbase + kc:gbase + kc + 1])

    def final_norm(self, tb):
        self.P.tag = 'final_norm'
        self.ps_n = 4
        P = self.P
        for sub in range(4):
            hv = self.arn(10 + 2 * sub, 2)
            hv = V(hv.ap.rearrange("p a b -> p (a b)"), hv.keys)
            P.act(hv, self.xb[:, sub, :], AF.Square, accum_out=self.small[:, sub:sub + 1])
            P.act(self.small[:, 4 + sub:5 + sub], self.small[:, sub:sub + 1], AF.Sqrt, scale=1.0 / D, bias=1e-6)
            P.recip(self.small[:, 8 + sub:9 + sub], self.small[:, 4 + sub:5 + sub])
            P.ts("dve", hv, self.xb[:, sub, :], self.small[:, 8 + sub:9 + sub], ALU.mult)
            if sub == 0:
                gfb = self.arn(18, 2)
                gfb = V(gfb.ap.rearrange("p a b -> p (a b)"), gfb.keys)
                P.dma("sp", gfb, V(self.di("norm_f_g").partition_broadcast(128), ()), "gfb")
            P.tt("dve", hv, hv, gfb, ALU.mult)
            r0 = tb * TB + sub * 128
            P.dma("sp", V(self.dout["y_p"].ap()[r0:r0 + 128, :], ("y_p",)), hv, "st_y%d" % sub)

    def load_consts(self, l):
        P = self.P
        tag = P.tag
        P.tag = 'pre'
        lcd = self.scr["lcd"].ap()
        P.dma("sp", self.LC[:, 0:1024], V(lcd[l][:, 0:1024], ("lcd",)), "lc_a")
        P.dma("pool", self.LCR[:, 1024:1536], V(self.scr["lcd"].bitcast(F32R).ap()[l][:, 1024:1536], ("lcd",)), "lc_b")
        P.dma("pool", V(self.LCR.h[:, 1536:2048].rearrange("p (k c) -> p k c", k=2), self.LC.keys),
              V(self.dir_("ssm_glu_w")[l].rearrange("(k p) c -> p k c", p=128), ()), "lc_b")
        for i, nm in enumerate(("gm_ln_g", "gm_ln_b", "tm_ln_g", "tm_ln_b")):
            P.dma("sp", self.BCP[:, i, :], V(self.di(nm)[l].partition_broadcast(128), ()), "bcp")
        P.tag = tag

    def layer(self, tb, l):
        P = self.P
        last = tb == self.nblk - 1

        class _S:
            def append(_, item, stages=self.stages):
                ld, comp = item

                def comp2(w, comp=comp):
                    P.suffix = '@%d.%d' % (tb, l)
                    comp(w)
                stages.append((ld, comp2))
        S = _S()

        def pre(w):
            P.tag = 'pre'
            self.ps_n = 8
            if tb == 0 and l == 0:
                self.load_consts(0)
            self.norm_to_hT(l * 8)

        S.append((None, pre))
        tiles = [[(0, 512)], [(1536, 512)], [(2048, 384)], [(768, 512)], [(1280, 256), (512, 256)]]
        for i, segs in enumerate(tiles):
            ncol = sum(n for _, n in segs)

            def ld(segs=segs, ncol=ncol):
                buf = self.WB[self.wbi % NWB]
                self.wbi += 1
                view = buf.h.bitcast(F32R)[:, 0:8 * ncol].rearrange("p (k c) -> p k c", k=8)
                off = 0
                for c0, n in segs:
                    src = self.dir_("w_in")[l][:, c0:c0 + n].rearrange("(k p) c -> p k c", p=128)
                    P.dma("pool", V(view[:, :, off:off + n], buf.keys), V(src, ()), "ld_" + buf.name)
                    off += n
                return TT(view, buf.name, buf.keys)

            def comp(w, i=i, segs=segs, ncol=ncol):
                P.tag = 'w_in'
                if i == 0:
                    for sub in range(6):
                        if sub < 4:
                            P.tag = 'w_in'
                            pp = self.ps()
                            for kc in range(8):
                                P.mm(pp[:, :], self.hs(kc, sub * 128, (sub + 1) * 128), w[:, kc, :], kc == 0, kc == 7)
                            self.gmlp_1(l, sub, pp)
                        if 1 <= sub < 5:
                            self.gmlp_2(l, sub - 1)
                        if 2 <= sub:
                            self.gmlp_3(l, sub - 2)
                else:
                    qs = []
                    for c0, n in segs:
                        qs += [(c0 + g * 128 - 512) // 128 for g in range(n // 128)]
                    pend = []
                    for cg, q in enumerate(qs):
                        P.tag = 'w_in'
                        pp = self.ps()
                        for kc in range(8):
                            P.mm(pp[:, :], w[:, kc, cg * 128:(cg + 1) * 128], self.hs(kc), kc == 0, kc == 7)
                        if q < 2:
                            pend.append((q, pp))
                        else:
                            self.consume_fm(tb, l, q, pp)
                    for q, pp in pend:
                        self.consume_fm(tb, l, q, pp)
                    if i == 2:
                        self.rwkv(tb, l, 0)
                    if i == 4:
                        self.rwkv(tb, l, 1)
                        nl = l + 1 if l + 1 < self.nlayer else 0
                        if not (tb == self.nblk - 1 and l == self.nlayer - 1):
                            self.load_consts(nl)

            S.append((ld, comp))
        for half in range(2):
            def ld(half=half):
                return self.wload(self.dir_("w_out")[l][:, half * 512:(half + 1) * 512].rearrange("(k p) c -> p k c", p=128), 8, 512)

            def comp(w, half=half):
                P.tag = 'w_out'
                yr = self.yT.r()
                for sub in range(4):
                    for kc in range(8):
                        P.mm(self.PSL[sub][:, :], yr[:, kc, sub * 128:(sub + 1) * 128], w[:, kc, :], kc == 0, kc == 7)
                    xs = self.xb[:, sub, half * 512:(half + 1) * 512]
                    P.tt("dve", xs, xs, self.PSL[sub][:, :], ALU.add)
                if half == 1:
                    self.norm_to_hT(32 + l * 8)

            S.append((ld, comp))
        for g in range(5):
            for which in range(2):
                def ld(g=g, which=which):
                    c0 = which * DFF + 512 * g
                    return self.wload(self.dir_("ffn_w_gu")[l][:, c0:c0 + 512].rearrange("(k p) c -> p k c", p=128), 8, 512)

                def comp(w, g=g, which=which):
                    P.tag = 'ffn_gu'
                    self.ps_n = 8
                    for fo in range(4):
                        j = 4 * g + fo
                        pz = self.ps()
                        for kc in range(8):
                            P.mm(pz[:, :], w[:, kc, fo * 128:(fo + 1) * 128], self.hs(kc), kc == 0, kc == 7)
                        if which == 0:
                            P.act(self.ar(j), pz[:, :], AF.Silu)
                        else:
                            P.tt("dve", self.ar(j, r=True), self.ar(j), pz[:, :], ALU.mult)

                S.append((ld, comp))

        def ld_last():
            buf = self.WB[self.wbi % NWB]
            self.wbi += 1
            view = buf.h.bitcast(F32R)[:, 0:4096].rearrange("p (k g c) -> p k g c", k=8, g=2)
            for g in range(2):
                c0 = g * DFF + 2560
                src = self.dir_("ffn_w_gu")[l][:, c0:c0 + 256].rearrange("(k p) c -> p k c", p=128)
                P.dma("pool", V(view[:, :, g, :], buf.keys), V(src, ()), "ld_" + buf.name)
            return TT(view, buf.name, buf.keys)

        def comp_last(w):
            P.tag = 'ffn_gu'
            self.ps_n = 8
            for fo in range(2):
                j = 20 + fo
                pg = self.ps()
                pu = self.ps()
                for kc in range(8):
                    P.mm(pg[:, :], w[:, kc, 0, fo * 128:(fo + 1) * 128], self.hs(kc), kc == 0, kc == 7)
                for kc in range(8):
                    P.mm(pu[:, :], w[:, kc, 1, fo * 128:(fo + 1) * 128], self.hs(kc), kc == 0, kc == 7)
                tmp = self.ar(22 + (j % 2))
                P.act(tmp, pg[:, :], AF.Silu)
                P.tt("dve", self.ar(j, r=True), tmp, pu[:, :], ALU.mult)

        S.append((ld_last, comp_last))
        for half in range(2):
            for jg in range(3):
                nj = 8 if jg < 2 else 6

                def ld(half=half, jg=jg, nj=nj):
                    src = self.dir_("ffn_w_down")[l][jg * 1024:jg * 1024 + nj * 128, half * 512:(half + 1) * 512]
                    return self.wload(src.rearrange("(j p) c -> p j c", p=128), nj, 512)

                def comp(w, half=half, jg=jg, nj=nj):
                    P.tag = 'ffn_down'
                    self.ps_n = 4
                    for jj in range(nj):
                        j = jg * 8 + jj
                        for sub in range(4):
                            P.mm(self.PSL[sub][:, :], self.ar(j, sub * 128, (sub + 1) * 128, r=True), w[:, jj, :],
                                 j == 0, j == 21)
                    if jg == 2:
                        for sub in range(4):
                            xs = self.xb[:, sub, half * 512:(half + 1) * 512]
                            P.tt("dve", xs, xs, self.PSL[sub][:, :], ALU.add)

                S.append((ld, comp))
        if last:
            S.append((None, lambda w: self.prompt_state_out(l)))

    def gmlp_1(self, l, sub, pp):
        self.P.tag = 'gmlp'
        P = self.P
        sl = 20 + sub
        u = self.ar(sl, 0, 256)
        vf = self.ar(sl, 256, 512)
        P.act(u, pp[:, 0:256], AF.Gelu_apprx_tanh)
        P.act(vf, pp[:, 256:512], AF.Gelu_apprx_tanh)
        ms = self.mstat
        P.bn_stats(ms[:, sub, 0:6], vf)
        P.bn_aggr(ms[:, sub, 6:8], ms[:, sub, 0:6])
        P.act(self.small[:, 16 + sub:17 + sub], ms[:, sub, 7:8], AF.Sqrt, bias=1e-5)
        P.recip(self.small[:, 20 + sub:21 + sub], self.small[:, 16 + sub:17 + sub])
        P.ts("dve", vf, vf, ms[:, sub, 6:7], ALU.subtract, self.small[:, 20 + sub:21 + sub], ALU.mult)
        P.tt("dve", vf, vf, self.BCP[:, 0, :], ALU.mult)
        P.tt("dve", vf, vf, self.BCP[:, 1, :], ALU.add)

    def gmlp_2(self, l, sub):
        self.P.tag = 'gmlp'
        P = self.P
        sl = 20 + sub
        p2 = self.ps()
        for h in range(4):
            P.mm(p2[:, h * 64:(h + 1) * 64], self.LC[:, h * 128:(h + 1) * 128], self.ar(sl, 256 + h * 64, 256 + (h + 1) * 64))
        for h in range(4):
            uh = self.ar(sl, h * 64, (h + 1) * 64)
            P.stt("dve", uh, p2[:, h * 64:(h + 1) * 64], self.PT[:, 2, 32 + l * 4 + h:33 + l * 4 + h], uh, ALU.add, ALU.mult)

    def gmlp_3(self, l, sub):
        self.P.tag = 'gmlp'
        P = self.P
        sl = 20 + sub
        p3 = self.ps()
        for t2 in range(2):
            P.transpose(p3[:, t2 * 128:(t2 + 1) * 128], self.ar(sl, t2 * 128, (t2 + 1) * 128), self.ident[:, :])
        yr = self.yT.r()
        for t2 in range(2):
            P.copy("act", yr[:, t2, sub * 128:(sub + 1) * 128], p3[:, t2 * 128:(t2 + 1) * 128])

    def consume_fm(self, tb, l, q, pp):
        self.P.tag = 'evac_fm'
        P = self.P
        if q < 2:
            P.act(self.hs(q), pp[:, :], AF.Copy)
            P.copy("dve", self.hs(4 + q, r=False), pp[:, :])
            P.act(self.hs(2 + q, p0=64, p1=128), pp[64:128, :], AF.Copy)
            P.copy("dve", self.hs(2 + q, p0=64, p1=96), self.zeros[64:96, :])
        elif q < 4:
            P.copy("act", self.ar(8 + q - 2), pp[:, :])
        elif q < 6:
            P.copy("act", self.ar(10 + q - 4), pp[:, :])
        elif q < 8:
            t = q - 6
            P.tt("dve", self.ar(12 + t), pp[:, :], self.ar(8 + t), ALU.mult)
            self.conv(tb, l, t)
        else:
            t = q - 8
            zd = 8 + t
            P.copy("act", self.ar(zd), pp[:, :])
            P.tt("dve", self.ar(15, 1, 512), self.ar(zd, 0, 511), self.ar(zd, 1, 512), ALU.subtract)
            P.tt("dve", self.ar(15, 0, 1), self.shift_st[:, l, t:t + 1], self.ar(zd, 0, 1), ALU.subtract)
            mu = self.PT[:, 1, 32 + l * 7 + t:33 + l * 7 + t]
            P.stt("dve", self.ar(t), self.ar(15), mu, self.ar(zd), ALU.mult, ALU.add)
            P.copy("act", self.shift_st[:, l, t:t + 1], self.ar(zd, 511, 512))

    def conv(self, tb, l, t):
        self.P.tag = 'conv'
        P = self.P
        z = 12 + t
        acc = 8 + t
        w = lambda j: self.PT[:, 0, 80 + l * 6 + j * 2 + t:81 + l * 6 + j * 2 + t]
        cb = self.PT[:, 0, 72 + l * 2 + t:73 + l * 2 + t]
        cst = lambda a, b: self.conv_st[:, l, t, a:b]
        P.ts("dve", self.ar(acc), self.ar(z), w(2), ALU.mult, cb, ALU.add)
        P.stt("dve", self.ar(acc, 1, 512), self.ar(z, 0, 511), w(1), self.ar(acc, 1, 512), ALU.mult, ALU.add)
        P.stt("dve", self.ar(acc, 0, 1), cst(1, 2), w(1), self.ar(acc, 0, 1), ALU.mult, ALU.add)
        P.stt("dve", self.ar(acc, 2, 512), self.ar(z, 0, 510), w(0), self.ar(acc, 2, 512), ALU.mult, ALU.add)
        P.stt("dve", self.ar(acc, 0, 2), cst(0, 2), w(0), self.ar(acc, 0, 2), ALU.mult, ALU.add)
        P.tt("dve", self.yT.r()[:, 4 + t, :], self.ar(acc), self.ar(10 + t), ALU.mult)
        P.copy("act", cst(0, 2), self.ar(z, 510, 512))

    def s5_ctpad(self, l, half):
        P = self.P
        P.tag = 's5'
        for ri in range(2):
            sl = 12 + ri
            P.copy("dve", self.ar(sl, r=True), self.zeros[:, :])
            for jl in range(4):
                c = jl * 128 + jl * 32
                b0 = 512 + (half * 2 + ri) * 128 + jl * 32
                P.copy("act", self.ar(sl, c, c + 32, r=True), self.LC[:, b0:b0 + 32])

    def s5_a(self, l, j):
        P = self.P
        P.tag = 's5'
        LCr = self.LCR
        etab = self.scr["etab"].ap()
        half, jl = divmod(j, 4)
        pr = self.ps()
        pi = self.ps()
        rows = slice(32 * jl, 32 * jl + 32)
        for ri, pz in ((0, pr), (1, pi)):
            c0 = 1024 + (half * 2 + ri) * 128
            if jl < 3:
                P.mm(pz[:, :], V(LCr.h[rows, c0:c0 + 128], LCr.keys), self.hs(half, 0, 512, 32 * jl, 32 * jl + 32))
            else:
                P.mm(pz[:, :], LCr[64:128, c0:c0 + 128], self.hs(2 + half, 0, 512, 64, 128))
        P.copy("act", self.ar(2), pr[:, :])
        P.copy("act", self.ar(3), pi[:, :])
        yield
        etv = self.arn(0, 2)
        P.dma("sp", V(etv.ap.rearrange("p a b -> p (a b)"), etv.keys), V(etab[l, j], ("etab",)), "et0")
        yield
        Ec, Es = self.ar(0), self.ar(1)
        A, B, C, Dd = self.ar(2), self.ar(3), self.ar(8), self.ar(9)
        P.tt("dve", C, B, Ec, ALU.mult)
        yield
        P.tt("dve", Dd, A, Es, ALU.mult)
        yield
        P.tt("dve", A, A, Ec, ALU.mult)
        yield
        P.tt("dve", B, B, Es, ALU.mult)
        yield
        P.tt("dve", A, A, B, ALU.add)
        yield
        P.tt("dve", C, C, Dd, ALU.subtract)
        yield

    def s5_b(self, l, j):
        P = self.P
        P.tag = 's5'
        half, jl = divmod(j, 4)
        Ec, Es = self.ar(0), self.ar(1)
        A, B, C, Dd = self.ar(2), self.ar(3), self.ar(8), self.ar(9)
        rho = bc(self.SPt[:, self.I_MAG, l * 8 + j:l * 8 + j + 1], [128, 512])
        P.scan(B, rho, A, self.ssm_st[:, l, j, 0:1], ALU.mult, ALU.add)
        yield
        P.scan(Dd, rho, C, self.ssm_st[:, l, j, 1:2], ALU.mult, ALU.add)
        yield
        P.tt("dve", A, B, Ec, ALU.mult)
        yield
        P.tt("dve", C, Dd, Es, ALU.mult)
        yield
        P.tt("dve", self.ar(10, r=True), A, C, ALU.subtract)
        yield
        P.tt("dve", self.ssm_st[:, l, j, 0:1], self.ar(2, 511, 512), self.ar(8, 511, 512), ALU.subtract)
        yield
        P.tt("dve", A, Dd, Ec, ALU.mult)
        yield
        P.tt("dve", C, B, Es, ALU.mult)
        yield
        P.tt("dve", self.ar(11, r=True), A, C, ALU.add)
        yield
        P.tt("dve", self.ssm_st[:, l, j, 1:2], self.ar(2, 511, 512), self.ar(8, 511, 512), ALU.add)
        yield

    def s5_c(self, l, j):
        P = self.P
        P.tag = 's5'
        half, jl = divmod(j, 4)
        if jl == 0:
            self.s5_ctpad(l, half)
        pc = self.ps()
        for ri in range(2):
            P.mm(pc[:, :], self.ar(12 + ri, jl * 128, (jl + 1) * 128, r=True), self.ar(10 + ri, r=True), ri == 0, ri == 1)
        if jl == 0:
            P.copy("act", self.ar(14 + half), pc[:, :])
            yield
        else:
            P.tt("dve", self.ar(14 + half), self.ar(14 + half), pc[:, :], ALU.add)
            yield

    def s5_tail(self, l):
        P = self.P
        P.tag = 's5'
        LCr = self.LCR
        for half in range(2):
            dcol = self.PT[:, 1, 16 + l * 2 + half:17 + l * 2 + half]
            P.stt("dve", self.ar(8 + half), self.hs(4 + half, r=False), dcol, self.ar(14 + half), ALU.mult, ALU.add)
            P.act(self.ar(8 + half), self.ar(8 + half), AF.Gelu_apprx_tanh)
            P.copy("act", self.ar(2 + half, r=True), self.ar(8 + half))
        for t in range(2):
            pg = self.ps()
            for kc in range(2):
                c0 = 1536 + kc * 256 + t * 128
                P.mm(pg[:, :], LCr[:, c0:c0 + 128], self.ar(2 + kc, r=True), kc == 0, kc == 1)
            gb = self.PT[:, 1, 24 + l * 2 + t:25 + l * 2 + t]
            P.act(self.ar(10 + t), pg[:, :], AF.Sigmoid, bias=gb)
            P.tt("dve", self.yT.r()[:, 2 + t, :], self.ar(8 + t), self.ar(10 + t), ALU.mult)

    def rwkv(self, tb, l, phase):
        self.P.tag = 'rwkv_prep'
        self.ps_n = 4
        P = self.P
        lr = self.lora.r()
        LX = 7
        if phase == 0:
            P.act(self.ar(LX, p0=0, p1=32, r=True), self.ar(6, p0=0, p1=32), AF.Tanh)
            P.act(self.ar(LX, p0=32, p1=64, r=True), self.ar(6, p0=32, p1=64), AF.Copy)
            P.act(self.ar(LX, p0=64, p1=128, r=True), self.ar(6, p0=64, p1=128), AF.Sigmoid)
        LD, AA, KK, KF, BV, CL, GI, GP, TMP = 8, 9, 10, 11, 12, 13, 14, 15, 15

        def prep_pair(hp):
            P.tag = 'rwkv_prep'
            AT, RT_, KH, BH, G, RK = (16 + 6 * hp + k for k in range(6))
            r_, k_, v_ = self.ar(hp), self.ar(2 + hp), self.ar(4 + hp)
            col = lambda g, base: self.PT[:, g, base + l * 2 + hp:base + l * 2 + hp + 1]
            pq = self.ps()
            P.mm(pq[:, :], lr[0:32, l, hp * 128:(hp + 1) * 128], self.ar(LX, p0=0, p1=32, r=True))
            P.act(self.ar(LD), pq[:, :], AF.Sigmoid, bias=col(0, 104))
            P.act(self.ar(LD), self.ar(LD), AF.Copy, scale=-0.6065306597126334)
            pa = self.ps()
            P.mm(pa[:, :], lr[32:64, l, hp * 128:(hp + 1) * 128], self.ar(LX, p0=32, p1=64, r=True))
            P.act(self.ar(AA), pa[:, :], AF.Sigmoid, bias=col(0, 112))
            P.act(self.ar(KK), k_, AF.Identity, scale=col(0, 120))
            P.act(self.ar(TMP), self.ar(KK), AF.Square)
            pn = self.ps()
            P.mm(pn[:, :], self.bones[:, :], self.ar(TMP))
            P.act(self.ar(TMP), pn[:, :], AF.Sqrt)
            P.ts("dve", self.ar(TMP), self.ar(TMP), 1e-12, ALU.max)
            P.recip(self.ar(TMP), self.ar(TMP))
            P.tt("dve", self.ar(KK), self.ar(KK), self.ar(TMP), ALU.mult)
            P.act(self.ar(TMP), self.ar(AA), AF.Identity, scale=col(1, 0), bias=col(2, 48))
            P.tt("dve", self.ar(KF), k_, self.ar(TMP), ALU.mult)
            P.tt("dve", self.ar(BV), self.ar(KK), self.ar(AA), ALU.mult)
            P.tt("dve", self.ar(TMP), r_, self.ar(KF), ALU.mult)
            P.act(self.ar(RK), self.ar(TMP), AF.Identity, scale=col(1, 8))
            for c in range(4):
                cs = (c * 128, (c + 1) * 128)
                P.scan(self.ar(CL, *cs), self.ones[:, :], self.ar(LD, *cs), 0.0, ALU.mult, ALU.add)
            P.act(self.ar(G), self.ar(CL), AF.Exp)
            P.act(self.ar(GI), self.ar(CL), AF.Exp, scale=-1.0)
            P.tt("dve", self.ar(GP), self.ar(CL), self.ar(LD), ALU.subtract)
            P.act(self.ar(GP), self.ar(GP), AF.Exp)
            P.stt("dve", self.ar(AT), self.ar(KK), -1.0, self.ar(GP), ALU.mult, ALU.mult)
            P.tt("dve", self.ar(RT_), r_, self.ar(G), ALU.mult)
            P.tt("dve", self.ar(KH), self.ar(KF), self.ar(GI), ALU.mult)
            P.tt("dve", self.ar(BH), self.ar(BV), self.ar(GI), ALU.mult)

        def post_a(c):
            P.tag = 'rwkv_post'
            cs = (c * 128, (c + 1) * 128)
            O = self.PSL[c]
            ms = self.mstat
            for h in range(4):
                P.bn_stats(ms[:, h, 0:6], O[:, h * 64:(h + 1) * 64])
                P.bn_aggr(ms[:, h, 6:8], ms[:, h, 0:6])
            P.act(self.small[:, 24:28], V(ms.h[:, :, 7], ms.keys), AF.Sqrt, bias=64e-5)
            P.recip(self.small[:, 28:32], self.small[:, 24:28])
            ON = 6 + (c % 2)
            on = lambda a, b: self.hs(ON, a, b, r=False)
            for h in range(4):
                P.ts("dve", on(h * 64, (h + 1) * 64), O[:, h * 64:(h + 1) * 64], ms[:, h, 6:7], ALU.subtract,
                     self.small[:, 28 + h:29 + h], ALU.mult)
            P.tt("dve", on(0, 256), on(0, 256), self.BCP[:, 2, :], ALU.mult)
            P.tt("dve", on(0, 256), on(0, 256), self.BCP[:, 3, :], ALU.add)
            pb = self.ps()
            for hp in range(2):
                P.mm(pb[:, hp * 2:(hp + 1) * 2], self.ar(16 + 6 * hp + 5, *cs), self.sel[:, :])
            P.copy("act", self.small[:, 32:36], pb[:, 0:4])
            pv = self.ps()
            for hp in range(2):
                P.transpose(pv[:, hp * 128:(hp + 1) * 128], self.ar(4 + hp, *cs), self.ident[:, :])
            for h in range(4):
                P.stt("dve", on(256 + h * 64, 256 + (h + 1) * 64), pv[:, h * 64:(h + 1) * 64],
                      self.small[:, 32 + h:33 + h], on(h * 64, (h + 1) * 64), ALU.mult, ALU.add)
            pg = self.ps()
            P.mm(pg[:, 0:256], self.ar(LX, cs[0], cs[1], 64, 128, r=True), lr[64:128, l, :])
            P.tt("dve", on(0, 256), on(256, 512), pg[:, 0:256], ALU.mult)

        def post_b(c):
            P.tag = 'rwkv_post'
            cs = (c * 128, (c + 1) * 128)
            ON = 6 + (c % 2)
            pt = self.ps()
            for t2 in range(2):
                P.transpose(pt[:, t2 * 128:(t2 + 1) * 128], self.hs(ON, t2 * 128, (t2 + 1) * 128, r=False), self.ident[:, :])
            yr = self.yT.r()
            for t2 in range(2):
                P.copy("act", yr[:, 6 + t2, cs[0]:cs[1]], pt[:, t2 * 128:(t2 + 1) * 128])

        seq = [(hp, c) for hp in range(2) for c in range(4)]
        after = {5: [lambda: post_a(0)], 6: [lambda: post_b(0), lambda: post_a(1)],
                 7: [lambda: post_b(1), lambda: post_a(2), lambda: post_a(3), lambda: post_b(2), lambda: post_b(3)]}
        if phase == 0:
            prep_pair(0)
            prep_pair(1)
            return
        import itertools

        def drain(g, k=None):
            cnt = 0
            for _ in g:
                cnt += 1
                if k is not None and cnt >= k:
                    break

        drain(self.s5_a(l, 0))
        self.rwkv_front(l, 0, 0, 0)
        for n, (hp, c) in enumerate(seq):
            tails = self.rwkv_tail(l, hp, c, n % 2)
            gens = []
            if n >= 1:
                gens.append(self.s5_c(l, n - 1))
            gens.append(self.s5_b(l, n))
            if n + 1 < 8:
                gens.append(self.s5_a(l, n + 1))
            g = itertools.chain(*gens)
            if n + 1 < len(seq):
                hp2, c2 = seq[n + 1]

                def hook(lev, tails=tails, g=g):
                    if 1 <= lev <= 3:
                        tails[lev - 1]()
                    drain(g, {0: 2, 1: 2, 2: 2, 3: 2, 4: 4, 5: 5}.get(lev, 5))
                self.rwkv_front(l, hp2, c2, (n + 1) % 2, hook)
                drain(g)
            else:
                for t in tails:
                    t()
                drain(g)
                drain(self.s5_c(l, n))
            for f in after.get(n, []):
                f()
        self.s5_tail(l)

    def rwkv_front(self, l, hp, c, st, hook=None):
        self.P.tag = 'rwkv_chunk'
        P = self.P
        AT, RT_, KH, BH, G, RK = (16 + 6 * hp + k for k in range(6))
        VS = 4 + hp
        cs = (c * 128, (c + 1) * 128)
        VT = 4 * st
        pp = self.ps()
        P.transpose(pp[:, 0:128], self.ar(VS, *cs), self.ident[:, :])
        P.transpose(pp[:, 128:256], self.ar(KH, *cs), self.ident[:, :])
        P.transpose(pp[:, 256:384], self.ar(BH, *cs), self.ident[:, :])
        P.copy("act", self.mq(VT, n=3), pp[:, 0:384])
        H = []
        for h2 in range(2):
            p0, p1 = h2 * 64, h2 * 64 + 64
            d = dict(zip(("AKT", "RKT", "RBT", "Z"), (8 + st * 8 + h2 * 4 + k for k in range(4))))
            d.update(zip(("L1", "U1", "La", "Lb", "Ua", "Ub"), (24 + h2 * 6 + k for k in range(6))))
            d.update(p0=p0, p1=p1, at=self.ar(AT, cs[0], cs[1], p0, p1), rt=self.ar(RT_, cs[0], cs[1], p0, p1),
                     kh=self.ar(KH, cs[0], cs[1], p0, p1), bh=self.ar(BH, cs[0], cs[1], p0, p1))
            H.append(d)
        if hook is not None:
            hook(0)
            self.P.tag = 'rwkv_chunk'
        banks = []
        for d in H:
            pA = self.ps()
            pB = self.ps()
            P.mm(pA[:, 0:128], d["at"], d["bh"])
            P.mm(pA[:, 128:256], d["bh"], d["at"])
            P.mm(pA[:, 256:384], d["kh"], d["at"])
            P.mm(pA[:, 384:512], d["kh"], d["rt"])
            P.mm(pB[:, 0:128], d["bh"], d["rt"])
            banks.append((pA, pB))
        for d, (pA, pB) in zip(H, banks):
            P.tt("dve", self.mq(d["U1"]), pA[:, 128:256], self.m_lt[:, :], ALU.mult)
            P.tt("dve", self.mq(d["L1"]), pA[:, 0:128], self.m_gt[:, :], ALU.mult)
            P.tt("pool", self.mq(d["Z"]), self.mq(d["U1"]), self.ident[:, :], ALU.add)
        for d, (pA, pB) in zip(H, banks):
            P.tt("dve", self.mq(d["AKT"]), pA[:, 256:384], self.m_lt[:, :], ALU.mult)
            P.tt("dve", self.mq(d["RKT"]), pA[:, 384:512], self.m_le[:, :], ALU.mult)
            P.tt("dve", self.mq(d["RBT"]), pB[:, 0:128], self.m_le[:, :], ALU.mult)
        for d in H:
            d["Lp"], d["Up"] = d["L1"], d["U1"]
        for lev in range(1, 7):
            pLs = []
            for d in H:
                Ln = d["La"] if lev % 2 == 0 else d["Lb"]
                Un = d["Ua"] if lev % 2 == 0 else d["Ub"]
                pL = self.ps()
                P.mm(pL[:, 0:128], self.mq(d["Up"]), self.mq(d["Lp"]))
                if lev < 6:
                    P.mm(pL[:, 128:256], self.mq(d["Lp"]), self.mq(d["Up"]))
                pLs.append((pL, Ln, Un))
            for d, (pL, Ln, Un) in zip(H, pLs):
                P.copy("act", self.mq(Ln), pL[:, 0:128])
                if lev < 6:
                    P.copy("act", self.mq(Un), pL[:, 128:256])
            pZs = []
            for d, (pL, Ln, Un) in zip(H, pLs):
                pZ = self.ps()
                P.mm(pZ[:, 0:128], self.mq(Ln), self.mq(d["Z"]))
                pZs.append(pZ)
                d["Lp"], d["Up"] = Ln, Un
            for d, pZ in zip(H, pZs):
                P.tt("dve", self.mq(d["Z"]), self.mq(d["Z"]), pZ[:, 0:128], ALU.add)
            if hook is not None:
                hook(lev)
                self.P.tag = 'rwkv_chunk'

    def rwkv_tail(self, l, hp, c, st):
        P = self.P
        AT, RT_, KH, BH, G, RK = (16 + 6 * hp + k for k in range(6))
        cs = (c * 128, (c + 1) * 128)
        VT, KT, BT, WT = (4 * st + k for k in range(4))
        RH = 36
        S0p = self.wkv_st[:, l, hp, :]
        q = lambda name, h2: 8 + st * 8 + h2 * 4 + ("AKT", "RKT", "RBT", "Z").index(name)

        def t1():
            P.tag = 'rwkv_tail'
            pR = self.ps()
            P.mm(pR[:, 0:128], self.ar(AT, *cs), S0p, True, False)
            for h2 in range(2):
                P.mm(pR[:, h2 * 64:h2 * 64 + 64], self.mq(q("AKT", h2)), self.mq(VT, h2 * 64, h2 * 64 + 64), False, h2 == 1)
            P.copy("act", self.mq(RH), pR[:, 0:128])

        def t2():
            P.tag = 'rwkv_tail'
            pW = self.ps()
            for h2 in range(2):
                P.mm(pW[:, h2 * 64:h2 * 64 + 64], self.mq(q("Z", h2)), self.mq(RH, h2 * 64, h2 * 64 + 64))
            P.copy("act", self.mq(WT), pW[:, 0:128])

        def t3():
            P.tag = 'rwkv_tail'
            O = self.PSL[c]
            P.mm(O[:, hp * 128:(hp + 1) * 128], self.ar(RT_, *cs), S0p, True, False)
            for h2 in range(2):
                oc = (hp * 2 + h2) * 64
                P.mm(O[:, oc:oc + 64], self.mq(q("RKT", h2)), self.mq(VT, h2 * 64, h2 * 64 + 64), False, False)
                P.mm(O[:, oc:oc + 64], self.mq(q("RBT", h2)), self.mq(WT, h2 * 64, h2 * 64 + 64), False, h2 == 1)
            pS = self.ps()
            P.mm(pS[:, 0:128], self.mq(KT), self.mq(VT), True, False)
            P.mm(pS[:, 0:128], self.mq(BT), self.mq(WT), False, True)
            for h2 in range(2):
                p0, p1 = h2 * 64, h2 * 64 + 64
                TM = 37 + h2
                Sd = self.wkv_st[p0:p1, l, hp, h2 * 64:h2 * 64 + 64]
                P.tt("dve", self.mq(TM, 0, 64, p0, p1), pS[p0:p1, h2 * 64:h2 * 64 + 64], Sd, ALU.add)
                P.ts("dve", Sd, self.mq(TM, 0, 64, p0, p1), self.ar(G, cs[1] - 1, cs[1], p0, p1), ALU.mult)

        return [t1, t2, t3]

    def prompt_state_out(self, l):
        self.P.tag = 'state_out'
        P = self.P
        o = self.dout
        nsc = dict(allow_slow_non_contiguous=True)
        for hp in range(2):
            pp = self.ps()
            P.transpose(pp[:, 0:128], self.wkv_st[:, l, hp, :], self.ident[:, :])
            P.copy("act", self.mq(27), pp[:, 0:128])
            for h2 in range(2):
                P.dma("sp", V(o["wkv_p"].ap()[l, 2 * hp + h2], ("wkv_p",)),
                      self.mq(27, h2 * 64, h2 * 64 + 64, h2 * 64, h2 * 64 + 64), "st_wkv")
        P.dma("sp", V(o["shift_p"].ap()[l].rearrange("(t p) -> p t", p=128), ("shift_p",)), self.shift_st[:, l, :], "stp_sh", **nsc)
        for ri, nm in enumerate(("re_p", "im_p")):
            dst = o[nm].ap()[l].rearrange("(j g) p -> (g p) j", g=2)
            P.dma("sp", V(dst, (nm,)), self.ssm_st[:, l, :, ri], "stp_s%d" % ri, **nsc)
        for t in range(2):
            dst = o["conv_p"].ap()[l][:, t * 128:(t + 1) * 128].rearrange("j c -> c j")
            P.dma("sp", V(dst, ("conv_p",)), self.conv_st[:, l, t, :], "stp_c%d" % t, **nsc)


def _flat(v):
    return V(v.ap.rearrange("p a b -> p (a b)"), v.keys)


class SampleMixin:
    def tk(self, s0, n, c0, c1):
        hh = self.arena.h[0:NS, s0:s0 + n, :].rearrange("p a b -> p (a b)")[:, c0:c1]
        return V(hh, tuple(("ar", j) for j in range(s0, s0 + n)))

    def tok2fm(self, src_fn, k, dst):
        P = self.P
        pp = self.ps()
        for j in range(k):
            P.transpose(pp[:, j * NS:(j + 1) * NS], src_fn(j), self.ident[0:NS, 0:NS])
        P.copy("act", dst, V(pp.h[:, 0:k * NS].rearrange("p (k c) -> p k c", k=k), pp.keys))

    def s_norm(self, gbase):
        self.P.tag = 's_norm'
        P = self.P
        x = self.tk(0, 2, 0, D)
        h = self.tk(9, 2, 0, D)
        sm = self.small
        P.act(h, x, AF.Square, accum_out=sm[0:NS, 0:1])
        P.act(sm[0:NS, 4:5], sm[0:NS, 0:1], AF.Sqrt, scale=1.0 / D, bias=1e-6)
        P.recip(sm[0:NS, 8:9], sm[0:NS, 4:5])
        P.ts("dve", h, x, sm[0:NS, 8:9], ALU.mult)
        hr = self.hT.r()
        pp = self.ps()
        for kc in range(8):
            P.transpose(pp[:, kc * NS:(kc + 1) * NS], self.tk(9, 2, kc * 128, (kc + 1) * 128), self.ident[0:NS, 0:NS])
        for kc in range(8):
            P.act(hr[:, kc, 0:NS], pp[:, kc * NS:(kc + 1) * NS], AF.Identity,
                  scale=self.PT[:, 0, gbase + kc:gbase + kc + 1])

    def sample(self):
        P = self.P
        S = self.stages
        d = self.di

        def start(w):
            self.ps_n = 4
            P.dma("sp", self.tk(0, 2, 0, D), V(d("x_sample"), ()), "xs_ld")

        S.append((None, start))
        for l in range(self.nlayer):
            self.sample_layer(l)

        def fin(w):
            x = self.tk(0, 2, 0, D)
            h = self.tk(9, 2, 0, D)
            sm = self.small
            P.act(h, x, AF.Square, accum_out=sm[0:NS, 0:1])
            P.act(sm[0:NS, 4:5], sm[0:NS, 0:1], AF.Sqrt, scale=1.0 / D, bias=1e-6)
            P.recip(sm[0:NS, 8:9], sm[0:NS, 4:5])
            P.ts("dve", h, x, sm[0:NS, 8:9], ALU.mult)
            g = self.tk(11, 2, 0, D)
            P.dma("sp", g, V(d("norm_f_g").partition_broadcast(NS), ()), "gfb")
            P.tt("dve", h, h, g, ALU.mult)
            P.dma("sp", V(self.dout["y_s"].ap(), ("y_s",)), h, "st_ys")

        S.append((None, fin))

    def sample_layer(self, l):
        P = self.P
        S = self.stages
        d = self.di
        hr = self.hT.r()

        def pre(w):
            lcd = self.scr["lcd"].ap()
            P.dma("sp", self.LC[:, 0:1024], V(lcd[l][:, 0:1024], ("lcd",)), "lc_a")
            P.dma("pool", self.LCR[:, 1024:1536], V(self.scr["lcd"].bitcast(F32R).ap()[l][:, 1024:1536], ("lcd",)), "lc_b")
            P.dma("pool", V(self.LCR.h[:, 1536:2048].rearrange("p (k c) -> p k c", k=2), self.LC.keys),
                  V(self.dir_("ssm_glu_w")[l].rearrange("(k p) c -> p k c", p=128), ()), "lc_b")
            for i, nm in enumerate(("gm_ln_g", "gm_ln_b", "tm_ln_g", "tm_ln_b")):
                P.dma("sp", self.BCP[:, i, :], V(d(nm)[l].partition_broadcast(128), ()), "bcp")
            P.dma("sp", self.tk(15, 2, 0, DTM), V(d("tm_mu")[l].partition_broadcast(NS), ()), "sbc")
            for i, nm in enumerate(("tm_w0", "tm_a0", "tm_k_k", "tm_k_a", None, "tm_r_k")):
                if nm is None:
                    continue
                src = d(nm)[l]
                if nm == "tm_r_k":
                    src = src.rearrange("h n -> (h n)")
                P.dma("sp", self.tk(17, 3, i * 256, (i + 1) * 256), V(src.partition_broadcast(NS), ()), "sbc")
            for j in range(3):
                P.dma("sp", self.tk(20, 3, j * 256, (j + 1) * 256), V(d("conv_w")[l, j].partition_broadcast(NS), ()), "sbc")
            P.dma("sp", self.tk(20, 3, 768, 1024), V(d("conv_b")[l].partition_broadcast(NS), ()), "sbc")
            P.dma("sp", self.tk(20, 3, 1024, 1280), V(d("ssm_d")[l].rearrange("g h -> (g h)").partition_broadcast(NS), ()), "sbc")
            P.dma("sp", self.tk(20, 3, 1280, 1536), V(d("ssm_glu_b")[l].partition_broadcast(NS), ()), "sbc")
            nsc = dict(allow_slow_non_contiguous=True)
            P.dma("sp", self.small[0:NS, 40:44], V(d("gm_ws")[l, :, 0, 0:1].rearrange("h o -> o h").to_broadcast([NS, 4]), ()), "sbc", **nsc)
            P.dma("sp", self.small[0:NS, 44:48], V(d("gm_bs")[l, :, 0:1].rearrange("h o -> o h").to_broadcast([NS, 4]), ()), "sbc", **nsc)
            P.seal("sbc")
            P.ts("dve", self.tk(17, 3, 4 * 256, 5 * 256), self.tk(17, 3, 3 * 256, 4 * 256), -1.0, ALU.mult, 1.0, ALU.add)
            self.s_norm(l * 8)

        S.append((None, pre))
        for i in range(5):
            c0 = 512 * i
            ncol = min(512, INC - c0)

            def ld(c0=c0, ncol=ncol):
                return self.wload(self.dir_("w_in")[l][:, c0:c0 + ncol].rearrange("(k p) c -> p k c", p=128), 8, ncol)

            def comp(w, i=i, c0=c0, ncol=ncol):
                pp = self.ps()
                for kc in range(8):
                    P.mm(pp[0:NS, 0:ncol], hr[:, kc, 0:NS], w[:, kc, :], kc == 0, kc == 7)
                P.copy("act", self.tk(2, 5, c0, c0 + ncol), pp[0:NS, 0:ncol])
                if i == 4:
                    self.s_mixers(l)

            S.append((ld, comp))
        for half in range(2):
            def ld(half=half):
                return self.wload(self.dir_("w_out")[l][:, half * 512:(half + 1) * 512].rearrange("(k p) c -> p k c", p=128), 8, 512)

            def comp(w, half=half):
                pp = self.ps()
                for kc in range(8):
                    P.mm(pp[0:NS, :], hr[:, kc, NS:2 * NS], w[:, kc, :], kc == 0, kc == 7)
                xs = self.tk(0, 2, half * 512, (half + 1) * 512)
                P.tt("dve", xs, xs, pp[0:NS, :], ALU.add)
                if half == 1:
                    self.s_norm(32 + l * 8)

            S.append((ld, comp))
        for g in range(5):
            for which in range(2):
                def ld(g=g, which=which):
                    c0 = which * DFF + 512 * g
                    return self.wload(self.dir_("ffn_w_gu")[l][:, c0:c0 + 512].rearrange("(k p) c -> p k c", p=128), 8, 512)

                def comp(w, g=g, which=which):
                    pp = self.ps()
                    for kc in range(8):
                        P.mm(pp[0:NS, :], hr[:, kc, 0:NS], w[:, kc, :], kc == 0, kc == 7)
                    dst = self.tk(9, 6, 512 * g, 512 * g + 512)
                    if which == 0:
                        P.act(dst, pp[0:NS, :], AF.Silu)
                    else:
                        P.tt("dve", dst, dst, pp[0:NS, :], ALU.mult)

                S.append((ld, comp))

        def ld_last():
            buf = self.WB[self.wbi % NWB]
            self.wbi += 1
            view = buf.h.bitcast(F32R)[:, 0:4096].rearrange("p (k g c) -> p k g c", k=8, g=2)
            for g in range(2):
                c0 = g * DFF + 2560
                src = self.dir_("ffn_w_gu")[l][:, c0:c0 + 256].rearrange("(k p) c -> p k c", p=128)
                P.dma("pool", V(view[:, :, g, :], buf.keys), V(src, ()), "ld_" + buf.name)
            return TT(view, buf.name, buf.keys)

        def comp_last(w):
            pp = self.ps()
            for kc in range(8):
                P.mm(pp[0:NS, :], hr[:, kc, 0:NS], V(w.h[:, kc, :, :].rearrange("p g c -> p (g c)"), w.keys), kc == 0, kc == 7)
            tmp = self.tk(23, 1, 0, 256)
            P.act(tmp, pp[0:NS, 0:256], AF.Silu)
            P.tt("dve", self.tk(9, 6, 2560, 2816), tmp, pp[0:NS, 256:512], ALU.mult)
            yr = self.yT.r()
            dst = V(yr.h[:, 0, 0:352].rearrange("p (k c) -> p k c", k=22), yr.keys)
            self.tok2fm(lambda j: self.tk(9, 6, j * 128, (j + 1) * 128), 22, dst)

        S.append((ld_last, comp_last))
        for half in range(2):
            for jg in range(3):
                nj = 8 if jg < 2 else 6

                def ld(half=half, jg=jg, nj=nj):
                    src = self.dir_("ffn_w_down")[l][jg * 1024:jg * 1024 + nj * 128, half * 512:(half + 1) * 512]
                    return self.wload(src.rearrange("(j p) c -> p j c", p=128), nj, 512)

                def comp(w, half=half, jg=jg, nj=nj):
                    yr = self.yT.r()
                    for jj in range(nj):
                        j = jg * 8 + jj
                        P.mm(self.PSL[0][0:NS, :], V(yr.h[:, 0, j * NS:(j + 1) * NS], yr.keys), w[:, jj, :], j == 0, j == 21)
                    if jg == 2:
                        xs = self.tk(0, 2, half * 512, (half + 1) * 512)
                        P.tt("dve", xs, xs, self.PSL[0][0:NS, :], ALU.add)

                S.append((ld, comp))

    def s_mixers(self, l):
        self.P.tag = 's_mixers'
        P = self.P
        d = self.di
        o = self.dout
        hr = self.hT.r()
        lr = self.lora.r()
        sm = self.small
        ms = self.mstat
        Z = lambda a, b: self.tk(2, 5, a, b)
        M = lambda a, b: self.tk(7, 2, a, b)
        p6 = lambda i: self.tk(17, 3, i * 256, (i + 1) * 256)
        c6 = lambda i: self.tk(20, 3, i * 256, (i + 1) * 256)
        h4 = lambda v: V(v.ap.rearrange("p (h n) -> p h n", h=4), v.keys)
        b4 = lambda v: V(v.ap.unsqueeze(2).to_broadcast([NS, 4, 64]), v.keys)
        u = self.tk(9, 1, 0, 256)
        vf = self.tk(9, 1, 256, 512)
        P.act(u, Z(0, 256), AF.Gelu_apprx_tanh)
        P.act(vf, Z(256, 512), AF.Gelu_apprx_tanh)
        P.bn_stats(ms[0:NS, 0, 0:6], vf)
        P.bn_aggr(ms[0:NS, 0, 6:8], ms[0:NS, 0, 0:6])
        P.act(sm[0:NS, 16:17], ms[0:NS, 0, 7:8], AF.Sqrt, bias=1e-5)
        P.recip(sm[0:NS, 20:21], sm[0:NS, 16:17])
        P.ts("dve", vf, vf, ms[0:NS, 0, 6:7], ALU.subtract, sm[0:NS, 20:21], ALU.mult)
        P.tt("dve", vf, vf, self.BCP[0:NS, 0, :], ALU.mult)
        P.tt("dve", vf, vf, self.BCP[0:NS, 1, :], ALU.add)
        P.dma("sp", V(o["chv_s"].ap()[l], ("chv_s",)), vf, "st_chv")
        t = self.tk(10, 1, 0, 256)
        P.tt("dve", h4(t), h4(vf), b4(sm[0:NS, 40:44]), ALU.mult)
        P.tt("dve", h4(t), h4(t), b4(sm[0:NS, 44:48]), ALU.add)
        P.tt("dve", M(0, 256), u, t, ALU.mult)
        if 'sm1' in DBG:
            return
        buf = self.tk(23, 1, 0, 512)
        P.dma("sp", buf, V(d("state_conv")[l].rearrange("b j c -> b (j c)"), ()), "s_ld_c")
        zz = self.tk(10, 1, 256, 512)
        P.tt("dve", zz, Z(1280, 1536), Z(768, 1024), ALU.mult)
        y = self.tk(11, 1, 0, 256)
        t2 = self.tk(11, 1, 256, 512)
        P.tt("dve", y, zz, c6(2), ALU.mult)
        P.tt("dve", t2, self.tk(23, 1, 0, 256), c6(0), ALU.mult)
        P.tt("dve", y, y, t2, ALU.add)
        P.tt("dve", t2, self.tk(23, 1, 256, 512), c6(1), ALU.mult)
        P.tt("dve", y, y, t2, ALU.add)
        P.tt("dve", y, y, c6(3), ALU.add)
        P.tt("dve", M(512, 768), y, Z(1024, 1280), ALU.mult)
        P.dma("sp", V(o["conv_s"].ap()[l][:, 0, :], ("conv_s",)), self.tk(23, 1, 256, 512), "st_cv0")
        P.dma("sp", V(o["conv_s"].ap()[l][:, 1, :], ("conv_s",)), zz, "st_cv1")
        if 'sm2' in DBG:
            return
        pp = self.ps()
        for j in range(2):
            P.transpose(pp[:, j * NS:(j + 1) * NS], Z(512 + j * 128, 640 + j * 128), self.ident[0:NS, 0:NS])
        ppv = lambda p0, p1: V(pp.h[p0:p1, 0:2 * NS].rearrange("p (k c) -> p k c", k=2), pp.keys)
        P.copy("act", hr[:, 0:2, 32:48], ppv(0, 128))
        P.copy("act", hr[64:128, 0:2, 48:64], ppv(64, 128))
        P.copy("dve", hr[64:96, 0:2, 48:64], V(self.zeros.h[64:96, 0:32].rearrange("p (k c) -> p k c", k=2), self.zeros.keys))
        if 'sm21' in DBG:
            return
        P.dma("sp", self.tk(9, 2, 0, 1024), V(d("state_ssm_re")[l].rearrange("b g p -> b (g p)"), ()), "s_ld_re")
        P.dma("sp", self.tk(11, 2, 0, 1024), V(d("state_ssm_im")[l].rearrange("b g p -> b (g p)"), ()), "s_ld_im")
        m3 = lambda q: V(self.mat.h[:, q, :].rearrange("p (j c) -> p j c", j=8), (("m", q),))
        self.tok2fm(lambda j: self.tk(9, 2, j * 128, (j + 1) * 128), 8, m3(0))
        self.tok2fm(lambda j: self.tk(11, 2, j * 128, (j + 1) * 128), 8, m3(1))
        if 'sm22' in DBG:
            return
        pbs = [self.ps() for _ in range(4)]
        for j in range(8):
            half, jl = divmod(j, 4)
            for ri in range(2):
                c0 = 1024 + (half * 2 + ri) * 128
                col = (half * 2 + ri) * NS
                if jl < 3:
                    rows = slice(32 * jl, 32 * jl + 32)
                    P.mm(pbs[jl][:, col:col + NS], V(self.LCR.h[rows, c0:c0 + 128], self.LCR.keys), hr[rows, half, 32:48])
                else:
                    P.mm(pbs[jl][:, col:col + NS], self.LCR[64:128, c0:c0 + 128], hr[64:128, half, 48:64])
        pbv = lambda jl, ri: V(pbs[jl].h[:, 0:64].rearrange("p (h r c) -> p h r c", h=2, r=2)[:, :, ri, :], pbs[jl].keys)
        m3j = lambda q, jl: V(self.mat.h[:, q, :].rearrange("p (h j c) -> p h j c", h=2, j=4)[:, :, jl, :], (("m", q),))
        lb = lambda idx: V(self.SPt.h[:, idx, l * 8:(l + 1) * 8].unsqueeze(2).to_broadcast([128, 8, NS]), self.SPt.keys)
        lbr, lbi = lb(self.I_LBR), lb(self.I_LBI)
        P.tt("dve", m3(4), m3(0), lbr, ALU.mult)
        P.tt("dve", m3(5), m3(1), lbi, ALU.mult)
        P.tt("dve", m3(4), m3(4), m3(5), ALU.subtract)
        for jl in range(4):
            P.tt("dve", m3j(2, jl), m3j(4, jl), pbv(jl, 0), ALU.add)
        P.tt("dve", m3(4), m3(1), lbr, ALU.mult)
        P.tt("dve", m3(5), m3(0), lbi, ALU.mult)
        P.tt("dve", m3(4), m3(4), m3(5), ALU.add)
        for jl in range(4):
            P.tt("dve", m3j(3, jl), m3j(4, jl), pbv(jl, 1), ALU.add)
        if 'sm23' in DBG:
            return
        py = self.ps()
        for j in range(8):
            half, jl = divmod(j, 4)
            for ri in range(2):
                c0 = 512 + (half * 2 + ri) * 128 + jl * 32
                P.mm(py[0:NS, half * 128 + jl * 32:half * 128 + jl * 32 + 32],
                     V(self.mat.h[:, 2 + ri, j * NS:(j + 1) * NS], (("m", 2 + ri),)), self.LC[:, c0:c0 + 32], ri == 0, ri == 1)
        yv = self.tk(10, 1, 0, 256)
        P.tt("dve", yv, Z(512, 768), c6(4), ALU.mult)
        P.tt("dve", yv, yv, py[0:NS, 0:256], ALU.add)
        P.act(yv, yv, AF.Gelu_apprx_tanh)
        if 'sm24' in DBG:
            return
        self.tok2fm(lambda j: self.tk(10, 1, j * 128, (j + 1) * 128), 2, hr[:, 2:4, 32:48])
        pg = self.ps()
        for kc in range(2):
            P.mm(pg[0:NS, 0:256], hr[:, 2 + kc, 32:48], self.LCR[:, 1536 + kc * 256:1792 + kc * 256], kc == 0, kc == 1)
        P.tt("dve", y, pg[0:NS, 0:256], c6(5), ALU.add)
        P.act(y, y, AF.Sigmoid)
        P.tt("dve", M(256, 512), yv, y, ALU.mult)
        if 'sm25' in DBG:
            return
        for ri, nm in enumerate(("re_s", "im_s")):
            pa_, pb_ = self.ps(), self.ps()
            for j in range(8):
                bank = pa_ if j < 4 else pb_
                P.transpose(bank[0:NS, (j % 4) * 128:(j % 4 + 1) * 128],
                            V(self.mat.h[:, 2 + ri, j * NS:(j + 1) * NS], (("m", 2 + ri),)), self.ident[:, :])
            so = self.tk(13, 2, 0, 1024)
            P.copy("act", self.tk(13, 2, 0, 512), pa_[0:NS, :])
            P.copy("act", self.tk(13, 2, 512, 1024), pb_[0:NS, :])
            P.dma("sp", V(o[nm].ap()[l].rearrange("b g p -> b (g p)"), (nm,)), so, "st_s%d" % ri)
        if 'sm3' in DBG:
            return
        zs = lambda a, b: self.tk(13, 2, a, b)
        zsa = zs(0, DTM)
        zd = Z(1536, 2432)
        P.dma("sp", zsa, V(d("state_shift")[l], ()), "s_ld2")
        P.dma("sp", V(o["shift_s"].ap()[l], ("shift_s",)), zd, "st_sh")
        P.tt("dve", zsa, zsa, zd, ALU.subtract)
        P.tt("dve", zsa, zsa, self.tk(15, 2, 0, DTM), ALU.mult)
        P.tt("dve", zsa, zsa, zd, ALU.add)
        lx = self.tk(23, 1, 0, 128)
        P.act(self.tk(23, 1, 0, 32), zs(768, 800), AF.Tanh)
        P.act(self.tk(23, 1, 32, 64), zs(800, 832), AF.Copy)
        P.act(self.tk(23, 1, 64, 128), zs(832, 896), AF.Sigmoid)
        pp = self.ps()
        P.transpose(pp[:, 0:NS], lx, self.ident[0:NS, 0:NS])
        P.copy("act", hr[:, 4, 32:48], pp[:, 0:NS])
        pq = self.ps()
        P.mm(pq[0:NS, 0:256], hr[0:32, 4, 32:48], lr[0:32, l, :])
        pa = self.ps()
        P.mm(pa[0:NS, 0:256], hr[32:64, 4, 32:48], lr[32:64, l, :])
        P.mm(self.PSL[1][0:NS, 0:256], hr[64:128, 4, 32:48], lr[64:128, l, :])
        vec = lambda j: self.tk(20, 3, j * 256, (j + 1) * 256)
        t = self.tk(9, 1, 0, 256)
        A = self.tk(9, 1, 256, 512)
        P.tt("dve", t, pq[0:NS, 0:256], p6(0), ALU.add)
        P.act(t, t, AF.Sigmoid)
        P.act(vec(1), t, AF.Exp, scale=-0.6065306597126334)
        P.tt("dve", A, pa[0:NS, 0:256], p6(1), ALU.add)
        P.act(A, A, AF.Sigmoid)
        kk = self.tk(10, 1, 0, 256)
        sq = self.tk(10, 1, 256, 512)
        P.tt("dve", kk, zs(256, 512), p6(2), ALU.mult)
        P.tt("dve", sq, kk, kk, ALU.mult)
        P.reduce(sm[0:NS, 24:28], h4(sq), ALU.add)
        P.act(sm[0:NS, 24:28], sm[0:NS, 24:28], AF.Sqrt)
        P.ts("dve", sm[0:NS, 24:28], sm[0:NS, 24:28], 1e-12, ALU.max)
        P.recip(sm[0:NS, 28:32], sm[0:NS, 24:28])
        P.tt("dve", h4(kk), h4(kk), b4(sm[0:NS, 28:32]), ALU.mult)
        P.tt("dve", t, A, p6(3), ALU.mult)
        P.tt("dve", t, t, p6(4), ALU.add)
        P.tt("dve", vec(2), zs(256, 512), t, ALU.mult)
        P.ts("dve", vec(4), kk, -1.0, ALU.mult)
        P.tt("dve", vec(5), kk, A, ALU.mult)
        P.copy("act", vec(0), zs(0, 256))
        P.copy("act", vec(3), zs(512, 768))
        P.tt("dve", t, zs(0, 256), vec(2), ALU.mult)
        P.tt("dve", t, t, p6(5), ALU.mult)
        P.reduce(sm[0:NS, 32:36], h4(t), ALU.add)
        if 'sm4' in DBG:
            return
        svec = self.scr["svec"].ap()
        P.dma("sp", V(svec.rearrange("j b c -> b j c"), ("svec",)),
              V(self.tk(20, 3, 0, 1536).ap.rearrange("p (j c) -> p j c", j=6), self.tk(20, 3, 0, 1536).keys), "sv_w")
        hi = lambda s0, n, c0, c1: V(self.arena.h[64:128, s0:s0 + n, :].rearrange("p a b -> p (a b)")[:, c0:c1],
                                     tuple(("ar", j) for j in range(s0, s0 + n)))
        vS = hi(16, 1, 0, 384)
        P.dma("sp", V(vS.ap.rearrange("p (j n) -> p j n", j=6), vS.keys),
              V(svec.rearrange("j b (h n) -> (b h) j n", h=4), ("svec",)), "sv_r")
        Sv = hi(0, 8, 0, 4096)
        Tv = hi(8, 8, 0, 4096)
        P.dma("sp", Sv, V(d("state_wkv")[l].rearrange("b h v k -> (b h) (v k)"), ()), "s_ld3")
        S3 = V(Sv.ap.rearrange("p (v k) -> p v k", v=64), Sv.keys)
        T3 = V(Tv.ap.rearrange("p (v k) -> p v k", v=64), Tv.keys)
        vj = lambda j: hi(16, 1, j * 64, (j + 1) * 64)
        kbc = lambda j: V(vj(j).ap.unsqueeze(1).to_broadcast([64, 64, 64]), vj(j).keys)
        vbc = lambda v: V(v.ap.unsqueeze(2).to_broadcast([64, 64, 64]), v.keys)
        sa = hi(17, 1, 0, 64)
        ov = hi(17, 1, 64, 128)
        P.tt("dve", T3, S3, kbc(4), ALU.mult)
        P.reduce(sa, T3, ALU.add)
        P.tt("pool", S3, S3, kbc(1), ALU.mult)
        P.tt("dve", T3, vbc(sa), kbc(5), ALU.mult)
        P.tt("pool", S3, S3, T3, ALU.add)
        P.tt("dve", T3, vbc(vj(3)), kbc(2), ALU.mult)
        P.tt("pool", S3, S3, T3, ALU.add)
        P.tt("dve", T3, S3, kbc(0), ALU.mult)
        P.reduce(ov, T3, ALU.add)
        P.dma("sp", V(o["wkv_s"].ap()[l].rearrange("b h v k -> (b h) (v k)"), ("wkv_s",)), Sv, "st_s3")
        if 'sm5' in DBG:
            return
        P.bn_stats(ms[64:128, 0, 0:6], ov)
        P.bn_aggr(ms[64:128, 0, 6:8], ms[64:128, 0, 0:6])
        P.act(sm[64:128, 50:51], ms[64:128, 0, 7:8], AF.Sqrt, bias=64e-5)
        P.recip(sm[64:128, 51:52], sm[64:128, 50:51])
        onh = hi(17, 1, 128, 192)
        P.ts("dve", onh, ov, ms[64:128, 0, 6:7], ALU.subtract, sm[64:128, 51:52], ALU.mult)
        son = self.scr["son"].ap()
        P.dma("sp", V(son.rearrange("b (h n) -> (b h) n", h=4), ("son",)), onh, "so_w")
        on = self.tk(9, 1, 0, 256)
        P.dma("sp", on, V(son, ("son",)), "so_r")
        P.tt("dve", on, on, self.BCP[0:NS, 2, :], ALU.mult)
        P.tt("dve", on, on, self.BCP[0:NS, 3, :], ALU.add)
        P.tt("dve", h4(A), h4(zs(512, 768)), b4(sm[0:NS, 32:36]), ALU.mult)
        P.tt("dve", on, on, A, ALU.add)
        P.tt("dve", M(768, 1024), on, self.PSL[1][0:NS, 0:256], ALU.mult)
        self.tok2fm(lambda j: M(j * 128, (j + 1) * 128), 8, hr[:, :, NS:2 * NS])


class Builder(SampleMixin, Builder0):
    pass


_CACHE = {}


def _get_nc():
    if "nc" not in _CACHE:
        _CACHE["nc"] = Builder().nc
    return _CACHE["nc"]


def kernel(**inputs):
    inp = {k: np.ascontiguousarray(np.asarray(v, dtype=np.float32)) for k, v in inputs.items()}
    nc = _get_nc()
    in_maps = []
    for c in range(NCORES):
        m = {}
        for n in W_SHAPES:
            m[n] = inp[n]
        m["x_prompt"] = np.ascontiguousarray(inp["x_prompt"][c])
        m["x_sample"] = np.ascontiguousarray(inp["x_sample"][c * NS:(c + 1) * NS, 0, :])
        for n in ("state_wkv", "state_shift", "state_ssm_re", "state_ssm_im", "state_conv"):
            m[n] = np.ascontiguousarray(inp[n][:, c * NS:(c + 1) * NS])
        in_maps.append(m)
    res = run_bass_kernel_spmd(nc, in_maps, core_ids=list(range(NCORES)))
    R = res.results
    cat = lambda n, ax: np.concatenate([np.asarray(r[n]) for r in R], axis=ax)
    stk = lambda n: np.stack([np.asarray(r[n]) for r in R], axis=1)
    y_p = np.stack([np.asarray(r["y_p"]) for r in R], axis=0)
    y_s = cat("y_s", 0)[:, None, :]
    outs = (y_p, y_s, stk("wkv_p"), cat("wkv_s", 1), stk("shift_p"), cat("shift_s", 1),
            stk("re_p"), cat("re_s", 1), stk("im_p"), cat("im_s", 1),
            stk("conv_p"), cat("conv_s", 1), cat("chv_s", 1)[:, :, None, :])
    return tuple(np.ascontiguousarray(o.astype(np.float32)) for o in outs)
```

```python
from contextlib import ExitStack
import math
import numpy as np
import concourse.bass as bass
import concourse.mybir as mybir
from concourse.bass_utils import run_bass_kernel_spmd

F32 = mybir.dt.float32
F32R = mybir.dt.float32r
I32 = mybir.dt.int32
ALU = mybir.AluOpType
AF = mybir.ActivationFunctionType
AX = mybir.AxisListType

EPOCH = 20000
DBG = set()
NCORES = 8
L = 4
D = 1024
T = 2048
TB = 512
NBLK = T // TB
NS = 16
INC = 2432
DFF = 2816
DTM = 896


class V:
    __slots__ = ("ap", "keys")

    def __init__(self, ap, keys):
        self.ap = ap
        self.keys = tuple(keys)


def bc(v, shape):
    return V(v.ap.to_broadcast(list(shape)), v.keys)


class TT:
    def __init__(self, h, name, keys=None):
        self.h = h
        self.name = name
        self.keys = (name,) if keys is None else tuple(keys)

    def __getitem__(self, idx):
        return V(self.h[idx], self.keys)

    def r(self):
        return TT(self.h.bitcast(F32R), self.name, self.keys)

    def i32(self):
        return TT(self.h.bitcast(I32), self.name, self.keys)


class Prog:
    ENG = ["pe", "act", "dve", "pool", "sp"]

    def __init__(self, nc):
        self.nc = nc
        self.ops = {e: [] for e in self.ENG}
        self.count = {e: 0 for e in self.ENG}
        self.lastw = {}
        self.readers = {}
        self.dma_cnt = {}
        self.sealed = {}
        self.semids = set()
        self.stack = ExitStack()
        self.tag = 'init'
        self.suffix = ''
        self.name2tag = {}

    def sbuf(self, name, shape, dtype=F32):
        h = self.stack.enter_context(self.nc.sbuf_tensor(name, list(shape), dtype))
        return TT(h, name)

    def psum(self, name, shape, dtype=F32):
        h = self.stack.enter_context(self.nc.psum_tensor(name, list(shape), dtype))
        return TT(h, name)

    def seal(self, group):
        self.sealed[("d", group)] = self.dma_cnt.get(group, 0)

    def _add(self, eng, fn, reads, writes, dma_group=None):
        deps = {}
        for k in reads:
            w = self.lastw.get(k)
            if w:
                for s, v in w.items():
                    if deps.get(s, 0) < v:
                        deps[s] = v
            if isinstance(k, str) and k.startswith("ps"):
                rd = self.readers.get(k)
                if rd:
                    for s, v in rd.items():
                        if s[1] != eng and deps.get(s, 0) < v:
                            deps[s] = v
        for k in writes:
            w = self.lastw.get(k)
            if w:
                for s, v in w.items():
                    if deps.get(s, 0) < v:
                        deps[s] = v
            rd = self.readers.get(k)
            if rd:
                for s, v in rd.items():
                    if deps.get(s, 0) < v:
                        deps[s] = v
        for s_, v_ in list(deps.items()):
            sv = self.sealed.get(s_)
            if sv is not None and v_ <= sv:
                deps[s_] = sv
        if dma_group is None:
            self.count[eng] += 1
            ep, val = divmod(self.count[eng] - 1, EPOCH)
            tok = (("e", eng, ep), val + 1)
            inc = 1
        else:
            g = self.dma_cnt.get(dma_group, 0) + 16
            self.dma_cnt[dma_group] = g
            tok = (("d", dma_group), g)
            inc = 16
        if eng == "pe":
            deps = {s: v for s, v in deps.items() if not (s[0] == "e" and s[1] == "pe")}
        self.semids.add(tok[0])
        self.ops[eng].append((fn, deps, tok, inc, self.tag + self.suffix))
        for k in reads:
            rd = self.readers.setdefault(k, {})
            if rd.get(tok[0], 0) < tok[1]:
                rd[tok[0]] = tok[1]
        for k in writes:
            w = self.lastw.setdefault(k, {})
            if w.get(tok[0], 0) < tok[1]:
                w[tok[0]] = tok[1]
        return tok

    @staticmethod
    def _keys(*vs):
        ks = []
        for v in vs:
            if isinstance(v, V):
                ks.extend(v.keys)
        return ks

    @staticmethod
    def _ap(v):
        return v.ap if isinstance(v, V) else v

    def mm(self, out, lhsT, rhs, start=True, stop=True):
        o, l, r = out.ap, lhsT.ap, rhs.ap
        return self._add("pe", lambda e: e.matmul(o, l, r, start=start, stop=stop),
                         self._keys(lhsT, rhs), self._keys(out))

    def transpose(self, out, in_, ident):
        o, i, d = out.ap, in_.ap, ident.ap
        return self._add("pe", lambda e: e.transpose(o, i, d), self._keys(in_, ident), self._keys(out))

    def act(self, out, in_, func, scale=1.0, bias=None, accum_out=None):
        o, i = out.ap, in_.ap
        kw = {}
        if bias is not None:
            kw["bias"] = self._ap(bias)
        if accum_out is not None:
            kw["accum_out"] = accum_out.ap
        sc = self._ap(scale)
        return self._add("act", lambda e: e.activation(o, i, func, scale=sc, **kw),
                         self._keys(in_, scale, bias), self._keys(out, accum_out))

    def tt(self, eng, out, in0, in1, op):
        o, a, b = out.ap, in0.ap, in1.ap
        return self._add(eng, lambda e: e.tensor_tensor(o, a, b, op), self._keys(in0, in1), self._keys(out))

    def ts(self, eng, out, in0, s1, op0, s2=None, op1=None):
        o, a = out.ap, in0.ap
        x1, x2 = self._ap(s1), self._ap(s2)
        kw = {}
        if op1 is not None:
            kw["op1"] = op1
        return self._add(eng, lambda e: e.tensor_scalar(o, a, x1, x2, op0, **kw),
                         self._keys(in0, s1, s2), self._keys(out))

    def stt(self, eng, out, in0, scalar, in1, op0, op1):
        o, a, b = out.ap, in0.ap, in1.ap
        s = self._ap(scalar)
        return self._add(eng, lambda e: e.scalar_tensor_tensor(o, a, s, b, op0, op1),
                         self._keys(in0, scalar, in1), self._keys(out))

    def scan(self, out, d0, d1, initial, op0, op1):
        o, a, b = out.ap, d0.ap, d1.ap
        ini = self._ap(initial)
        return self._add("dve", lambda e: e.tensor_tensor_scan(o, a, b, ini, op0, op1),
                         self._keys(d0, d1, initial), self._keys(out))

    def copy(self, eng, out, in_):
        o, i = out.ap, in_.ap
        if eng == "act":
            return self._add("act", lambda e: e.copy(o, i), self._keys(in_), self._keys(out))
        return self._add(eng, lambda e: e.tensor_copy(o, i), self._keys(in_), self._keys(out))

    def memset(self, eng, out, val):
        o = out.ap
        return self._add(eng, lambda e: e.memset(o, val), [], self._keys(out))

    def reduce(self, out, in_, op, axis=AX.X):
        o, i = out.ap, in_.ap
        return self._add("dve", lambda e: e.tensor_reduce(o, i, axis, op), self._keys(in_), self._keys(out))

    def recip(self, out, in_):
        o, i = out.ap, in_.ap
        return self._add("dve", lambda e: e.reciprocal(o, i), self._keys(in_), self._keys(out))

    def bn_stats(self, out, in_):
        o, i = out.ap, in_.ap
        return self._add("dve", lambda e: e.bn_stats(o, i), self._keys(in_), self._keys(out))

    def bn_aggr(self, out, in_):
        o, i = out.ap, in_.ap
        return self._add("dve", lambda e: e.bn_aggr(o, i), self._keys(in_), self._keys(out))

    def affine_select(self, out, in_, pattern, compare_op, fill, base, channel_multiplier):
        o, i = out.ap, in_.ap
        return self._add("pool", lambda e: e.affine_select(o, i, pattern, compare_op, fill, base=base,
                                                           channel_multiplier=channel_multiplier),
                         self._keys(in_), self._keys(out))

    def dma(self, q, out, in_, group, **kw):
        o, i = out.ap, in_.ap
        return self._add(q, lambda e: e.dma_start(out=o, in_=i, **kw), self._keys(in_), self._keys(out),
                         dma_group=group)

    def emit(self):
        nc = self.nc
        sems = {}
        for sid in sorted(self.semids, key=str):
            nm = "s_" + "_".join(str(x) for x in sid)
            sems[sid] = self.stack.enter_context(nc.semaphore(nm))
        finals = [(("d", g), v) for g, v in self.dma_cnt.items()]
        sealed = self.sealed
        bname = {"pe": "tensor", "act": "scalar", "dve": "vector", "pool": "gpsimd", "sp": "sync"}
        with nc.Block() as block:
            for eng in self.ENG:
                ops = self.ops[eng]

                def body(e, ops=ops, eng=eng):
                    seen = {}
                    for fn, deps, tok, inc, tag in ops:
                        for s, v in deps.items():
                            if seen.get(s, 0) < v:
                                e.wait_ge(sems[s], v)
                                seen[s] = v
                        ins = fn(e)
                        ins.then_inc(sems[tok[0]], inc)
                        try:
                            self.name2tag[ins.ins.name] = tag
                        except Exception:
                            pass
                    if eng == "sp":
                        for s, v in finals:
                            if seen.get(s, 0) < v:
                                e.wait_ge(sems[s], v)

                getattr(block, bname[eng])(body)
        self.stack.close()


NSLOT = 28
NWB = 3
TWO_PI_SAFE = 6.2831845
INV_2PI = 1.0 / (2.0 * math.pi)

W_SHAPES = {
    "norm1_g": (L, D), "w_in": (L, D, INC), "gm_ln_g": (L, 256), "gm_ln_b": (L, 256),
    "gm_ws": (L, 4, 128, 128), "gm_bs": (L, 4, 128),
    "ssm_a_re": (L, 16, 64), "ssm_a_im": (L, 16, 64), "ssm_log_dt": (L, 16, 64),
    "ssm_b_re": (L, 16, 64, 16), "ssm_b_im": (L, 16, 64, 16),
    "ssm_c_re": (L, 16, 16, 64), "ssm_c_im": (L, 16, 16, 64), "ssm_d": (L, 16, 16),
    "ssm_glu_w": (L, 256, 256), "ssm_glu_b": (L, 256), "conv_w": (L, 3, 256), "conv_b": (L, 256),
    "tm_mu": (L, DTM), "tm_w0": (L, 256), "tm_w2": (L, 32, 256), "tm_a0": (L, 256),
    "tm_a2": (L, 32, 256), "tm_g2": (L, 64, 256), "tm_k_k": (L, 256), "tm_k_a": (L, 256),
    "tm_r_k": (L, 4, 64), "tm_ln_g": (L, 256), "tm_ln_b": (L, 256),
    "w_out": (L, D, D), "norm2_g": (L, D), "ffn_w_gu": (L, D, 2 * DFF), "ffn_w_down": (L, DFF, D),
    "norm_f_g": (D,),
}
IN_SHAPES = {
    "x_prompt": (T, D), "x_sample": (NS, D), "state_wkv": (L, NS, 4, 64, 64),
    "state_shift": (L, NS, DTM), "state_ssm_re": (L, NS, 16, 64), "state_ssm_im": (L, NS, 16, 64),
    "state_conv": (L, NS, 2, 256),
}
OUT_SHAPES = {
    "y_p": (T, D), "y_s": (NS, D), "wkv_p": (L, 4, 64, 64), "wkv_s": (L, NS, 4, 64, 64),
    "shift_p": (L, DTM), "shift_s": (L, NS, DTM), "re_p": (L, 16, 64), "re_s": (L, NS, 16, 64),
    "im_p": (L, 16, 64), "im_s": (L, NS, 16, 64), "conv_p": (L, 2, 256), "conv_s": (L, NS, 2, 256),
    "chv_s": (L, NS, 256),
}


class Builder0:
    def __init__(self, do_sample=True, nblk=NBLK, nlayer=L, max_stages=None, do_prep=True):
        self.do_sample = do_sample
        self.nblk = nblk
        self.nlayer = nlayer
        nc = bass.Bass("TRN2", target_bir_lowering=False)
        self.nc = nc
        self.P = Prog(nc)
        self.din = {}
        for n, s in list(IN_SHAPES.items()) + list(W_SHAPES.items()):
            self.din[n] = nc.dram_tensor(n, list(s), F32, kind="ExternalInput")
        self.dout = {}
        for n, s in OUT_SHAPES.items():
            self.dout[n] = nc.dram_tensor(n, list(s), F32, kind="ExternalOutput")
        self.scr = {}
        for n, s in {"etab": (L, 8, 128, 1024), "lcd": (L, 128, 1536),
                     "svec": (6, NS, 256), "son": (NS, 256)}.items():
            self.scr[n] = nc.dram_tensor("scr_" + n, list(s), F32, kind="Internal")
        self.max_stages = max_stages
        self.alloc()
        if do_prep:
            self.prep()
        self.main()
        self.P.emit()

    def di(self, name):
        return self.din[name].ap()

    def dir_(self, name):
        return self.din[name].bitcast(F32R).ap()

    def ar(self, i, c0=0, c1=512, p0=0, p1=128, r=False):
        h = self.arena_r if r else self.arena.h
        return V(h[p0:p1, i, c0:c1], (("ar", i),))

    def hs(self, i, c0=0, c1=512, p0=0, p1=128, r=True):
        h = self.hTr if r else self.hTf
        return V(h[p0:p1, i, c0:c1], (("hT", i),))

    def arn(self, i, n, c0=0, c1=512, r=False):
        h = self.arena_r if r else self.arena.h
        return V(h[:, i:i + n, c0:c1], tuple(("ar", j) for j in range(i, i + n)))

    def ps(self):
        t = self.PS[self.psi % self.ps_n]
        self.psi += 1
        return t

    def mq(self, q, c0=0, c1=128, p0=0, p1=128, n=1):
        if n == 1:
            return V(self.mat.h[p0:p1, q, c0:c1], (("m", q),))
        hh = self.mat.h[p0:p1, q:q + n, :].rearrange("p a b -> p (a b)")
        return V(hh, tuple(("m", j) for j in range(q, q + n)))

    def ms(self, i, c0=0, c1=512, p0=0, p1=128):
        hh = self.mat.h[p0:p1, 4 * i:4 * i + 4, :].rearrange("p a b -> p (a b)")[:, c0:c1]
        return V(hh, tuple(("m", j) for j in range(4 * i, 4 * i + 4)))

    def pt(self, g, r0, n=1):
        return self.PT[:, g, r0:r0 + n]

    def alias_r(self, tt, shape):
        addr = None
        for a in self.nc.allocations:
            if a.name == tt.name + "_set":
                addr = a.memorylocations[0].addr
        assert addr is not None
        return self.nc.alloc_sbuf_tensor_at(tt.name + "_r", list(shape), F32R, offset=addr)

    def alloc(self):
        P = self.P
        self.arena = P.sbuf("arena", [128, NSLOT, 512])
        self.arena_r = self.alias_r(self.arena, [128, NSLOT, 512])
        self.PS = [P.psum("ps%d" % i, [128, 512]) for i in range(8)]
        self.psi = 0
        self.PSL = self.PS[4:8]
        self.ps_n = 4
        self.ident = P.sbuf("ident", [128, 128])
        self.ones = P.sbuf("ones", [128, 128])
        self.zeros = P.sbuf("zeros", [128, 512])
        self.m_le = P.sbuf("m_le", [128, 128])
        self.m_lt = P.sbuf("m_lt", [128, 128])
        self.m_gt = P.sbuf("m_gt", [128, 128])
        self.bones = P.sbuf("bones", [128, 128])
        self.sel = P.sbuf("sel", [128, 2])
        self.RT = P.sbuf("RT", [128, 3, 128])
        self.PT = P.sbuf("PT", [128, 3, 128])
        self.SPt = P.sbuf("SPt", [128, 24, 32])
        self.SPi = P.sbuf("SPi", [128, 32], I32)
        self.lora = P.sbuf("lora", [128, L, 256])
        self.xb = P.sbuf("xb", [128, 4, D])
        self.hT = P.sbuf("hT", [128, 8, TB])
        self.hT.keys = tuple(("hT", k) for k in range(8))
        hT_addr = [a.memorylocations[0].addr for a in self.nc.allocations if a.name == "hT_set"][0]
        self.hTf = self.nc.alloc_sbuf_tensor_at("hT_f", [128, 8, TB], F32, offset=hT_addr)
        self.hTr = self.hT.h.bitcast(F32R)
        self.yT = P.sbuf("yT", [128, 8, TB])
        self.WB = [P.sbuf("wb%d" % i, [128, 4096]) for i in range(NWB)]
        self.wbi = 0
        stg_addr = [a.memorylocations[0].addr for a in self.nc.allocations if a.name == "wb0_set"][0]
        self.STG = TT(self.nc.alloc_sbuf_tensor_at("STG", [128, 1536], F32, offset=stg_addr), "STG", ("wb0",))
        self.LC = P.sbuf("LC", [128, 2048])
        self.LCR = TT(self.alias_r(self.LC, [128, 2048]), "LC")
        self.BCP = P.sbuf("BCP", [128, 4, 256])
        self.small = P.sbuf("small", [128, 64])
        self.ssm_st = P.sbuf("ssm_st", [128, L, 8, 2])
        self.conv_st = P.sbuf("conv_st", [128, L, 2, 2])
        self.shift_st = P.sbuf("shift_st", [128, L, 7])
        self.wkv_st = P.sbuf("wkv_st", [128, L, 2, 128])
        self.mstat = P.sbuf("mstat", [128, 4, 8])
        self.mat = P.sbuf("mat", [128, 40, 128])

    def prep(self):
        self.P.tag = 'prep'
        P = self.P
        d = self.di
        P.memset("pool", self.ones[:, :], 1.0)
        P.memset("pool", self.zeros[:, :], 0.0)
        P.affine_select(self.ident[:, :], self.ones[:, :], [[-1, 128]], ALU.is_equal, 0.0, 0, 1)
        P.affine_select(self.m_gt[:, :], self.ones[:, :], [[-1, 128]], ALU.is_gt, 0.0, 0, 1)
        P.affine_select(self.m_le[:, :], self.ones[:, :], [[1, 128]], ALU.is_ge, 0.0, 0, -1)
        P.affine_select(self.m_lt[:, :], self.ones[:, :], [[1, 128]], ALU.is_gt, 0.0, 0, -1)
        P.memset("pool", self.bones[:, :], 0.0)
        P.memset("pool", self.bones[0:64, 0:64], 1.0)
        P.memset("pool", self.bones[64:128, 64:128], 1.0)
        P.memset("pool", self.sel[:, :], 0.0)
        P.memset("pool", self.sel[0:64, 0:1], 1.0)
        P.memset("pool", self.sel[64:128, 1:2], 1.0)
        for t in (self.ssm_st, self.conv_st, self.shift_st, self.wkv_st):
            P.memset("dve", t[:], 0.0)
        P.memset("dve", self.RT[:], 0.0)
        self.rt_keys = []

        def rows(g, r0, name, n):
            src = self.din[name].ap()
            nd = len(src.shape)
            letters = "abcd"[:nd]
            flat = src.rearrange("%s -> (%s)" % (" ".join(letters), " ".join(letters)))
            src2 = flat.rearrange("(r c) -> r c", c=128)
            fk = ("rt", g, r0)
            self.rt_keys.append(fk)
            P.dma("sp", V(self.RT.h[r0:r0 + n, g, :], (fk,)), V(src2, self.RT.keys), "rt")

        rows(0, 0, "norm1_g", 32); rows(0, 32, "norm2_g", 32); rows(0, 64, "norm_f_g", 8)
        rows(0, 72, "conv_b", 8); rows(0, 80, "conv_w", 24); rows(0, 104, "tm_w0", 8)
        rows(0, 112, "tm_a0", 8); rows(0, 120, "tm_k_k", 8)
        rows(1, 0, "tm_k_a", 8); rows(1, 8, "tm_r_k", 8); rows(1, 16, "ssm_d", 8)
        rows(1, 24, "ssm_glu_b", 8); rows(1, 32, "tm_mu", 28)
        rows(1, 64, "ssm_a_re", 32); rows(1, 96, "ssm_a_im", 32)
        rows(2, 0, "ssm_log_dt", 32); rows(2, 32, "gm_bs", 16)
        P.seal("rt")
        for g in range(3):
            pp = self.ps()
            P.transpose(pp[:, 0:128], V(self.RT.h[:, g, :], self.RT.keys + tuple(self.rt_keys)), self.ident[:, :])
            P.copy("dve", self.PT[:, g, :], pp[:, 0:128])
        P.ts("dve", self.PT[:, 2, 48:56], self.PT[:, 1, 0:8], -1.0, ALU.mult, 1.0, ALU.add)
        for l in range(L):
            lr = self.lora.r()
            P.dma("pool", lr[0:32, l, :], V(self.dir_("tm_w2")[l], ()), "lora")
            P.dma("pool", lr[32:64, l, :], V(self.dir_("tm_a2")[l], ()), "lora")
            P.dma("pool", lr[64:128, l, :], V(self.dir_("tm_g2")[l], ()), "lora")
        P.seal("lora")
        sp = lambda i: self.SPt[:, i, :]
        a_re, a_im, ldt = self.PT[:, 1, 64:96], self.PT[:, 1, 96:128], self.PT[:, 2, 0:32]
        LAM, DT, MAG, TH, FS, FF, FR, SIN, COS, LBR, LBI, DEN, FRE, FIM, T1, T2 = range(16)
        self.I_LBR, self.I_LBI, self.I_MAG, self.I_FRE, self.I_FIM = LBR, LBI, MAG, FRE, FIM
        P.ts("dve", sp(LAM), a_re, -1e-4, ALU.min)
        P.act(sp(DT), ldt, AF.Exp)
        P.tt("dve", sp(T1), sp(LAM), sp(DT), ALU.mult)
        P.act(sp(MAG), sp(T1), AF.Exp)
        P.tt("dve", sp(TH), a_im, sp(DT), ALU.mult)
        for dst, off in ((SIN, 0.0), (COS, 0.25)):
            P.ts("dve", sp(FS), sp(TH), INV_2PI, ALU.mult, off, ALU.add)
            P.copy("dve", self.SPi[:, :], sp(FS))
            P.copy("dve", sp(FF), self.SPi[:, :])
            P.tt("dve", sp(FR), sp(FS), sp(FF), ALU.subtract)
            P.act(sp(dst), sp(FR), AF.Sin, scale=TWO_PI_SAFE)
        P.tt("dve", sp(LBR), sp(MAG), sp(COS), ALU.mult)
        P.tt("dve", sp(LBI), sp(MAG), sp(SIN), ALU.mult)
        P.tt("dve", sp(T1), sp(LAM), sp(LAM), ALU.mult)
        P.tt("dve", sp(T2), a_im, a_im, ALU.mult)
        P.tt("dve", sp(DEN), sp(T1), sp(T2), ALU.add)
        P.recip(sp(DEN), sp(DEN))
        LM1 = 16
        P.ts("dve", sp(LM1), sp(LBR), -1.0, ALU.add)
        P.tt("dve", sp(T1), sp(LM1), sp(LAM), ALU.mult)
        P.tt("dve", sp(T2), sp(LBI), a_im, ALU.mult)
        P.tt("dve", sp(T1), sp(T1), sp(T2), ALU.add)
        P.tt("dve", sp(FRE), sp(T1), sp(DEN), ALU.mult)
        P.tt("dve", sp(T1), sp(LBI), sp(LAM), ALU.mult)
        P.tt("dve", sp(T2), sp(LM1), a_im, ALU.mult)
        P.tt("dve", sp(T1), sp(T1), sp(T2), ALU.subtract)
        P.tt("dve", sp(FIM), sp(T1), sp(DEN), ALU.mult)
        etab = TT(self.scr["etab"], "etab")
        lcd = TT(self.scr["lcd"], "lcd")
        for l in range(self.nlayer):
            EC, ES, TA, TBs = 0, 8, 16, 20
            cs = self.SPt[:, COS, l * 8:(l + 1) * 8]
            sn = self.SPt[:, SIN, l * 8:(l + 1) * 8]
            P.copy("dve", self.arn(EC, 8, 0, 1), V(cs.ap.unsqueeze(2), cs.keys))
            P.copy("dve", self.arn(ES, 8, 0, 1), V(sn.ap.unsqueeze(2), sn.keys))
            n = 1
            while n < 512:
                cn = bc(self.arn(EC, 8, n - 1, n), [128, 8, n])
                snb = bc(self.arn(ES, 8, n - 1, n), [128, 8, n])
                ns_ = (n + 511) // 512
                def tmpv(base, n=n):
                    hh = self.arena.h[:, base:base + 4, :].rearrange("p a b -> p (a b)")[:, 0:8 * n]
                    return V(hh.rearrange("p (a b) -> p a b", a=8), tuple(("ar", j) for j in range(base, base + 4)))
                t1, t2 = tmpv(TA), tmpv(TBs)
                P.tt("dve", t1, self.arn(EC, 8, 0, n), cn, ALU.mult)
                P.tt("dve", t2, self.arn(ES, 8, 0, n), snb, ALU.mult)
                P.tt("dve", self.arn(EC, 8, n, 2 * n), t1, t2, ALU.subtract)
                P.tt("dve", t1, self.arn(ES, 8, 0, n), cn, ALU.mult)
                P.tt("dve", t2, self.arn(EC, 8, 0, n), snb, ALU.mult)
                P.tt("dve", self.arn(ES, 8, n, 2 * n), t1, t2, ALU.add)
                n *= 2
            P.dma("sp", V(etab.h[l].rearrange("j p c -> p j c")[:, :, 0:512], ("etab",)), self.arn(EC, 8), "etab_w")
            P.dma("sp", V(etab.h[l].rearrange("j p c -> p j c")[:, :, 512:1024], ("etab",)), self.arn(ES, 8), "etab_w")
            WS = 0
            wsv = self.ms(WS)
            P.dma("sp", V(wsv.ap.rearrange("p (h s) -> p h s", h=4), wsv.keys),
                  V(d("gm_ws")[l].rearrange("h t s -> t h s"), ()), "prep_ws")
            for h in range(4):
                pp = self.ps()
                P.transpose(pp[:, 0:128], self.ms(WS, h * 128, (h + 1) * 128), self.ident[:, :])
                P.tt("dve", self.STG[:, h * 128:(h + 1) * 128], pp[:, 0:128], self.m_le[:, :], ALU.mult)
            CN = 1
            P.memset("pool", self.ms(CN), 0.0)
            cn_keys = []
            for ri, nm in enumerate(("ssm_c_re", "ssm_c_im")):
                for g in range(16):
                    half, gl = divmod(g, 8)
                    c0 = (half * 2 + ri) * 128 + (g % 2) * 64
                    fk = ("cn", l, ri, g)
                    cn_keys.append(fk)
                    P.dma("sp", V(self.ms(CN, c0, c0 + 64, gl * 16, gl * 16 + 16).ap, (fk,)),
                          V(d(nm)[l, g], self.ms(CN).keys), "prep_c")
            for half in range(2):
                for ri in range(2):
                    c0 = (half * 2 + ri) * 128
                    pp = self.ps()
                    src = self.ms(CN, c0, c0 + 128)
                    P.transpose(pp[:, 0:128], V(src.ap, src.keys + tuple(cn_keys)), self.ident[:, :])
                    if ri == 0:
                        P.copy("act", self.STG[:, 512 + c0:512 + c0 + 128], pp[:, 0:128])
                    else:
                        P.act(self.STG[:, 512 + c0:512 + c0 + 128], pp[:, 0:128], AF.Copy, scale=-1.0)
            BN = 2
            P.memset("pool", self.ms(BN), 0.0)
            bn_keys = []
            for ri, nm in enumerate(("ssm_b_re", "ssm_b_im")):
                for g in range(16):
                    half, gl = divmod(g, 8)
                    c0 = ri * 256 + half * 128 + gl * 16
                    p0 = (g % 2) * 64
                    fk = ("bn", l, ri, g)
                    bn_keys.append(fk)
                    P.dma("sp", V(self.ms(BN, c0, c0 + 16, p0, p0 + 64).ap, (fk,)),
                          V(d(nm)[l, g], self.ms(BN).keys), "prep_b")
            BO = 3
            def v8(slot, c0):
                vv = self.ms(slot, c0, c0 + 256)
                return V(vv.ap.rearrange("p (a b) -> p a b", a=8), vv.keys + (tuple(bn_keys) if slot == BN else ()))
            fre = self.SPt[:, FRE, l * 8:(l + 1) * 8]
            fim = self.SPt[:, FIM, l * 8:(l + 1) * 8]
            freb = V(fre.ap.unsqueeze(2).to_broadcast([128, 8, 32]), fre.keys)
            fimb = V(fim.ap.unsqueeze(2).to_broadcast([128, 8, 32]), fim.keys)
            t1 = v8(WS, 0); t2 = v8(WS, 256)
            P.tt("dve", t1, v8(BN, 0), freb, ALU.mult)
            P.tt("dve", t2, v8(BN, 256), fimb, ALU.mult)
            P.tt("dve", v8(BO, 0), t1, t2, ALU.subtract)
            P.tt("dve", t1, v8(BN, 0), fimb, ALU.mult)
            P.tt("dve", t2, v8(BN, 256), freb, ALU.mult)
            P.tt("dve", v8(BO, 256), t1, t2, ALU.add)
            for half in range(2):
                for ri in range(2):
                    pp = self.ps()
                    c0 = ri * 256 + half * 128
                    P.transpose(pp[:, 0:128], self.ms(BO, c0, c0 + 128), self.ident[:, :])
                    o0 = 1024 + (half * 2 + ri) * 128
                    P.copy("act", self.STG[:, o0:o0 + 128], pp[:, 0:128])
            P.dma("sp", V(lcd.h[l], ("lcd",)), self.STG[:, :], "lcd_w")

    def wload(self, src, a, b):
        buf = self.WB[self.wbi % NWB]
        self.wbi += 1
        view = buf.h.bitcast(F32R)[:, 0:a * b].rearrange("p (a b) -> p a b", a=a)
        self.P.dma("pool", V(view, buf.keys), V(src, ()), "ld_" + buf.name)
        return TT(view, buf.name, buf.keys)

    def main(self):
        P = self.P
        self.stages = []
        for tb in range(self.nblk):
            self.stages.append((None, lambda w, tb=tb: self.load_x(tb)))
            for l in range(self.nlayer):
                self.layer(tb, l)
            self.stages.append((None, lambda w, tb=tb: self.final_norm(tb)))
        if self.do_sample:
            self.sample()
        st = self.stages if self.max_stages is None else self.stages[:self.max_stages]
        views = [None] * len(st)
        nxt = 0

        def advance():
            nonlocal nxt
            while nxt < len(st):
                i = nxt
                nxt += 1
                if st[i][0] is not None:
                    views[i] = st[i][0]()
                    return

        for _ in range(NWB - 1):
            advance()
        for i in range(len(st)):
            if st[i][0] is not None:
                advance()
            st[i][1](views[i])

    def load_x(self, tb):
        self.P.tag = 'load_x'
        src = self.di("x_prompt")[tb * TB:(tb + 1) * TB, :].rearrange("(s p) c -> p s c", p=128)
        self.P.dma("sp", self.xb[:, :, :], V(src, ()), "xb_ld")

    def norm_to_hT(self, gbase):
        self.P.tag = 'norm'
        P = self.P
        self.ps_n = 8
        hview = self.arena.h[:, 0:8, :].rearrange("p (s a) b -> p s (a b)", s=4)
        hkeys = tuple(("ar", j) for j in range(8))
        banks = [self.ps() for _ in range(8)]
        for sub in range(4):
            hv = V(hview[:, sub, :], hkeys[2 * sub:2 * sub + 2])
            P.act(hv, self.xb[:, sub, :], AF.Square, accum_out=self.small[:, sub:sub + 1])
            P.act(self.small[:, 4 + sub:5 + sub], self.small[:, sub:sub + 1], AF.Sqrt, scale=1.0 / D, bias=1e-6)
            P.recip(self.small[:, 8 + sub:9 + sub], self.small[:, 4 + sub:5 + sub])
            P.ts("dve", hv, self.xb[:, sub, :], self.small[:, 8 + sub:9 + sub], ALU.mult)
            for kc in range(8):
                hk = V(hview[:, sub, kc * 128:(kc + 1) * 128], hkeys[2 * sub:2 * sub + 2])
                P.transpose(banks[kc][:, sub * 128:(sub + 1) * 128], hk, self.ident[:, :])
        for kc in range(8):
            P.act(self.hs(kc), banks[kc][:, :], AF.Identity, scale=self.PT[:, 0, gbase + kc:gbase + kc + 1])

    def final_norm(self, tb):
        self.P.tag = 'final_norm'
        self.ps_n = 4
        P = self.P
        for sub in range(4):
            hv = self.arn(10 + 2 * sub, 2)
            hv = V(hv.ap.rearrange("p a b -> p (a b)"), hv.keys)
            P.act(hv, self.xb[:, sub, :], AF.Square, accum_out=self.small[:, sub:sub + 1])
            P.act(self.small[:, 4 + sub:5 + sub], self.small[:, sub:sub + 1], AF.Sqrt, scale=1.0 / D, bias=1e-6)
            P.recip(self.small[:, 8 + sub:9 + sub], self.small[:, 4 + sub:5 + sub])
            P.ts("dve", hv, self.xb[:, sub, :], self.small[:, 8 + sub:9 + sub], ALU.mult)
            if sub == 0:
                gfb = self.arn(18, 2)
                gfb = V(gfb.ap.rearrange("p a b -> p (a b)"), gfb.keys)
                P.dma("sp", gfb, V(self.di("norm_f_g").partition_broadcast(128), ()), "gfb")
            P.tt("dve", hv, hv, gfb, ALU.mult)
            r0 = tb * TB + sub * 128
            P.dma("sp", V(self.dout["y_p"].ap()[r0:r0 + 128, :], ("y_p",)), hv, "st_y%d" % sub)

    def load_consts(self, l):
        P = self.P
        tag = P.tag
        P.tag = 'pre'
        lcd = self.scr["lcd"].ap()
        P.dma("sp", self.LC[:, 0:1024], V(lcd[l][:, 0:1024], ("lcd",)), "lc_a")
        P.dma("pool", self.LCR[:, 1024:1536], V(self.scr["lcd"].bitcast(F32R).ap()[l][:, 1024:1536], ("lcd",)), "lc_b")
        P.dma("pool", V(self.LCR.h[:, 1536:2048].rearrange("p (k c) -> p k c", k=2), self.LC.keys),
              V(self.dir_("ssm_glu_w")[l].rearrange("(k p) c -> p k c", p=128), ()), "lc_b")
        for i, nm in enumerate(("gm_ln_g", "gm_ln_b", "tm_ln_g", "tm_ln_b")):
            P.dma("sp", self.BCP[:, i, :], V(self.di(nm)[l].partition_broadcast(128), ()), "bcp")
        P.tag = tag

    def layer(self, tb, l):
        P = self.P
        last = tb == self.nblk - 1

        class _S:
            def append(_, item, stages=self.stages):
                ld, comp = item

                def comp2(w, comp=comp):
                    P.suffix = '@%d.%d' % (tb, l)
                    comp(w)
                stages.append((ld, comp2))
        S = _S()

        def pre(w):
            P.tag = 'pre'
            self.ps_n = 8
            if tb == 0 and l == 0:
                self.load_consts(0)
            self.norm_to_hT(l * 8)

        S.append((None, pre))
        tiles = [[(0, 512)], [(1536, 512)], [(2048, 384)], [(768, 512)], [(1280, 256), (512, 256)]]
        for i, segs in enumerate(tiles):
            ncol = sum(n for _, n in segs)

            def ld(segs=segs, ncol=ncol):
                buf = self.WB[self.wbi % NWB]
                self.wbi += 1
                view = buf.h.bitcast(F32R)[:, 0:8 * ncol].rearrange("p (k c) -> p k c", k=8)
                off = 0
                for c0, n in segs:
                    src = self.dir_("w_in")[l][:, c0:c0 + n].rearrange("(k p) c -> p k c", p=128)
                    P.dma("pool", V(view[:, :, off:off + n], buf.keys), V(src, ()), "ld_" + buf.name)
                    off += n
                return TT(view, buf.name, buf.keys)

            def comp(w, i=i, segs=segs, ncol=ncol):
                P.tag = 'w_in'
                if i == 0:
                    for sub in range(6):
                        if sub < 4:
                            P.tag = 'w_in'
                            pp = self.ps()
                            for kc in range(8):
                                P.mm(pp[:, :], self.hs(kc, sub * 128, (sub + 1) * 128), w[:, kc, :], kc == 0, kc == 7)
                            self.gmlp_1(l, sub, pp)
                        if 1 <= sub < 5:
                            self.gmlp_2(l, sub - 1)
                        if 2 <= sub:
                            self.gmlp_3(l, sub - 2)
                else:
                    qs = []
                    for c0, n in segs:
                        qs += [(c0 + g * 128 - 512) // 128 for g in range(n // 128)]
                    pend = []
                    for cg, q in enumerate(qs):
                        P.tag = 'w_in'
                        pp = self.ps()
                        for kc in range(8):
                            P.mm(pp[:, :], w[:, kc, cg * 128:(cg + 1) * 128], self.hs(kc), kc == 0, kc == 7)
                        if q < 2:
                            pend.append((q, pp))
                        else:
                            self.consume_fm(tb, l, q, pp)
                    for q, pp in pend:
                        self.consume_fm(tb, l, q, pp)
                    if i == 2:
                        self.rwkv(tb, l, 0)
                        self.ps_n = 8
                    if i == 4:
                        self.rwkv(tb, l, 1)
                        nl = l + 1 if l + 1 < self.nlayer else 0
                        if not (tb == self.nblk - 1 and l == self.nlayer - 1):
                            self.load_consts(nl)

            S.append((ld, comp))
        for half in range(2):
            def ld(half=half):
                return self.wload(self.dir_("w_out")[l][:, half * 512:(half + 1) * 512].rearrange("(k p) c -> p k c", p=128), 8, 512)

            def comp(w, half=half):
                P.tag = 'w_out'
                yr = self.yT.r()
                for sub in range(4):
                    for kc in range(8):
                        P.mm(self.PSL[sub][:, :], yr[:, kc, sub * 128:(sub + 1) * 128], w[:, kc, :], kc == 0, kc == 7)
                    xs = self.xb[:, sub, half * 512:(half + 1) * 512]
                    P.tt("dve", xs, xs, self.PSL[sub][:, :], ALU.add)
                if half == 1:
                    self.norm_to_hT(32 + l * 8)

            S.append((ld, comp))
        for g in range(5):
            for which in range(2):
                def ld(g=g, which=which):
                    c0 = which * DFF + 512 * g
                    return self.wload(self.dir_("ffn_w_gu")[l][:, c0:c0 + 512].rearrange("(k p) c -> p k c", p=128), 8, 512)

                def comp(w, g=g, which=which):
                    P.tag = 'ffn_gu'
                    self.ps_n = 8
                    for fo in range(4):
                        j = 4 * g + fo
                        pz = self.ps()
                        for kc in range(8):
                            P.mm(pz[:, :], w[:, kc, fo * 128:(fo + 1) * 128], self.hs(kc), kc == 0, kc == 7)
                        if which == 0:
                            P.act(self.ar(j), pz[:, :], AF.Silu)
                        else:
                            P.tt("dve", self.ar(j, r=True), self.ar(j), pz[:, :], ALU.mult)

                S.append((ld, comp))

        def ld_last():
            buf = self.WB[self.wbi % NWB]
            self.wbi += 1
            view = buf.h.bitcast(F32R)[:, 0:4096].rearrange("p (k g c) -> p k g c", k=8, g=2)
            for g in range(2):
                c0 = g * DFF + 2560
                src = self.dir_("ffn_w_gu")[l][:, c0:c0 + 256].rearrange("(k p) c -> p k c", p=128)
                P.dma("pool", V(view[:, :, g, :], buf.keys), V(src, ()), "ld_" + buf.name)
            return TT(view, buf.name, buf.keys)

        def comp_last(w):
            P.tag = 'ffn_gu'
            self.ps_n = 8
            for fo in range(2):
                j = 20 + fo
                pg = self.ps()
                pu = self.ps()
                for kc in range(8):
                    P.mm(pg[:, :], w[:, kc, 0, fo * 128:(fo + 1) * 128], self.hs(kc), kc == 0, kc == 7)
                for kc in range(8):
                    P.mm(pu[:, :], w[:, kc, 1, fo * 128:(fo + 1) * 128], self.hs(kc), kc == 0, kc == 7)
                tmp = self.ar(22 + (j % 2))
                P.act(tmp, pg[:, :], AF.Silu)
                P.tt("dve", self.ar(j, r=True), tmp, pu[:, :], ALU.mult)

        S.append((ld_last, comp_last))
        for half in range(2):
            for jg in range(3):
                nj = 8 if jg < 2 else 6

                def ld(half=half, jg=jg, nj=nj):
                    src = self.dir_("ffn_w_down")[l][jg * 1024:jg * 1024 + nj * 128, half * 512:(half + 1) * 512]
                    return self.wload(src.rearrange("(j p) c -> p j c", p=128), nj, 512)

                def comp(w, half=half, jg=jg, nj=nj):
                    P.tag = 'ffn_down'
                    self.ps_n = 4
                    for jj in range(nj):
                        j = jg * 8 + jj
                        for sub in range(4):
                            P.mm(self.PSL[sub][:, :], self.ar(j, sub * 128, (sub + 1) * 128, r=True), w[:, jj, :],
                                 j == 0, j == 21)
                    if jg == 2:
                        for sub in range(4):
                            xs = self.xb[:, sub, half * 512:(half + 1) * 512]
                            P.tt("dve", xs, xs, self.PSL[sub][:, :], ALU.add)

                S.append((ld, comp))
        if last:
            S.append((None, lambda w: self.prompt_state_out(l)))

    def gmlp_1(self, l, sub, pp):
        self.P.tag = 'gmlp'
        P = self.P
        sl = 20 + sub
        u = self.ar(sl, 0, 256)
        vf = self.ar(sl, 256, 512)
        P.act(u, pp[:, 0:256], AF.Gelu_apprx_tanh)
        P.act(vf, pp[:, 256:512], AF.Gelu_apprx_tanh)
        ms = self.mstat
        P.bn_stats(ms[:, sub, 0:6], vf)
        P.bn_aggr(ms[:, sub, 6:8], ms[:, sub, 0:6])
        P.act(self.small[:, 16 + sub:17 + sub], ms[:, sub, 7:8], AF.Sqrt, bias=1e-5)
        P.recip(self.small[:, 20 + sub:21 + sub], self.small[:, 16 + sub:17 + sub])
        P.ts("dve", vf, vf, ms[:, sub, 6:7], ALU.subtract, self.small[:, 20 + sub:21 + sub], ALU.mult)
        P.tt("dve", vf, vf, self.BCP[:, 0, :], ALU.mult)
        P.tt("dve", vf, vf, self.BCP[:, 1, :], ALU.add)

    def gmlp_2(self, l, sub):
        self.P.tag = 'gmlp'
        P = self.P
        sl = 20 + sub
        p2 = self.ps()
        for h in range(4):
            P.mm(p2[:, h * 64:(h + 1) * 64], self.LC[:, h * 128:(h + 1) * 128], self.ar(sl, 256 + h * 64, 256 + (h + 1) * 64))
        for h in range(4):
            uh = self.ar(sl, h * 64, (h + 1) * 64)
            P.stt("dve", uh, p2[:, h * 64:(h + 1) * 64], self.PT[:, 2, 32 + l * 4 + h:33 + l * 4 + h], uh, ALU.add, ALU.mult)

    def gmlp_3(self, l, sub):
        self.P.tag = 'gmlp'
        P = self.P
        sl = 20 + sub
        p3 = self.ps()
        for t2 in range(2):
            P.transpose(p3[:, t2 * 128:(t2 + 1) * 128], self.ar(sl, t2 * 128, (t2 + 1) * 128), self.ident[:, :])
        yr = self.yT.r()
        for t2 in range(2):
            P.copy("act", yr[:, t2, sub * 128:(sub + 1) * 128], p3[:, t2 * 128:(t2 + 1) * 128])

    def consume_fm(self, tb, l, q, pp):
        self.P.tag = 'evac_fm'
        P = self.P
        if q < 2:
            P.act(self.hs(q), pp[:, :], AF.Copy)
            P.copy("dve", self.hs(4 + q, r=False), pp[:, :])
            P.act(self.hs(2 + q, p0=64, p1=128), pp[64:128, :], AF.Copy)
            P.copy("dve", self.hs(2 + q, p0=64, p1=96), self.zeros[64:96, :])
        elif q < 4:
            P.copy("act", self.ar(8 + q - 2), pp[:, :])
        elif q < 6:
            P.copy("act", self.ar(10 + q - 4), pp[:, :])
        elif q < 8:
            t = q - 6
            P.tt("dve", self.ar(12 + t), pp[:, :], self.ar(8 + t), ALU.mult)
            self.conv(tb, l, t)
        else:
            t = q - 8
            zd = 8 + t
            P.copy("act", self.ar(zd), pp[:, :])
            P.tt("dve", self.ar(15, 1, 512), self.ar(zd, 0, 511), self.ar(zd, 1, 512), ALU.subtract)
            P.tt("dve", self.ar(15, 0, 1), self.shift_st[:, l, t:t + 1], self.ar(zd, 0, 1), ALU.subtract)
            mu = self.PT[:, 1, 32 + l * 7 + t:33 + l * 7 + t]
            P.stt("dve", self.ar(t), self.ar(15), mu, self.ar(zd), ALU.mult, ALU.add)
            P.copy("act", self.shift_st[:, l, t:t + 1], self.ar(zd, 511, 512))

    def conv(self, tb, l, t):
        self.P.tag = 'conv'
        P = self.P
        z = 12 + t
        acc = 8 + t
        w = lambda j: self.PT[:, 0, 80 + l * 6 + j * 2 + t:81 + l * 6 + j * 2 + t]
        cb = self.PT[:, 0, 72 + l * 2 + t:73 + l * 2 + t]
        cst = lambda a, b: self.conv_st[:, l, t, a:b]
        P.ts("dve", self.ar(acc), self.ar(z), w(2), ALU.mult, cb, ALU.add)
        P.stt("dve", self.ar(acc, 1, 512), self.ar(z, 0, 511), w(1), self.ar(acc, 1, 512), ALU.mult, ALU.add)
        P.stt("dve", self.ar(acc, 0, 1), cst(1, 2), w(1), self.ar(acc, 0, 1), ALU.mult, ALU.add)
        P.stt("dve", self.ar(acc, 2, 512), self.ar(z, 0, 510), w(0), self.ar(acc, 2, 512), ALU.mult, ALU.add)
        P.stt("dve", self.ar(acc, 0, 2), cst(0, 2), w(0), self.ar(acc, 0, 2), ALU.mult, ALU.add)
        P.tt("dve", self.yT.r()[:, 4 + t, :], self.ar(acc), self.ar(10 + t), ALU.mult)
        P.copy("act", cst(0, 2), self.ar(z, 510, 512))

    def s5_ctpad(self, l, half):
        P = self.P
        P.tag = 's5'
        for ri in range(2):
            sl = 12 + ri
            P.copy("dve", self.ar(sl, r=True), self.zeros[:, :])
            for jl in range(4):
                c = jl * 128 + jl * 32
                b0 = 512 + (half * 2 + ri) * 128 + jl * 32
                P.copy("act", self.ar(sl, c, c + 32, r=True), self.LC[:, b0:b0 + 32])

    def s5_a(self, l, j):
        P = self.P
        P.tag = 's5'
        LCr = self.LCR
        etab = self.scr["etab"].ap()
        half, jl = divmod(j, 4)
        pr = self.ps()
        pi = self.ps()
        rows = slice(32 * jl, 32 * jl + 32)
        for ri, pz in ((0, pr), (1, pi)):
            c0 = 1024 + (half * 2 + ri) * 128
            if jl < 3:
                P.mm(pz[:, :], V(LCr.h[rows, c0:c0 + 128], LCr.keys), self.hs(half, 0, 512, 32 * jl, 32 * jl + 32))
            else:
                P.mm(pz[:, :], LCr[64:128, c0:c0 + 128], self.hs(2 + half, 0, 512, 64, 128))
        P.copy("act", self.ar(2), pr[:, :])
        P.copy("act", self.ar(3), pi[:, :])
        yield
        etv = self.arn(0, 2)
        P.dma("sp", V(etv.ap.rearrange("p a b -> p (a b)"), etv.keys), V(etab[l, j], ("etab",)), "et0")
        yield
        Ec, Es = self.ar(0), self.ar(1)
        A, B, C, Dd = self.ar(2), self.ar(3), self.ar(8), self.ar(9)
        P.tt("dve", C, B, Ec, ALU.mult)
        yield
        P.tt("dve", Dd, A, Es, ALU.mult)
        yield
        P.tt("dve", A, A, Ec, ALU.mult)
        yield
        P.tt("dve", B, B, Es, ALU.mult)
        yield
        P.tt("dve", A, A, B, ALU.add)
        yield
        P.tt("dve", C, C, Dd, ALU.subtract)
        yield

    def s5_b(self, l, j):
        P = self.P
        P.tag = 's5'
        half, jl = divmod(j, 4)
        Ec, Es = self.ar(0), self.ar(1)
        A, B, C, Dd = self.ar(2), self.ar(3), self.ar(8), self.ar(9)
        rho = bc(self.SPt[:, self.I_MAG, l * 8 + j:l * 8 + j + 1], [128, 512])
        P.scan(B, rho, A, self.ssm_st[:, l, j, 0:1], ALU.mult, ALU.add)
        yield
        P.scan(Dd, rho, C, self.ssm_st[:, l, j, 1:2], ALU.mult, ALU.add)
        yield
        P.tt("dve", A, B, Ec, ALU.mult)
        yield
        P.tt("dve", C, Dd, Es, ALU.mult)
        yield
        P.tt("dve", self.ar(10, r=True), A, C, ALU.subtract)
        yield
        P.tt("dve", self.ssm_st[:, l, j, 0:1], self.ar(2, 511, 512), self.ar(8, 511, 512), ALU.subtract)
        yield
        P.tt("dve", A, Dd, Ec, ALU.mult)
        yield
        P.tt("dve", C, B, Es, ALU.mult)
        yield
        P.tt("dve", self.ar(11, r=True), A, C, ALU.add)
        yield
        P.tt("dve", self.ssm_st[:, l, j, 1:2], self.ar(2, 511, 512), self.ar(8, 511, 512), ALU.add)
        yield

    def s5_c(self, l, j):
        P = self.P
        P.tag = 's5'
        half, jl = divmod(j, 4)
        if jl == 0:
            self.s5_ctpad(l, half)
        pc = self.ps()
        for ri in range(2):
            P.mm(pc[:, :], self.ar(12 + ri, jl * 128, (jl + 1) * 128, r=True), self.ar(10 + ri, r=True), ri == 0, ri == 1)
        if jl == 0:
            P.copy("act", self.ar(14 + half), pc[:, :])
            yield
        else:
            P.tt("dve", self.ar(14 + half), self.ar(14 + half), pc[:, :], ALU.add)
            yield

    def s5_tail(self, l):
        P = self.P
        P.tag = 's5'
        LCr = self.LCR
        for half in range(2):
            dcol = self.PT[:, 1, 16 + l * 2 + half:17 + l * 2 + half]
            P.stt("dve", self.ar(8 + half), self.hs(4 + half, r=False), dcol, self.ar(14 + half), ALU.mult, ALU.add)
            P.act(self.ar(8 + half), self.ar(8 + half), AF.Gelu_apprx_tanh)
            P.copy("act", self.ar(2 + half, r=True), self.ar(8 + half))
        for t in range(2):
            pg = self.ps()
            for kc in range(2):
                c0 = 1536 + kc * 256 + t * 128
                P.mm(pg[:, :], LCr[:, c0:c0 + 128], self.ar(2 + kc, r=True), kc == 0, kc == 1)
            gb = self.PT[:, 1, 24 + l * 2 + t:25 + l * 2 + t]
            P.act(self.ar(10 + t), pg[:, :], AF.Sigmoid, bias=gb)
            P.tt("dve", self.yT.r()[:, 2 + t, :], self.ar(8 + t), self.ar(10 + t), ALU.mult)

    def rwkv(self, tb, l, phase):
        self.P.tag = 'rwkv_prep'
        self.ps_n = 4
        P = self.P
        lr = self.lora.r()
        LX = 7
        if phase == 0:
            P.act(self.ar(LX, p0=0, p1=32, r=True), self.ar(6, p0=0, p1=32), AF.Tanh)
            P.act(self.ar(LX, p0=32, p1=64, r=True), self.ar(6, p0=32, p1=64), AF.Copy)
            P.act(self.ar(LX, p0=64, p1=128, r=True), self.ar(6, p0=64, p1=128), AF.Sigmoid)
        LD, AA, KK, KF, BV, CL, GI, GP, TMP = 8, 9, 10, 11, 12, 13, 14, 15, 15

        def prep_pair(hp):
            P.tag = 'rwkv_prep'
            AT, RT_, KH, BH, G, RK = (16 + 6 * hp + k for k in range(6))
            r_, k_, v_ = self.ar(hp), self.ar(2 + hp), self.ar(4 + hp)
            col = lambda g, base: self.PT[:, g, base + l * 2 + hp:base + l * 2 + hp + 1]
            pq = self.ps()
            P.mm(pq[:, :], lr[0:32, l, hp * 128:(hp + 1) * 128], self.ar(LX, p0=0, p1=32, r=True))
            P.act(self.ar(LD), pq[:, :], AF.Sigmoid, bias=col(0, 104))
            P.act(self.ar(LD), self.ar(LD), AF.Copy, scale=-0.6065306597126334)
            pa = self.ps()
            P.mm(pa[:, :], lr[32:64, l, hp * 128:(hp + 1) * 128], self.ar(LX, p0=32, p1=64, r=True))
            P.act(self.ar(AA), pa[:, :], AF.Sigmoid, bias=col(0, 112))
            P.act(self.ar(KK), k_, AF.Identity, scale=col(0, 120))
            P.act(self.ar(TMP), self.ar(KK), AF.Square)
            pn = self.ps()
            P.mm(pn[:, :], self.bones[:, :], self.ar(TMP))
            P.act(self.ar(TMP), pn[:, :], AF.Sqrt)
            P.ts("dve", self.ar(TMP), self.ar(TMP), 1e-12, ALU.max)
            P.recip(self.ar(TMP), self.ar(TMP))
            P.tt("dve", self.ar(KK), self.ar(KK), self.ar(TMP), ALU.mult)
            P.act(self.ar(TMP), self.ar(AA), AF.Identity, scale=col(1, 0), bias=col(2, 48))
            P.tt("dve", self.ar(KF), k_, self.ar(TMP), ALU.mult)
            P.tt("dve", self.ar(BV), self.ar(KK), self.ar(AA), ALU.mult)
            P.tt("dve", self.ar(TMP), r_, self.ar(KF), ALU.mult)
            P.act(self.ar(RK), self.ar(TMP), AF.Identity, scale=col(1, 8))
            for c in range(4):
                cs = (c * 128, (c + 1) * 128)
                P.scan(self.ar(CL, *cs), self.ones[:, :], self.ar(LD, *cs), 0.0, ALU.mult, ALU.add)
            P.act(self.ar(G), self.ar(CL), AF.Exp)
            P.act(self.ar(GI), self.ar(CL), AF.Exp, scale=-1.0)
            P.tt("dve", self.ar(GP), self.ar(CL), self.ar(LD), ALU.subtract)
            P.act(self.ar(GP), self.ar(GP), AF.Exp)
            P.stt("dve", self.ar(AT), self.ar(KK), -1.0, self.ar(GP), ALU.mult, ALU.mult)
            P.tt("dve", self.ar(RT_), r_, self.ar(G), ALU.mult)
            P.tt("dve", self.ar(KH), self.ar(KF), self.ar(GI), ALU.mult)
            P.tt("dve", self.ar(BH), self.ar(BV), self.ar(GI), ALU.mult)

        def post_a(c):
            P.tag = 'rwkv_post'
            cs = (c * 128, (c + 1) * 128)
            O = self.PSL[c]
            ms = self.mstat
            for h in range(4):
                P.bn_stats(ms[:, h, 0:6], O[:, h * 64:(h + 1) * 64])
                P.bn_aggr(ms[:, h, 6:8], ms[:, h, 0:6])
            P.act(self.small[:, 24:28], V(ms.h[:, :, 7], ms.keys), AF.Sqrt, bias=64e-5)
            P.recip(self.small[:, 28:32], self.small[:, 24:28])
            ON = 6 + (c % 2)
            on = lambda a, b: self.hs(ON, a, b, r=False)
            for h in range(4):
                P.ts("dve", on(h * 64, (h + 1) * 64), O[:, h * 64:(h + 1) * 64], ms[:, h, 6:7], ALU.subtract,
                     self.small[:, 28 + h:29 + h], ALU.mult)
            P.tt("dve", on(0, 256), on(0, 256), self.BCP[:, 2, :], ALU.mult)
            P.tt("dve", on(0, 256), on(0, 256), self.BCP[:, 3, :], ALU.add)
            pb = self.ps()
            for hp in range(2):
                P.mm(pb[:, hp * 2:(hp + 1) * 2], self.ar(16 + 6 * hp + 5, *cs), self.sel[:, :])
            P.copy("act", self.small[:, 32:36], pb[:, 0:4])
            pv = self.ps()
            for hp in range(2):
                P.transpose(pv[:, hp * 128:(hp + 1) * 128], self.ar(4 + hp, *cs), self.ident[:, :])
            for h in range(4):
                P.stt("dve", on(256 + h * 64, 256 + (h + 1) * 64), pv[:, h * 64:(h + 1) * 64],
                      self.small[:, 32 + h:33 + h], on(h * 64, (h + 1) * 64), ALU.mult, ALU.add)
            pg = self.ps()
            P.mm(pg[:, 0:256], self.ar(LX, cs[0], cs[1], 64, 128, r=True), lr[64:128, l, :])
            P.tt("dve", on(0, 256), on(256, 512), pg[:, 0:256], ALU.mult)

        def post_b(c):
            P.tag = 'rwkv_post'
            cs = (c * 128, (c + 1) * 128)
            ON = 6 + (c % 2)
            pt = self.ps()
            for t2 in range(2):
                P.transpose(pt[:, t2 * 128:(t2 + 1) * 128], self.hs(ON, t2 * 128, (t2 + 1) * 128, r=False), self.ident[:, :])
            yr = self.yT.r()
            for t2 in range(2):
                P.copy("act", yr[:, 6 + t2, cs[0]:cs[1]], pt[:, t2 * 128:(t2 + 1) * 128])

        seq = [(hp, c) for hp in range(2) for c in range(4)]
        after = {5: [lambda: post_a(0)], 6: [lambda: post_b(0), lambda: post_a(1)],
                 7: [lambda: post_b(1), lambda: post_a(2), lambda: post_a(3), lambda: post_b(2), lambda: post_b(3)]}
        if phase == 0:
            prep_pair(0)
            prep_pair(1)
            return
        import itertools

        def drain(g, k=None):
            cnt = 0
            for _ in g:
                cnt += 1
                if k is not None and cnt >= k:
                    break

        drain(self.s5_a(l, 0))
        self.rwkv_front(l, 0, 0, 0)
        for n, (hp, c) in enumerate(seq):
            tails = self.rwkv_tail(l, hp, c, n % 2)
            gens = []
            if n >= 1:
                gens.append(self.s5_c(l, n - 1))
            gens.append(self.s5_b(l, n))
            if n + 1 < 8:
                gens.append(self.s5_a(l, n + 1))
            g = itertools.chain(*gens)
            if n + 1 < len(seq):
                hp2, c2 = seq[n + 1]

                def hook(lev, tails=tails, g=g):
                    if 1 <= lev <= 3:
                        tails[lev - 1]()
                    drain(g, {0: 4, 1: 2, 2: 2, 3: 2}.get(lev, 4))
                self.rwkv_front(l, hp2, c2, (n + 1) % 2, hook)
                drain(g)
            else:
                for t in tails:
                    t()
                drain(g)
                drain(self.s5_c(l, n))
            for f in after.get(n, []):
                f()
        self.s5_tail(l)

    def rwkv_front(self, l, hp, c, st, hook=None):
        self.P.tag = 'rwkv_chunk'
        P = self.P
        AT, RT_, KH, BH, G, RK = (16 + 6 * hp + k for k in range(6))
        VS = 4 + hp
        cs = (c * 128, (c + 1) * 128)
        VT = 4 * st
        pp = self.ps()
        P.transpose(pp[:, 0:128], self.ar(VS, *cs), self.ident[:, :])
        P.transpose(pp[:, 128:256], self.ar(KH, *cs), self.ident[:, :])
        P.transpose(pp[:, 256:384], self.ar(BH, *cs), self.ident[:, :])
        P.copy("act", self.mq(VT, n=3), pp[:, 0:384])
        H = []
        for h2 in range(2):
            p0, p1 = h2 * 64, h2 * 64 + 64
            d = dict(zip(("AKT", "RKT", "RBT", "Z"), (8 + st * 8 + h2 * 4 + k for k in range(4))))
            d.update(zip(("L1", "U1", "La", "Lb", "Ua", "Ub"), (24 + h2 * 6 + k for k in range(6))))
            d.update(p0=p0, p1=p1, at=self.ar(AT, cs[0], cs[1], p0, p1), rt=self.ar(RT_, cs[0], cs[1], p0, p1),
                     kh=self.ar(KH, cs[0], cs[1], p0, p1), bh=self.ar(BH, cs[0], cs[1], p0, p1))
            H.append(d)
        if hook is not None:
            hook(0)
            self.P.tag = 'rwkv_chunk'
        banks = []
        for d in H:
            pA = self.ps()
            pB = self.ps()
            P.mm(pA[:, 0:128], d["at"], d["bh"])
            P.mm(pA[:, 128:256], d["bh"], d["at"])
            P.mm(pA[:, 256:384], d["kh"], d["at"])
            P.mm(pA[:, 384:512], d["kh"], d["rt"])
            P.mm(pB[:, 0:128], d["bh"], d["rt"])
            banks.append((pA, pB))
        for d, (pA, pB) in zip(H, banks):
            P.tt("dve", self.mq(d["U1"]), pA[:, 128:256], self.m_lt[:, :], ALU.mult)
            P.tt("dve", self.mq(d["L1"]), pA[:, 0:128], self.m_gt[:, :], ALU.mult)
            P.tt("pool", self.mq(d["Z"]), self.mq(d["U1"]), self.ident[:, :], ALU.add)
        for d, (pA, pB) in zip(H, banks):
            P.tt("dve", self.mq(d["AKT"]), pA[:, 256:384], self.m_lt[:, :], ALU.mult)
            P.tt("dve", self.mq(d["RKT"]), pA[:, 384:512], self.m_le[:, :], ALU.mult)
            P.tt("dve", self.mq(d["RBT"]), pB[:, 0:128], self.m_le[:, :], ALU.mult)
        for d in H:
            d["Lp"], d["Up"] = d["L1"], d["U1"]
        for lev in range(1, 7):
            pLs = []
            for d in H:
                Ln = d["La"] if lev % 2 == 0 else d["Lb"]
                Un = d["Ua"] if lev % 2 == 0 else d["Ub"]
                pL = self.ps()
                P.mm(pL[:, 0:128], self.mq(d["Up"]), self.mq(d["Lp"]))
                if lev < 6:
                    P.mm(pL[:, 128:256], self.mq(d["Lp"]), self.mq(d["Up"]))
                pLs.append((pL, Ln, Un))
            for d, (pL, Ln, Un) in zip(H, pLs):
                P.copy("act", self.mq(Ln), pL[:, 0:128])
                if lev < 6:
                    P.copy("act", self.mq(Un), pL[:, 128:256])
            pZs = []
            for d, (pL, Ln, Un) in zip(H, pLs):
                pZ = self.ps()
                P.mm(pZ[:, 0:128], self.mq(Ln), self.mq(d["Z"]))
                pZs.append(pZ)
                d["Lp"], d["Up"] = Ln, Un
            for d, pZ in zip(H, pZs):
                P.tt("dve", self.mq(d["Z"]), self.mq(d["Z"]), pZ[:, 0:128], ALU.add)
            if hook is not None:
                hook(lev)
                self.P.tag = 'rwkv_chunk'

    def rwkv_tail(self, l, hp, c, st):
        P = self.P
        AT, RT_, KH, BH, G, RK = (16 + 6 * hp + k for k in range(6))
        cs = (c * 128, (c + 1) * 128)
        VT, KT, BT, WT = (4 * st + k for k in range(4))
        RH = 36
        S0p = self.wkv_st[:, l, hp, :]
        q = lambda name, h2: 8 + st * 8 + h2 * 4 + ("AKT", "RKT", "RBT", "Z").index(name)

        def t1():
            P.tag = 'rwkv_tail'
            pR = self.ps()
            P.mm(pR[:, 0:128], self.ar(AT, *cs), S0p, True, False)
            for h2 in range(2):
                P.mm(pR[:, h2 * 64:h2 * 64 + 64], self.mq(q("AKT", h2)), self.mq(VT, h2 * 64, h2 * 64 + 64), False, h2 == 1)
            P.copy("act", self.mq(RH), pR[:, 0:128])

        def t2():
            P.tag = 'rwkv_tail'
            pW = self.ps()
            for h2 in range(2):
                P.mm(pW[:, h2 * 64:h2 * 64 + 64], self.mq(q("Z", h2)), self.mq(RH, h2 * 64, h2 * 64 + 64))
            P.copy("act", self.mq(WT), pW[:, 0:128])

        def t3():
            P.tag = 'rwkv_tail'
            O = self.PSL[c]
            P.mm(O[:, hp * 128:(hp + 1) * 128], self.ar(RT_, *cs), S0p, True, False)
            for h2 in range(2):
                oc = (hp * 2 + h2) * 64
                P.mm(O[:, oc:oc + 64], self.mq(q("RKT", h2)), self.mq(VT, h2 * 64, h2 * 64 + 64), False, False)
                P.mm(O[:, oc:oc + 64], self.mq(q("RBT", h2)), self.mq(WT, h2 * 64, h2 * 64 + 64), False, h2 == 1)
            pS = self.ps()
            P.mm(pS[:, 0:128], self.mq(KT), self.mq(VT), True, False)
            P.mm(pS[:, 0:128], self.mq(BT), self.mq(WT), False, True)
            for h2 in range(2):
                p0, p1 = h2 * 64, h2 * 64 + 64
                TM = 37 + h2
                Sd = self.wkv_st[p0:p1, l, hp, h2 * 64:h2 * 64 + 64]
                P.tt("dve", self.mq(TM, 0, 64, p0, p1), pS[p0:p1, h2 * 64:h2 * 64 + 64], Sd, ALU.add)
                P.ts("dve", Sd, self.mq(TM, 0, 64, p0, p1), self.ar(G, cs[1] - 1, cs[1], p0, p1), ALU.mult)

        return [t1, t2, t3]

    def prompt_state_out(self, l):
        self.P.tag = 'state_out'
        P = self.P
        o = self.dout
        nsc = dict(allow_slow_non_contiguous=True)
        for hp in range(2):
            pp = self.ps()
            P.transpose(pp[:, 0:128], self.wkv_st[:, l, hp, :], self.ident[:, :])
            P.copy("act", self.mq(27), pp[:, 0:128])
            for h2 in range(2):
                P.dma("sp", V(o["wkv_p"].ap()[l, 2 * hp + h2], ("wkv_p",)),
                      self.mq(27, h2 * 64, h2 * 64 + 64, h2 * 64, h2 * 64 + 64), "st_wkv")
        P.dma("sp", V(o["shift_p"].ap()[l].rearrange("(t p) -> p t", p=128), ("shift_p",)), self.shift_st[:, l, :], "stp_sh", **nsc)
        for ri, nm in enumerate(("re_p", "im_p")):
            dst = o[nm].ap()[l].rearrange("(j g) p -> (g p) j", g=2)
            P.dma("sp", V(dst, (nm,)), self.ssm_st[:, l, :, ri], "stp_s%d" % ri, **nsc)
        for t in range(2):
            dst = o["conv_p"].ap()[l][:, t * 128:(t + 1) * 128].rearrange("j c -> c j")
            P.dma("sp", V(dst, ("conv_p",)), self.conv_st[:, l, t, :], "stp_c%d" % t, **nsc)


def _flat(v):
    return V(v.ap.rearrange("p a b -> p (a b)"), v.keys)


class SampleMixin:
    def tk(self, s0, n, c0, c1):
        hh = self.arena.h[0:NS, s0:s0 + n, :].rearrange("p a b -> p (a b)")[:, c0:c1]
        return V(hh, tuple(("ar", j) for j in range(s0, s0 + n)))

    def tok2fm(self, src_fn, k, dst):
        P = self.P
        pp = self.ps()
        for j in range(k):
            P.transpose(pp[:, j * NS:(j + 1) * NS], src_fn(j), self.ident[0:NS, 0:NS])
        P.copy("act", dst, V(pp.h[:, 0:k * NS].rearrange("p (k c) -> p k c", k=k), pp.keys))

    def s_norm(self, gbase):
        self.P.tag = 's_norm'
        P = self.P
        x = self.tk(0, 2, 0, D)
        h = self.tk(9, 2, 0, D)
        sm = self.small
        P.act(h, x, AF.Square, accum_out=sm[0:NS, 0:1])
        P.act(sm[0:NS, 4:5], sm[0:NS, 0:1], AF.Sqrt, scale=1.0 / D, bias=1e-6)
        P.recip(sm[0:NS, 8:9], sm[0:NS, 4:5])
        P.ts("dve", h, x, sm[0:NS, 8:9], ALU.mult)
        hr = self.hT.r()
        pp = self.ps()
        for kc in range(8):
            P.transpose(pp[:, kc * NS:(kc + 1) * NS], self.tk(9, 2, kc * 128, (kc + 1) * 128), self.ident[0:NS, 0:NS])
        for kc in range(8):
            P.act(hr[:, kc, 0:NS], pp[:, kc * NS:(kc + 1) * NS], AF.Identity,
                  scale=self.PT[:, 0, gbase + kc:gbase + kc + 1])

    def sample(self):
        P = self.P
        S = self.stages
        d = self.di

        def start(w):
            self.ps_n = 4
            P.dma("sp", self.tk(0, 2, 0, D), V(d("x_sample"), ()), "xs_ld")

        S.append((None, start))
        for l in range(self.nlayer):
            self.sample_layer(l)

        def fin(w):
            x = self.tk(0, 2, 0, D)
            h = self.tk(9, 2, 0, D)
            sm = self.small
            P.act(h, x, AF.Square, accum_out=sm[0:NS, 0:1])
            P.act(sm[0:NS, 4:5], sm[0:NS, 0:1], AF.Sqrt, scale=1.0 / D, bias=1e-6)
            P.recip(sm[0:NS, 8:9], sm[0:NS, 4:5])
            P.ts("dve", h, x, sm[0:NS, 8:9], ALU.mult)
            g = self.tk(11, 2, 0, D)
            P.dma("sp", g, V(d("norm_f_g").partition_broadcast(NS), ()), "gfb")
            P.tt("dve", h, h, g, ALU.mult)
            P.dma("sp", V(self.dout["y_s"].ap(), ("y_s",)), h, "st_ys")

        S.append((None, fin))

    def sample_layer(self, l):
        P = self.P
        S = self.stages
        d = self.di
        hr = self.hT.r()

        def pre(w):
            lcd = self.scr["lcd"].ap()
            P.dma("sp", self.LC[:, 0:1024], V(lcd[l][:, 0:1024], ("lcd",)), "lc_a")
            P.dma("pool", self.LCR[:, 1024:1536], V(self.scr["lcd"].bitcast(F32R).ap()[l][:, 1024:1536], ("lcd",)), "lc_b")
            P.dma("pool", V(self.LCR.h[:, 1536:2048].rearrange("p (k c) -> p k c", k=2), self.LC.keys),
                  V(self.dir_("ssm_glu_w")[l].rearrange("(k p) c -> p k c", p=128), ()), "lc_b")
            for i, nm in enumerate(("gm_ln_g", "gm_ln_b", "tm_ln_g", "tm_ln_b")):
                P.dma("sp", self.BCP[:, i, :], V(d(nm)[l].partition_broadcast(128), ()), "bcp")
            P.dma("sp", self.tk(15, 2, 0, DTM), V(d("tm_mu")[l].partition_broadcast(NS), ()), "sbc")
            for i, nm in enumerate(("tm_w0", "tm_a0", "tm_k_k", "tm_k_a", None, "tm_r_k")):
                if nm is None:
                    continue
                src = d(nm)[l]
                if nm == "tm_r_k":
                    src = src.rearrange("h n -> (h n)")
                P.dma("sp", self.tk(17, 3, i * 256, (i + 1) * 256), V(src.partition_broadcast(NS), ()), "sbc")
            for j in range(3):
                P.dma("sp", self.tk(20, 3, j * 256, (j + 1) * 256), V(d("conv_w")[l, j].partition_broadcast(NS), ()), "sbc")
            P.dma("sp", self.tk(20, 3, 768, 1024), V(d("conv_b")[l].partition_broadcast(NS), ()), "sbc")
            P.dma("sp", self.tk(20, 3, 1024, 1280), V(d("ssm_d")[l].rearrange("g h -> (g h)").partition_broadcast(NS), ()), "sbc")
            P.dma("sp", self.tk(20, 3, 1280, 1536), V(d("ssm_glu_b")[l].partition_broadcast(NS), ()), "sbc")
            nsc = dict(allow_slow_non_contiguous=True)
            P.dma("sp", self.small[0:NS, 40:44], V(d("gm_ws")[l, :, 0, 0:1].rearrange("h o -> o h").to_broadcast([NS, 4]), ()), "sbc", **nsc)
            P.dma("sp", self.small[0:NS, 44:48], V(d("gm_bs")[l, :, 0:1].rearrange("h o -> o h").to_broadcast([NS, 4]), ()), "sbc", **nsc)
            P.seal("sbc")
            P.ts("dve", self.tk(17, 3, 4 * 256, 5 * 256), self.tk(17, 3, 3 * 256, 4 * 256), -1.0, ALU.mult, 1.0, ALU.add)
            self.s_norm(l * 8)

        S.append((None, pre))
        for i in range(5):
            c0 = 512 * i
            ncol = min(512, INC - c0)

            def ld(c0=c0, ncol=ncol):
                return self.wload(self.dir_("w_in")[l][:, c0:c0 + ncol].rearrange("(k p) c -> p k c", p=128), 8, ncol)

            def comp(w, i=i, c0=c0, ncol=ncol):
                pp = self.ps()
                for kc in range(8):
                    P.mm(pp[0:NS, 0:ncol], hr[:, kc, 0:NS], w[:, kc, :], kc == 0, kc == 7)
                P.copy("act", self.tk(2, 5, c0, c0 + ncol), pp[0:NS, 0:ncol])
                if i == 4:
                    self.s_mixers(l)

            S.append((ld, comp))
        for half in range(2):
            def ld(half=half):
                return self.wload(self.dir_("w_out")[l][:, half * 512:(half + 1) * 512].rearrange("(k p) c -> p k c", p=128), 8, 512)

            def comp(w, half=half):
                pp = self.ps()
                for kc in range(8):
                    P.mm(pp[0:NS, :], hr[:, kc, NS:2 * NS], w[:, kc, :], kc == 0, kc == 7)
                xs = self.tk(0, 2, half * 512, (half + 1) * 512)
                P.tt("dve", xs, xs, pp[0:NS, :], ALU.add)
                if half == 1:
                    self.s_norm(32 + l * 8)

            S.append((ld, comp))
        for g in range(5):
            for which in range(2):
                def ld(g=g, which=which):
                    c0 = which * DFF + 512 * g
                    return self.wload(self.dir_("ffn_w_gu")[l][:, c0:c0 + 512].rearrange("(k p) c -> p k c", p=128), 8, 512)

                def comp(w, g=g, which=which):
                    pp = self.ps()
                    for kc in range(8):
                        P.mm(pp[0:NS, :], hr[:, kc, 0:NS], w[:, kc, :], kc == 0, kc == 7)
                    dst = self.tk(9, 6, 512 * g, 512 * g + 512)
                    if which == 0:
                        P.act(dst, pp[0:NS, :], AF.Silu)
                    else:
                        P.tt("dve", dst, dst, pp[0:NS, :], ALU.mult)

                S.append((ld, comp))

        def ld_last():
            buf = self.WB[self.wbi % NWB]
            self.wbi += 1
            view = buf.h.bitcast(F32R)[:, 0:4096].rearrange("p (k g c) -> p k g c", k=8, g=2)
            for g in range(2):
                c0 = g * DFF + 2560
                src = self.dir_("ffn_w_gu")[l][:, c0:c0 + 256].rearrange("(k p) c -> p k c", p=128)
                P.dma("pool", V(view[:, :, g, :], buf.keys), V(src, ()), "ld_" + buf.name)
            return TT(view, buf.name, buf.keys)

        def comp_last(w):
            pp = self.ps()
            for kc in range(8):
                P.mm(pp[0:NS, :], hr[:, kc, 0:NS], V(w.h[:, kc, :, :].rearrange("p g c -> p (g c)"), w.keys), kc == 0, kc == 7)
            tmp = self.tk(23, 1, 0, 256)
            P.act(tmp, pp[0:NS, 0:256], AF.Silu)
            P.tt("dve", self.tk(9, 6, 2560, 2816), tmp, pp[0:NS, 256:512], ALU.mult)
            yr = self.yT.r()
            dst = V(yr.h[:, 0, 0:352].rearrange("p (k c) -> p k c", k=22), yr.keys)
            self.tok2fm(lambda j: self.tk(9, 6, j * 128, (j + 1) * 128), 22, dst)

        S.append((ld_last, comp_last))
        for half in range(2):
            for jg in range(3):
                nj = 8 if jg < 2 else 6

                def ld(half=half, jg=jg, nj=nj):
                    src = self.dir_("ffn_w_down")[l][jg * 1024:jg * 1024 + nj * 128, half * 512:(half + 1) * 512]
                    return self.wload(src.rearrange("(j p) c -> p j c", p=128), nj, 512)

                def comp(w, half=half, jg=jg, nj=nj):
                    yr = self.yT.r()
                    for jj in range(nj):
                        j = jg * 8 + jj
                        P.mm(self.PSL[0][0:NS, :], V(yr.h[:, 0, j * NS:(j + 1) * NS], yr.keys), w[:, jj, :], j == 0, j == 21)
                    if jg == 2:
                        xs = self.tk(0, 2, half * 512, (half + 1) * 512)
                        P.tt("dve", xs, xs, self.PSL[0][0:NS, :], ALU.add)

                S.append((ld, comp))

    def s_mixers(self, l):
        self.P.tag = 's_mixers'
        P = self.P
        d = self.di
        o = self.dout
        hr = self.hT.r()
        lr = self.lora.r()
        sm = self.small
        ms = self.mstat
        Z = lambda a, b: self.tk(2, 5, a, b)
        M = lambda a, b: self.tk(7, 2, a, b)
        p6 = lambda i: self.tk(17, 3, i * 256, (i + 1) * 256)
        c6 = lambda i: self.tk(20, 3, i * 256, (i + 1) * 256)
        h4 = lambda v: V(v.ap.rearrange("p (h n) -> p h n", h=4), v.keys)
        b4 = lambda v: V(v.ap.unsqueeze(2).to_broadcast([NS, 4, 64]), v.keys)
        u = self.tk(9, 1, 0, 256)
        vf = self.tk(9, 1, 256, 512)
        P.act(u, Z(0, 256), AF.Gelu_apprx_tanh)
        P.act(vf, Z(256, 512), AF.Gelu_apprx_tanh)
        P.bn_stats(ms[0:NS, 0, 0:6], vf)
        P.bn_aggr(ms[0:NS, 0, 6:8], ms[0:NS, 0, 0:6])
        P.act(sm[0:NS, 16:17], ms[0:NS, 0, 7:8], AF.Sqrt, bias=1e-5)
        P.recip(sm[0:NS, 20:21], sm[0:NS, 16:17])
        P.ts("dve", vf, vf, ms[0:NS, 0, 6:7], ALU.subtract, sm[0:NS, 20:21], ALU.mult)
        P.tt("dve", vf, vf, self.BCP[0:NS, 0, :], ALU.mult)
        P.tt("dve", vf, vf, self.BCP[0:NS, 1, :], ALU.add)
        P.dma("sp", V(o["chv_s"].ap()[l], ("chv_s",)), vf, "st_chv")
        t = self.tk(10, 1, 0, 256)
        P.tt("dve", h4(t), h4(vf), b4(sm[0:NS, 40:44]), ALU.mult)
        P.tt("dve", h4(t), h4(t), b4(sm[0:NS, 44:48]), ALU.add)
        P.tt("dve", M(0, 256), u, t, ALU.mult)
        if 'sm1' in DBG:
            return
        buf = self.tk(23, 1, 0, 512)
        P.dma("sp", buf, V(d("state_conv")[l].rearrange("b j c -> b (j c)"), ()), "s_ld_c")
        zz = self.tk(10, 1, 256, 512)
        P.tt("dve", zz, Z(1280, 1536), Z(768, 1024), ALU.mult)
        y = self.tk(11, 1, 0, 256)
        t2 = self.tk(11, 1, 256, 512)
        P.tt("dve", y, zz, c6(2), ALU.mult)
        P.tt("dve", t2, self.tk(23, 1, 0, 256), c6(0), ALU.mult)
        P.tt("dve", y, y, t2, ALU.add)
        P.tt("dve", t2, self.tk(23, 1, 256, 512), c6(1), ALU.mult)
        P.tt("dve", y, y, t2, ALU.add)
        P.tt("dve", y, y, c6(3), ALU.add)
        P.tt("dve", M(512, 768), y, Z(1024, 1280), ALU.mult)
        P.dma("sp", V(o["conv_s"].ap()[l][:, 0, :], ("conv_s",)), self.tk(23, 1, 256, 512), "st_cv0")
        P.dma("sp", V(o["conv_s"].ap()[l][:, 1, :], ("conv_s",)), zz, "st_cv1")
        if 'sm2' in DBG:
            return
        pp = self.ps()
        for j in range(2):
            P.transpose(pp[:, j * NS:(j + 1) * NS], Z(512 + j * 128, 640 + j * 128), self.ident[0:NS, 0:NS])
        ppv = lambda p0, p1: V(pp.h[p0:p1, 0:2 * NS].rearrange("p (k c) -> p k c", k=2), pp.keys)
        P.copy("act", hr[:, 0:2, 32:48], ppv(0, 128))
        P.copy("act", hr[64:128, 0:2, 48:64], ppv(64, 128))
        P.copy("dve", hr[64:96, 0:2, 48:64], V(self.zeros.h[64:96, 0:32].rearrange("p (k c) -> p k c", k=2), self.zeros.keys))
        if 'sm21' in DBG:
            return
        P.dma("sp", self.tk(9, 2, 0, 1024), V(d("state_ssm_re")[l].rearrange("b g p -> b (g p)"), ()), "s_ld_re")
        P.dma("sp", self.tk(11, 2, 0, 1024), V(d("state_ssm_im")[l].rearrange("b g p -> b (g p)"), ()), "s_ld_im")
        m3 = lambda q: V(self.mat.h[:, q, :].rearrange("p (j c) -> p j c", j=8), (("m", q),))
        self.tok2fm(lambda j: self.tk(9, 2, j * 128, (j + 1) * 128), 8, m3(0))
        self.tok2fm(lambda j: self.tk(11, 2, j * 128, (j + 1) * 128), 8, m3(1))
        if 'sm22' in DBG:
            return
        pbs = [self.ps() for _ in range(4)]
        for j in range(8):
            half, jl = divmod(j, 4)
            for ri in range(2):
                c0 = 1024 + (half * 2 + ri) * 128
                col = (half * 2 + ri) * NS
                if jl < 3:
                    rows = slice(32 * jl, 32 * jl + 32)
                    P.mm(pbs[jl][:, col:col + NS], V(self.LCR.h[rows, c0:c0 + 128], self.LCR.keys), hr[rows, half, 32:48])
                else:
                    P.mm(pbs[jl][:, col:col + NS], self.LCR[64:128, c0:c0 + 128], hr[64:128, half, 48:64])
        pbv = lambda jl, ri: V(pbs[jl].h[:, 0:64].rearrange("p (h r c) -> p h r c", h=2, r=2)[:, :, ri, :], pbs[jl].keys)
        m3j = lambda q, jl: V(self.mat.h[:, q, :].rearrange("p (h j c) -> p h j c", h=2, j=4)[:, :, jl, :], (("m", q),))
        lb = lambda idx: V(self.SPt.h[:, idx, l * 8:(l + 1) * 8].unsqueeze(2).to_broadcast([128, 8, NS]), self.SPt.keys)
        lbr, lbi = lb(self.I_LBR), lb(self.I_LBI)
        P.tt("dve", m3(4), m3(0), lbr, ALU.mult)
        P.tt("dve", m3(5), m3(1), lbi, ALU.mult)
        P.tt("dve", m3(4), m3(4), m3(5), ALU.subtract)
        for jl in range(4):
            P.tt("dve", m3j(2, jl), m3j(4, jl), pbv(jl, 0), ALU.add)
        P.tt("dve", m3(4), m3(1), lbr, ALU.mult)
        P.tt("dve", m3(5), m3(0), lbi, ALU.mult)
        P.tt("dve", m3(4), m3(4), m3(5), ALU.add)
        for jl in range(4):
            P.tt("dve", m3j(3, jl), m3j(4, jl), pbv(jl, 1), ALU.add)
        if 'sm23' in DBG:
            return
        py = self.ps()
        for j in range(8):
            half, jl = divmod(j, 4)
            for ri in range(2):
                c0 = 512 + (half * 2 + ri) * 128 + jl * 32
                P.mm(py[0:NS, half * 128 + jl * 32:half * 128 + jl * 32 + 32],
                     V(self.mat.h[:, 2 + ri, j * NS:(j + 1) * NS], (("m", 2 + ri),)), self.LC[:, c0:c0 + 32], ri == 0, ri == 1)
        yv = self.tk(10, 1, 0, 256)
        P.tt("dve", yv, Z(512, 768), c6(4), ALU.mult)
        P.tt("dve", yv, yv, py[0:NS, 0:256], ALU.add)
        P.act(yv, yv, AF.Gelu_apprx_tanh)
        if 'sm24' in DBG:
            return
        self.tok2fm(lambda j: self.tk(10, 1, j * 128, (j + 1) * 128), 2, hr[:, 2:4, 32:48])
        pg = self.ps()
        for kc in range(2):
            P.mm(pg[0:NS, 0:256], hr[:, 2 + kc, 32:48], self.LCR[:, 1536 + kc * 256:1792 + kc * 256], kc == 0, kc == 1)
        P.tt("dve", y, pg[0:NS, 0:256], c6(5), ALU.add)
        P.act(y, y, AF.Sigmoid)
        P.tt("dve", M(256, 512), yv, y, ALU.mult)
        if 'sm25' in DBG:
            return
        for ri, nm in enumerate(("re_s", "im_s")):
            pa_, pb_ = self.ps(), self.ps()
            for j in range(8):
                bank = pa_ if j < 4 else pb_
                P.transpose(bank[0:NS, (j % 4) * 128:(j % 4 + 1) * 128],
                            V(self.mat.h[:, 2 + ri, j * NS:(j + 1) * NS], (("m", 2 + ri),)), self.ident[:, :])
            so = self.tk(13, 2, 0, 1024)
            P.copy("act", self.tk(13, 2, 0, 512), pa_[0:NS, :])
            P.copy("act", self.tk(13, 2, 512, 1024), pb_[0:NS, :])
            P.dma("sp", V(o[nm].ap()[l].rearrange("b g p -> b (g p)"), (nm,)), so, "st_s%d" % ri)
        if 'sm3' in DBG:
            return
        zs = lambda a, b: self.tk(13, 2, a, b)
        zsa = zs(0, DTM)
        zd = Z(1536, 2432)
        P.dma("sp", zsa, V(d("state_shift")[l], ()), "s_ld2")
        P.dma("sp", V(o["shift_s"].ap()[l], ("shift_s",)), zd, "st_sh")
        P.tt("dve", zsa, zsa, zd, ALU.subtract)
        P.tt("dve", zsa, zsa, self.tk(15, 2, 0, DTM), ALU.mult)
        P.tt("dve", zsa, zsa, zd, ALU.add)
        lx = self.tk(23, 1, 0, 128)
        P.act(self.tk(23, 1, 0, 32), zs(768, 800), AF.Tanh)
        P.act(self.tk(23, 1, 32, 64), zs(800, 832), AF.Copy)
        P.act(self.tk(23, 1, 64, 128), zs(832, 896), AF.Sigmoid)
        pp = self.ps()
        P.transpose(pp[:, 0:NS], lx, self.ident[0:NS, 0:NS])
        P.copy("act", hr[:, 4, 32:48], pp[:, 0:NS])
        pq = self.ps()
        P.mm(pq[0:NS, 0:256], hr[0:32, 4, 32:48], lr[0:32, l, :])
        pa = self.ps()
        P.mm(pa[0:NS, 0:256], hr[32:64, 4, 32:48], lr[32:64, l, :])
        P.mm(self.PSL[1][0:NS, 0:256], hr[64:128, 4, 32:48], lr[64:128, l, :])
        vec = lambda j: self.tk(20, 3, j * 256, (j + 1) * 256)
        t = self.tk(9, 1, 0, 256)
        A = self.tk(9, 1, 256, 512)
        P.tt("dve", t, pq[0:NS, 0:256], p6(0), ALU.add)
        P.act(t, t, AF.Sigmoid)
        P.act(vec(1), t, AF.Exp, scale=-0.6065306597126334)
        P.tt("dve", A, pa[0:NS, 0:256], p6(1), ALU.add)
        P.act(A, A, AF.Sigmoid)
        kk = self.tk(10, 1, 0, 256)
        sq = self.tk(10, 1, 256, 512)
        P.tt("dve", kk, zs(256, 512), p6(2), ALU.mult)
        P.tt("dve", sq, kk, kk, ALU.mult)
        P.reduce(sm[0:NS, 24:28], h4(sq), ALU.add)
        P.act(sm[0:NS, 24:28], sm[0:NS, 24:28], AF.Sqrt)
        P.ts("dve", sm[0:NS, 24:28], sm[0:NS, 24:28], 1e-12, ALU.max)
        P.recip(sm[0:NS, 28:32], sm[0:NS, 24:28])
        P.tt("dve", h4(kk), h4(kk), b4(sm[0:NS, 28:32]), ALU.mult)
        P.tt("dve", t, A, p6(3), ALU.mult)
        P.tt("dve", t, t, p6(4), ALU.add)
        P.tt("dve", vec(2), zs(256, 512), t, ALU.mult)
        P.ts("dve", vec(4), kk, -1.0, ALU.mult)
        P.tt("dve", vec(5), kk, A, ALU.mult)
        P.copy("act", vec(0), zs(0, 256))
        P.copy("act", vec(3), zs(512, 768))
        P.tt("dve", t, zs(0, 256), vec(2), ALU.mult)
        P.tt("dve", t, t, p6(5), ALU.mult)
        P.reduce(sm[0:NS, 32:36], h4(t), ALU.add)
        if 'sm4' in DBG:
            return
        svec = self.scr["svec"].ap()
        P.dma("sp", V(svec.rearrange("j b c -> b j c"), ("svec",)),
              V(self.tk(20, 3, 0, 1536).ap.rearrange("p (j c) -> p j c", j=6), self.tk(20, 3, 0, 1536).keys), "sv_w")
        hi = lambda s0, n, c0, c1: V(self.arena.h[64:128, s0:s0 + n, :].rearrange("p a b -> p (a b)")[:, c0:c1],
                                     tuple(("ar", j) for j in range(s0, s0 + n)))
        vS = hi(16, 1, 0, 384)
        P.dma("sp", V(vS.ap.rearrange("p (j n) -> p j n", j=6), vS.keys),
              V(svec.rearrange("j b (h n) -> (b h) j n", h=4), ("svec",)), "sv_r")
        Sv = hi(0, 8, 0, 4096)
        Tv = hi(8, 8, 0, 4096)
        P.dma("sp", Sv, V(d("state_wkv")[l].rearrange("b h v k -> (b h) (v k)"), ()), "s_ld3")
        S3 = V(Sv.ap.rearrange("p (v k) -> p v k", v=64), Sv.keys)
        T3 = V(Tv.ap.rearrange("p (v k) -> p v k", v=64), Tv.keys)
        vj = lambda j: hi(16, 1, j * 64, (j + 1) * 64)
        kbc = lambda j: V(vj(j).ap.unsqueeze(1).to_broadcast([64, 64, 64]), vj(j).keys)
        vbc = lambda v: V(v.ap.unsqueeze(2).to_broadcast([64, 64, 64]), v.keys)
        sa = hi(17, 1, 0, 64)
        ov = hi(17, 1, 64, 128)
        P.tt("dve", T3, S3, kbc(4), ALU.mult)
        P.reduce(sa, T3, ALU.add)
        P.tt("pool", S3, S3, kbc(1), ALU.mult)
        P.tt("dve", T3, vbc(sa), kbc(5), ALU.mult)
        P.tt("pool", S3, S3, T3, ALU.add)
        P.tt("dve", T3, vbc(vj(3)), kbc(2), ALU.mult)
        P.tt("pool", S3, S3, T3, ALU.add)
        P.tt("dve", T3, S3, kbc(0), ALU.mult)
        P.reduce(ov, T3, ALU.add)
        P.dma("sp", V(o["wkv_s"].ap()[l].rearrange("b h v k -> (b h) (v k)"), ("wkv_s",)), Sv, "st_s3")
        if 'sm5' in DBG:
            return
        P.bn_stats(ms[64:128, 0, 0:6], ov)
        P.bn_aggr(ms[64:128, 0, 6:8], ms[64:128, 0, 0:6])
        P.act(sm[64:128, 50:51], ms[64:128, 0, 7:8], AF.Sqrt, bias=64e-5)
        P.recip(sm[64:128, 51:52], sm[64:128, 50:51])
        onh = hi(17, 1, 128, 192)
        P.ts("dve", onh, ov, ms[64:128, 0, 6:7], ALU.subtract, sm[64:128, 51:52], ALU.mult)
        son = self.scr["son"].ap()
        P.dma("sp", V(son.rearrange("b (h n) -> (b h) n", h=4), ("son",)), onh, "so_w")
        on = self.tk(9, 1, 0, 256)
        P.dma("sp", on, V(son, ("son",)), "so_r")
        P.tt("dve", on, on, self.BCP[0:NS, 2, :], ALU.mult)
        P.tt("dve", on, on, self.BCP[0:NS, 3, :], ALU.add)
        P.tt("dve", h4(A), h4(zs(512, 768)), b4(sm[0:NS, 32:36]), ALU.mult)
        P.tt("dve", on, on, A, ALU.add)
        P.tt("dve", M(768, 1024), on, self.PSL[1][0:NS, 0:256], ALU.mult)
        self.tok2fm(lambda j: M(j * 128, (j + 1) * 128), 8, hr[:, :, NS:2 * NS])


class Builder(SampleMixin, Builder0):
    pass


_CACHE = {}


def _get_nc():
    if "nc" not in _CACHE:
        _CACHE["nc"] = Builder().nc
    return _CACHE["nc"]


def kernel(**inputs):
    inp = {k: np.ascontiguousarray(np.asarray(v, dtype=np.float32)) for k, v in inputs.items()}
    nc = _get_nc()
    in_maps = []
    for c in range(NCORES):
        m = {}
        for n in W_SHAPES:
            m[n] = inp[n]
        m["x_prompt"] = np.ascontiguousarray(inp["x_prompt"][c])
        m["x_sample"] = np.ascontiguousarray(inp["x_sample"][c * NS:(c + 1) * NS, 0, :])
        for n in ("state_wkv", "state_shift", "state_ssm_re", "state_ssm_im", "state_conv"):
            m[n] = np.ascontiguousarray(inp[n][:, c * NS:(c + 1) * NS])
        in_maps.append(m)
    res = run_bass_kernel_spmd(nc, in_maps, core_ids=list(range(NCORES)))
    R = res.results
    cat = lambda n, ax: np.concatenate([np.asarray(r[n]) for r in R], axis=ax)
    stk = lambda n: np.stack([np.asarray(r[n]) for r in R], axis=1)
    y_p = np.stack([np.asarray(r["y_p"]) for r in R], axis=0)
    y_s = cat("y_s", 0)[:, None, :]
    outs = (y_p, y_s, stk("wkv_p"), cat("wkv_s", 1), stk("shift_p"), cat("shift_s", 1),
            stk("re_p"), cat("re_s", 1), stk("im_p"), cat("im_s", 1),
            stk("conv_p"), cat("conv_s", 1), cat("chv_s", 1)[:, :, None, :])
    return tuple(np.ascontiguousarray(o.astype(np.float32)) for o in outs)
```
